# Optimizing a Trainium2 kernel written in Bass

```python
import jax
import jax.numpy as jnp
from jax import lax
import numpy as np

D_MODEL = 1024
BATCH = 2
SEQ = 8192
DEPTH = 2

GRID_W = 64
CTX_LEN = 256
N_MOD = 9
D_FF = 2816
GLA_HEADS = 4
GLA_DK = 64
GLA_DV = 128
GLA_GATE_RANK = 16
GLA_TAU = 16.0
GLA_CHUNK = 64
FOURIER_GROUPS = 4
FOURIER_CH = 128
MLA_HEADS = 8
MLA_NOPE = 64
MLA_ROPE = 32
MLA_V = 64
MLA_Q_RANK = 384
MLA_KV_RANK = 256
ROPE_BASE = 10000.0
Q_BLOCK = 128
N_BRANCH = 3
BRANCH_W = 512
EPS = 1e-6
IN_SPLIT = (
    ('gla_q', GLA_HEADS * GLA_DK),
    ('gla_k', GLA_HEADS * GLA_DK),
    ('gla_v', GLA_HEADS * GLA_DV),
    ('gla_g', GLA_HEADS * GLA_DV),
    ('gla_a_f', GLA_GATE_RANK),
    ('gla_a_b', GLA_GATE_RANK),
    ('fourier', FOURIER_GROUPS * FOURIER_CH),
    ('mla_cq', MLA_Q_RANK),
    ('mla_ckv', MLA_KV_RANK),
    ('mla_kr', MLA_ROPE),
    ('gates', N_BRANCH * D_MODEL),
)
D_IN = sum(s for _, s in IN_SPLIT)

kernel_name = 'hybrid_gla_fnet_mla_macaron'


def _rmsnorm(x, g):
    xf = x.astype(jnp.float32)
    y = xf * lax.rsqrt(jnp.mean(xf * xf, axis=-1, keepdims=True) + EPS)
    return (y * g.astype(jnp.float32)).astype(x.dtype)


def _modulation(cond, w, b):
    m = jax.nn.silu(cond) @ w + b
    return jnp.split(m[:, None, :], N_MOD, axis=-1)


def _half_ffn(x, mod, g_pre, g_post, wg, wu, wd):
    shift, scale, gate = mod
    h = _rmsnorm(x, g_pre) * (1 + scale) + shift
    y = (jax.nn.silu(h @ wg) * (h @ wu)) @ wd
    return x + 0.5 * gate * _rmsnorm(y, g_post)


def _split_in(z):
    out = {}
    off = 0
    for name, size in IN_SPLIT:
        out[name] = z[..., off:off + size]
        off += size
    return out


def _axial_rope_tables(n):
    ROWS = n // GRID_W
    rows = jnp.repeat(jnp.arange(ROWS, dtype=jnp.float32), GRID_W)
    cols = jnp.tile(jnp.arange(GRID_W, dtype=jnp.float32), ROWS)
    half = MLA_ROPE // 2
    inv_freq = ROPE_BASE ** (-jnp.arange(0, half, 2, dtype=jnp.float32) / half)
    ang_r = rows[:, None] * inv_freq
    ang_c = cols[:, None] * inv_freq
    return (jnp.cos(ang_r)[:, None], jnp.sin(ang_r)[:, None], jnp.cos(ang_c)[:, None], jnp.sin(ang_c)[:, None])


def _rotate(x, cos, sin):
    h = x.shape[-1] // 2
    x1, x2 = x[..., :h], x[..., h:]
    return jnp.concatenate([x1 * cos - x2 * sin, x2 * cos + x1 * sin], axis=-1)


def _rope2d(x, tabs):
    cr, sr, cc, sc = tabs
    xf = x.astype(jnp.float32)
    a = MLA_ROPE // 2
    out = jnp.concatenate([_rotate(xf[..., :a], cr, sr), _rotate(xf[..., a:], cc, sc)], axis=-1)
    return out.astype(x.dtype)


def _gla_chunked(q, k, v, log_a, s0):
    b, n, h, dk = q.shape
    dv = v.shape[-1]
    nc = n // GLA_CHUNK

    def chunks(t):
        return t.astype(jnp.float32).reshape(b, nc, GLA_CHUNK, h, t.shape[-1]).transpose(1, 0, 3, 2, 4)

    tril = jnp.tril(jnp.ones((GLA_CHUNK, GLA_CHUNK), bool))[:, :, None]

    def step(state, inp):
        qc, kc, vc, gc = inp
        cum = jnp.cumsum(gc, axis=2)
        rel = jnp.where(tril, cum[:, :, :, None, :] - cum[:, :, None, :, :], -jnp.inf)
        attn = jnp.einsum('bhtd,bhtsd,bhsd->bhts', qc, jnp.exp(rel), kc)
        out = jnp.einsum('bhts,bhse->bhte', attn, vc) + jnp.einsum('bhtd,bhde->bhte', qc * jnp.exp(cum), state)
        tot = cum[:, :, -1:, :]
        state = jnp.exp(tot)[:, :, 0, :, None] * state + jnp.einsum('bhsd,bhse->bhde', kc * jnp.exp(tot - cum), vc)
        return state, out

    s_fin, o = lax.scan(step, s0, (chunks(q), chunks(k), chunks(v), chunks(log_a)))
    return o.transpose(1, 0, 3, 2, 4).reshape(b, n, h, dv), s_fin


def _gla_branch(z, s0_f, s0_b, w_dec, b_dec, g_norm):
    b, n, _ = z['gla_q'].shape
    dt = z['gla_v'].dtype

    def heads(t, d):
        return t.reshape(b, n, GLA_HEADS, d)

    q = heads(z['gla_q'], GLA_DK) * GLA_DK ** -0.5
    k = heads(z['gla_k'], GLA_DK)
    v = heads(z['gla_v'], GLA_DV)
    la_f = heads(jax.nn.log_sigmoid((z['gla_a_f'] @ w_dec[0] + b_dec[0]).astype(jnp.float32)) / GLA_TAU, GLA_DK)
    la_b = heads(jax.nn.log_sigmoid((z['gla_a_b'] @ w_dec[1] + b_dec[1]).astype(jnp.float32)) / GLA_TAU, GLA_DK)
    o_f, s_f = _gla_chunked(q, k, v, la_f, s0_f)
    o_b, s_b = _gla_chunked(q[:, ::-1], k[:, ::-1], v[:, ::-1], la_b[:, ::-1], s0_b)
    o = (o_f + o_b[:, ::-1]).astype(dt)
    y = _rmsnorm(o, g_norm) * jax.nn.silu(heads(z['gla_g'], GLA_DV))
    return y.reshape(b, n, GLA_HEADS * GLA_DV), s_f, s_b


def _fourier(f):
    b, n, _ = f.shape
    g = f.astype(jnp.float32).reshape(b, n, FOURIER_GROUPS, FOURIER_CH)
    y = jnp.fft.fft2(g, axes=(1, 3), norm='ortho').real
    return y.reshape(b, n, FOURIER_GROUPS * FOURIER_CH).astype(f.dtype)


def _mla_project(z, q_norm, w_uq, kv_norm, w_ukv):
    b, n, _ = z['mla_cq'].shape
    q = (_rmsnorm(z['mla_cq'], q_norm) @ w_uq).reshape(b, n, MLA_HEADS, MLA_NOPE + MLA_ROPE)
    kv = (_rmsnorm(z['mla_ckv'], kv_norm) @ w_ukv).reshape(b, n, MLA_HEADS, MLA_NOPE + MLA_V)
    return q[..., :MLA_NOPE], q[..., MLA_NOPE:], kv[..., :MLA_NOPE], z['mla_kr'], kv[..., MLA_NOPE:]


def _mla_context_attention(qn, qr, kn, kr, v):
    scale = (MLA_NOPE + MLA_ROPE) ** -0.5
    s = jnp.einsum('bqhd,bkhd->bhqk', qn, kn) + jnp.einsum('bqhr,bkr->bhqk', qr, kr)
    p = jax.nn.softmax(s.astype(jnp.float32) * scale, axis=-1).astype(v.dtype)
    o = jnp.einsum('bhqk,bkhe->bqhe', p, v)
    return o.reshape(o.shape[0], o.shape[1], MLA_HEADS * MLA_V)


def _mla_latent_attention(qn, qr, kn, kr, v, kn_c, kr_c, v_c):
    b, n, h, _ = qn.shape
    nb = n // Q_BLOCK
    scale = (MLA_NOPE + MLA_ROPE) ** -0.5

    def blocks(t):
        return t.reshape(b, nb, Q_BLOCK, *t.shape[2:]).swapaxes(0, 1)

    def attend(blk):
        qn_i, qr_i = blk
        s_lat = jnp.einsum('bqhd,bkhd->bhqk', qn_i, kn) + jnp.einsum('bqhr,bkr->bhqk', qr_i, kr)
        s_ctx = jnp.einsum('bqhd,bkhd->bhqk', qn_i, kn_c) + jnp.einsum('bqhr,bkr->bhqk', qr_i, kr_c)
        s = jnp.concatenate([s_lat, s_ctx], axis=-1).astype(jnp.float32) * scale
        p = jax.nn.softmax(s, axis=-1).astype(v.dtype)
        return jnp.einsum('bhqk,bkhe->bqhe', p[..., :n], v) + jnp.einsum('bhqk,bkhe->bqhe', p[..., n:], v_c)

    o = lax.map(attend, (blocks(qn), blocks(qr)))
    return o.swapaxes(0, 1).reshape(b, n, h * MLA_V)


def _merge(ya, yb, yc, gates, w_branch, w_out):
    g = jax.nn.sigmoid(gates.astype(jnp.float32)).astype(ya.dtype)
    ga, gb, gc = jnp.split(g, N_BRANCH, axis=-1)
    m = ga * (ya @ w_branch[0]) + gb * (yb @ w_branch[1]) + gc * (yc @ w_branch[2])
    return m @ w_out


def _token_mixing(xl, xc, mod_l, mod_c, g_pre, g_post, w_in, w_dec, b_dec, g_gla, q_norm, w_uq, kv_norm, w_ukv, w_branch, w_out, rope, ctx_out):
    shift_l, scale_l, gate_l = mod_l
    shift_c, scale_c, gate_c = mod_c
    hl = _rmsnorm(xl, g_pre) * (1 + scale_l) + shift_l
    hc = _rmsnorm(xc, g_pre) * (1 + scale_c) + shift_c
    zl = _split_in(hl @ w_in)
    zc = _split_in(hc @ w_in)
    zero = jnp.zeros((xc.shape[0], GLA_HEADS, GLA_DK, GLA_DV), jnp.float32)
    ya_c, s_f, s_b = _gla_branch(zc, zero, zero, w_dec, b_dec, g_gla)
    ya_l, _, _ = _gla_branch(zl, s_f, s_b, w_dec, b_dec, g_gla)
    yb_l = _fourier(zl['fourier'])
    qn_c, qr_c, kn_c, kr_c, v_c = _mla_project(zc, q_norm, w_uq, kv_norm, w_ukv)
    qn_l, qr_l, kn_l, kr_l, v_l = _mla_project(zl, q_norm, w_uq, kv_norm, w_ukv)
    qr_l = _rope2d(qr_l, rope)
    kr_l = _rope2d(kr_l[:, :, None, :], rope)[:, :, 0, :]
    yc_l = _mla_latent_attention(qn_l, qr_l, kn_l, kr_l, v_l, kn_c, kr_c, v_c)
    xl = xl + gate_l * _rmsnorm(_merge(ya_l, yb_l, yc_l, zl['gates'], w_branch, w_out), g_post)
    if ctx_out:
        yb_c = _fourier(zc['fourier'])
        yc_c = _mla_context_attention(qn_c, qr_c, kn_c, kr_c, v_c)
        xc = xc + gate_c * _rmsnorm(_merge(ya_c, yb_c, yc_c, zc['gates'], w_branch, w_out), g_post)
    return xl, xc


def setup_inputs(seed: int = 0) -> dict:
    key = jax.random.key(seed)
    ks = jax.random.split(key, 21)
    L, D = DEPTH, D_MODEL

    def nrm(k, shape, scale=1.0):
        return scale * jax.random.normal(k, shape, jnp.float32)

    return {
        'x': nrm(ks[0], (BATCH, SEQ, D)),
        'c': nrm(ks[1], (BATCH, D)),
        'ctx': nrm(ks[2], (BATCH, CTX_LEN, D)),
        'c_ctx': nrm(ks[3], (D,)),
        'w_mod': nrm(ks[4], (L, D, N_MOD * D), 0.5 * D ** -0.5),
        'b_mod': nrm(ks[5], (L, N_MOD * D), 0.01),
        'norm_pre': 1.0 + nrm(ks[6], (L, 3, D), 0.02),
        'norm_post': 1.0 + nrm(ks[7], (L, 3, D), 0.02),
        'ffn_w_gate': nrm(ks[8], (L, 2, D, D_FF), D ** -0.5),
        'ffn_w_up': nrm(ks[9], (L, 2, D, D_FF), D ** -0.5),
        'ffn_w_down': nrm(ks[10], (L, 2, D_FF, D), D_FF ** -0.5),
        'w_in': nrm(ks[11], (L, D, D_IN), D ** -0.5),
        'gla_w_decay': nrm(ks[12], (L, 2, GLA_GATE_RANK, GLA_HEADS * GLA_DK), GLA_GATE_RANK ** -0.5),
        'gla_b_decay': nrm(ks[13], (L, 2, GLA_HEADS * GLA_DK), 0.01),
        'gla_norm': 1.0 + nrm(ks[14], (L, GLA_DV), 0.02),
        'mla_q_norm': 1.0 + nrm(ks[15], (L, MLA_Q_RANK), 0.02),
        'mla_w_uq': nrm(ks[16], (L, MLA_Q_RANK, MLA_HEADS * (MLA_NOPE + MLA_ROPE)), MLA_Q_RANK ** -0.5),
        'mla_kv_norm': 1.0 + nrm(ks[17], (L, MLA_KV_RANK), 0.02),
        'mla_w_ukv': nrm(ks[18], (L, MLA_KV_RANK, MLA_HEADS * (MLA_NOPE + MLA_V)), MLA_KV_RANK ** -0.5),
        'w_branch': nrm(ks[19], (L, N_BRANCH, BRANCH_W, D), BRANCH_W ** -0.5),
        'w_out': nrm(ks[20], (L, D, D), D ** -0.5),
    }


def reference(x, c, ctx, c_ctx, w_mod, b_mod, norm_pre, norm_post, ffn_w_gate, ffn_w_up, ffn_w_down, w_in, gla_w_decay, gla_b_decay, gla_norm, mla_q_norm, mla_w_uq, mla_kv_norm, mla_w_ukv, w_branch, w_out):
    rope = _axial_rope_tables(x.shape[1])
    xl, xc = x, ctx
    for layer in range(DEPTH):
        last = layer == DEPTH - 1
        mod_l = _modulation(c, w_mod[layer], b_mod[layer])
        mod_c = _modulation(c_ctx[None, :], w_mod[layer], b_mod[layer])
        ffn_a = (norm_pre[layer, 0], norm_post[layer, 0], ffn_w_gate[layer, 0], ffn_w_up[layer, 0], ffn_w_down[layer, 0])
        ffn_b = (norm_pre[layer, 2], norm_post[layer, 2], ffn_w_gate[layer, 1], ffn_w_up[layer, 1], ffn_w_down[layer, 1])
        xl = _half_ffn(xl, mod_l[0:3], *ffn_a)
        xc = _half_ffn(xc, mod_c[0:3], *ffn_a)
        xl, xc = _token_mixing(xl, xc, mod_l[3:6], mod_c[3:6], norm_pre[layer, 1], norm_post[layer, 1], w_in[layer], gla_w_decay[layer], gla_b_decay[layer], gla_norm[layer], mla_q_norm[layer], mla_w_uq[layer], mla_kv_norm[layer], mla_w_ukv[layer], w_branch[layer], w_out[layer], rope, not last)
        xl = _half_ffn(xl, mod_l[6:9], *ffn_b)
        if not last:
            xc = _half_ffn(xc, mod_c[6:9], *ffn_b)
    return xl
```

```python
SCORE_MAX_MEASURED = {"layer0": 6.08, "layer1": 5.99}

import numpy as np
import concourse.bass as bass
import concourse.mybir as mybir
from concourse.bass_utils import run_bass_kernel_spmd

F32 = mybir.dt.float32
BF16 = mybir.dt.bfloat16
AF = mybir.ActivationFunctionType
ALU = mybir.AluOpType
AX = mybir.AxisListType


class Op:
    __slots__ = ("eng", "fn", "deps", "signal", "val", "dma", "dsem", "dval", "name", "gen", "reuse")


class Sched:
    ENGS = ("pe", "act", "dve", "pool", "sp")

    def __init__(self, nc, sems, dma_sems, same_engine_sync=True):
        self.nc = nc
        self.sem = sems
        self.dma_sems = dma_sems
        self.dma_rr = {e: 0 for e in self.ENGS}
        self.dma_val = {}
        self.count = {e: 0 for e in self.ENGS}
        self.waited = {e: {} for e in self.ENGS}
        self.last_w = {}
        self.readers = {}
        self.ops = {e: [] for e in self.ENGS}
        self.same = same_engine_sync
        self.all_dma_sems = {}
        self.gen = 0

    def add(self, eng, fn, reads=(), writes=(), dma=False, name=None):
        op = Op()
        op.eng, op.fn, op.dma, op.signal, op.name = eng, fn, dma, False, name
        op.val = op.dsem = op.dval = None
        op.gen = None
        deps = set()
        for k in reads:
            w = self.last_w.get(k)
            if w is not None:
                deps.add(w)
        for k in writes:
            w = self.last_w.get(k)
            if w is not None:
                deps.add(w)
            for r in self.readers.get(k, ()):
                deps.add(r)
        for k in reads:
            self.readers.setdefault(k, []).append(op)
        for k in writes:
            self.last_w[k] = op
            self.readers[k] = []
        deps.discard(op)
        op.deps = deps
        self.ops[eng].append(op)
        return op

    def rotate_engine_sems(self, new_sems):
        assert all(not self.ops[e] for e in self.ENGS), "rotate only at a phase boundary"
        self.sem = new_sems
        self.count = {e: 0 for e in self.ENGS}

    def _needs_wait(self, op, d):
        if d.dma:
            return True
        if d.eng == op.eng and not op.dma:
            if op.eng == "pe":
                return False
            return self.same
        return True

    def flush(self, block, final=False):
        self.gen = getattr(self, "gen", 0) + 1
        gen = self.gen
        for e in self.ENGS:
            for op in self.ops[e]:
                op.gen = gen
        for e in self.ENGS:
            for op in self.ops[e]:
                for d in op.deps:
                    if d.gen == gen and not d.dma and self._needs_wait(op, d):
                        d.signal = True
        for e in self.ENGS:
            c = self.count[e]
            for op in self.ops[e]:
                if op.dma:
                    lst = self.dma_sems[e]
                    s = lst[self.dma_rr[e] % len(lst)]
                    self.dma_rr[e] += 1
                    prev = self.dma_val.get(id(s), 0)
                    op.reuse = (s, prev)
                    op.dsem, op.dval = s, prev + 16
                    self.dma_val[id(s)] = prev + 16
                    self.all_dma_sems[id(s)] = s
                elif op.signal:
                    c += 1
                    op.val = c
            self.count[e] = c
        pending = self.ops
        self.ops = {e: [] for e in self.ENGS}
        handles = {"pe": block.tensor, "act": block.scalar, "dve": block.vector,
                   "pool": block.gpsimd, "sp": block.sync}
        for e in self.ENGS:
            if not pending[e] and not (final and e == "sp"):
                continue
            self._emit_engine(handles[e], e, pending[e], gen, final and e == "sp")

    def _emit_engine(self, deco, e, ops, gen, final):
        sched = self

        def body(eng):
            waited = sched.waited[e]
            for op in ops:
                need = {}
                for d in op.deps:
                    if d.dma:
                        s, v = d.dsem, d.dval
                    else:
                        if d.gen != gen or not sched._needs_wait(op, d):
                            continue
                        s, v = sched.sem[d.eng], d.val
                        assert v is not None and v > 0, (op.name, d.name, d.eng, v)
                    key = id(s)
                    if waited.get(key, 0) >= v:
                        continue
                    if key not in need or need[key][1] < v:
                        need[key] = (s, v)
                if op.dma:
                    s, prev = op.reuse
                    if prev > 0 and waited.get(id(s), 0) < prev:
                        if id(s) not in need or need[id(s)][1] < prev:
                            need[id(s)] = (s, prev)
                for key, (s, v) in need.items():
                    eng.wait_ge(s, v)
                    waited[key] = v
                inst = op.fn(eng)
                if op.dma:
                    inst.then_inc(op.dsem, 16)
                elif op.signal:
                    inst.then_inc(sched.sem[e], 1)
            if final:
                for key, s in sched.all_dma_sems.items():
                    v = sched.dma_val[key]
                    if waited.get(key, 0) < v:
                        eng.wait_ge(s, v)
                        waited[key] = v

        deco(body)


D = 1024
KC = 8
DFF = 2816
FC = 22
NMOD = 9
DIN = 5824
CTX = 256
SEG = 2048
NTOK = CTX + SEG
EPS = 1e-6
TT = [(0, 256, True)] + [(CTX + 512 * i, 512, False) for i in range(4)]


class K:
    def __init__(self, st):
        self.st = st
        self._stacks = [st]
        self.nc = nc = bass.Bass("TRN2", target_bir_lowering=False)
        E = st.enter_context
        sems = {e: E(nc.semaphore("s_" + e)) for e in Sched.ENGS}
        dsems = {e: [E(nc.semaphore(f"d_{e}{i}")) for i in range(12)] for e in ("sp", "pool", "act")}
        self.S = Sched(nc, sems, dsems)
        self.allsems = list(sems.values()) + sum(dsems.values(), [])
        self.uid = 0
        with nc.Block() as blk:
            @blk.sync
            def _(sp):
                for s in self.allsems:
                    sp.sem_clear(s)

    def _uname(self, name):
        self.uid += 1
        return f"{name}_{self.uid}"

    def sb(self, name, shape, dt):
        return self._stacks[-1].enter_context(self.nc.sbuf_tensor(self._uname(name), shape, dt))

    def ps(self, name, shape, dt=F32):
        return self._stacks[-1].enter_context(self.nc.psum_tensor(self._uname(name), shape, dt))

    def phase(self):
        import contextlib
        kk = self

        @contextlib.contextmanager
        def cm():
            sub = contextlib.ExitStack()
            kk._stacks.append(sub)
            try:
                yield
                kk.flush(final=True)
            finally:
                kk._stacks.pop()
                sub.close()
        return cm()

    def dram_in(self, name, shape, dt=F32):
        return self.nc.dram_tensor(name, list(shape), dt, kind="ExternalInput").ap()

    def dram_out(self, name, shape, dt=F32):
        return self.nc.dram_tensor(name, list(shape), dt, kind="ExternalOutput").ap()

    def flush(self, final=False):
        with self.nc.Block() as blk:
            self.S.flush(blk, final=final)

    def dma(self, q, out, in_, r=(), w=(), slow=False):
        if slow:
            return self.S.add(q, lambda e: e.dma_start(out=out, in_=in_, allow_slow_non_contiguous=True), reads=r, writes=w, dma=True)
        return self.S.add(q, lambda e: e.dma_start(out=out, in_=in_), reads=r, writes=w, dma=True)

    def mm(self, out, lhsT, rhs, start, stop, r=(), w=()):
        return self.S.add("pe", lambda e: e.matmul(out, lhsT=lhsT, rhs=rhs, start=start, stop=stop), reads=r, writes=w)

    def tr(self, out, in_, ident, r=(), w=()):
        return self.S.add("pe", lambda e: e.transpose(out, in_, ident), reads=r, writes=w)

    def act(self, out, in_, func, r=(), w=(), **kw):
        return self.S.add("act", lambda e: e.activation(out=out, in_=in_, func=func, **kw), reads=r, writes=w)

    def op(self, eng, fn, r=(), w=()):
        return self.S.add(eng, fn, reads=r, writes=w)


def emit_small_vectors(k, vecs, ident_f32, out_cols, key, pss):
    R = sum(v.shape[0] for v in vecs)
    assert R <= 128
    stg = k.sb(f"stg_{key}", [128, 128], F32)
    pst = pss
    r0 = 0
    for i, v in enumerate(vecs):
        n = v.shape[0]
        k.dma("sp", stg[r0:r0 + n, :], v, w=[(key, "stg", i)])
        r0 += n
    k.tr(pst[:, :R], stg[:R, :], ident_f32[:R, :R], r=[(key, "stg", i) for i in range(len(vecs))] + ["ident"], w=["pss"])
    k.op("dve", lambda e: e.memset(out_cols[:, :], 0.0), w=[key])
    k.op("dve", lambda e: e.tensor_copy(out=out_cols[:, :R], in_=pst[:, :R]), r=["pss", key], w=[key])


def emit_modulation(k, l, w_mod, b_cols, cs_bf, modT, wbuf, psm):
    NB = 18
    for nb in range(NB):
        slot = nb % 2
        wt = wbuf[slot]
        k.dma("pool", wt[:, :, :], w_mod[0, :, nb * 512:(nb + 1) * 512].rearrange("(kc p) n -> p kc n", p=128),
              w=[("wmod", slot)])
        for oc in range(4):
            jk = nb * 4 + oc
            for kc in range(KC):
                k.mm(psm[:, jk, :], wt[:, kc, oc * 128:(oc + 1) * 128], cs_bf[:, kc, :], kc == 0, kc == KC - 1,
                     r=[("wmod", slot), "cs_bf"], w=["pss"])
    k.op("dve", lambda e: e.tensor_tensor(out=modT[:, :, :], in0=psm[:, :, :],
                                          in1=b_cols.unsqueeze(2).to_broadcast([128, 72, 2]), op=ALU.add),
         r=["pss", ("vec", l)], w=[("modT", l)])


def emit_mod_columns(k, l, modT, vec, colA, colG):
    for i in range(3):
        half = 1.0 if i == 1 else 0.5
        sc = modT[:, (3 * i + 1) * 8:(3 * i + 2) * 8, :]
        gt = modT[:, (3 * i + 2) * 8:(3 * i + 3) * 8, :]
        npre = vec[:, 72 + 8 * i:72 + 8 * i + 8].unsqueeze(2).to_broadcast([128, 8, 2])
        npost = vec[:, 96 + 8 * i:96 + 8 * i + 8].unsqueeze(2).to_broadcast([128, 8, 2])
        k.op("dve", lambda e, sc=sc, npre=npre, i=i: e.scalar_tensor_tensor(out=colA[:, i, :, :], in0=sc, scalar=1.0, in1=npre,
                                                                            op0=ALU.add, op1=ALU.mult),
             r=[("modT", l), ("vec", l)], w=[("colA", l, i)])
        k.op("dve", lambda e, gt=gt, npost=npost, i=i, half=half: e.scalar_tensor_tensor(out=colG[:, i, :, :], in0=gt, scalar=half,
                                                                                         in1=npost, op0=ALU.mult, op1=ALU.mult),
             r=[("modT", l), ("vec", l)], w=[("colG", l, i)])


class NormBufs:
    def __init__(self, k):
        self.sq = k.sb("nb_sq", [128, 8, 512], BF16)
        self.ms = k.ps("nb_ms", [128, 512], F32)
        self.rstd = k.sb("nb_rstd", [128, 512], F32)
        self.tmp = k.sb("nb_tmp", [128, 512], F32)
        self.ones = k.sb("nb_ones", [128, 128], BF16)
        k.op("dve", lambda e: e.memset(self.ones[:, :], 1.0 / D), w=["ones"])


def emit_rstd(k, nb, src_chunks, N, rkeys, nch=KC, ones=None, ones_key="ones"):
    ones = nb.ones if ones is None else ones
    for kc in range(nch):
        k.act(nb.sq[:, kc, :N], src_chunks(kc), AF.Square, r=list(rkeys(kc)), w=[("sq", kc)])
    for kc in range(nch):
        k.mm(nb.ms[:, :N], ones[:, :], nb.sq[:, kc, :N], kc == 0, kc == nch - 1, r=[ones_key, ("sq", kc)], w=["ms"])
    k.act(nb.rstd[:, :N], nb.ms[:, :N], AF.Ln, bias=EPS, r=["ms"], w=["rstd"])
    k.act(nb.rstd[:, :N], nb.rstd[:, :N], AF.Exp, scale=-0.5, r=["rstd"], w=["rstd"])


def emit_prenorm(k, nb, l, i, x, h, t0, N, tix, ti, modT, colA):
    emit_rstd(k, nb, lambda kc: x[:, kc, t0:t0 + N], N, lambda kc: [("x", kc, ti)])
    for kc in range(KC):
        k.op("dve", lambda e, kc=kc: e.scalar_tensor_tensor(out=nb.tmp[:, :N], in0=x[:, kc, t0:t0 + N],
                                                            scalar=colA[:, i, kc, tix:tix + 1], in1=nb.rstd[:, :N],
                                                            op0=ALU.mult, op1=ALU.mult),
             r=[("x", kc, ti), ("colA", l, i), "rstd"], w=["tmp"])
        k.act(h[:, kc, :N], nb.tmp[:, :N], AF.Identity, bias=modT[:, (3 * i) * 8 + kc, tix:tix + 1],
              r=["tmp", ("modT", l)], w=[("h", kc)])


def emit_post_residual(k, nb, l, i, x, y, t0, N, tix, ti, colG):
    emit_rstd(k, nb, lambda kc: y[:, kc, :N], N, lambda kc: [("y", kc)])
    for kc in range(KC):
        k.op("dve", lambda e, kc=kc: e.scalar_tensor_tensor(out=nb.tmp[:, :N], in0=y[:, kc, :N],
                                                            scalar=colG[:, i, kc, tix:tix + 1], in1=nb.rstd[:, :N],
                                                            op0=ALU.mult, op1=ALU.mult),
             r=[("y", kc), ("colG", l, i), "rstd"], w=["tmp"])
        k.op("dve", lambda e, kc=kc: e.tensor_tensor(out=x[:, kc, t0:t0 + N], in0=x[:, kc, t0:t0 + N], in1=nb.tmp[:, :N], op=ALU.add),
             r=["tmp", ("x", kc, ti)], w=[("x", kc, ti)])


class FfnBufs:
    def __init__(self, k):
        self.wg = [k.sb(f"ff_wg{s}", [128, 8, 256], BF16) for s in range(2)]
        self.wu = [k.sb(f"ff_wu{s}", [128, 8, 256], BF16) for s in range(2)]
        self.wd = [k.sb(f"ff_wd{s}", [128, FC, 128], BF16) for s in range(2)]
        self.pg = [k.ps(f"ff_pg{s}", [128, 512], F32) for s in range(2)]
        self.pu = [k.ps(f"ff_pu{s}", [128, 512], F32) for s in range(2)]
        self.py = [k.ps(f"ff_py{s}", [128, 512], F32) for s in range(2)]
        self.sg = k.sb("ff_sg", [128, 512], F32)
        self.a = k.sb("ff_a", [128, FC, 512], BF16)
        self.h = k.sb("ff_h", [128, 8, 512], BF16)
        self.y = k.sb("ff_y", [128, 8, 512], F32)
        self.cnt = 0


def emit_half_ffn(k, nb, fb, l, i, fi, x, modT, colA, colG, wgate, wup, wdown, skip_ctx=False):
    for ti, (t0, N, is_ctx) in enumerate(TT):
        if is_ctx and skip_ctx:
            continue
        tix = 1 if is_ctx else 0
        emit_prenorm(k, nb, l, i, x, fb.h, t0, N, tix, ti, modT, colA)
        for g in range(FC // 2):
            s = fb.cnt % 2
            fb.cnt += 1
            c0 = g * 256
            k.dma("pool", fb.wg[s][:, :, :], wgate[0, 0, :, c0:c0 + 256].rearrange("(kc p) n -> p kc n", p=128), w=[("wg", s)])
            k.dma("pool", fb.wu[s][:, :, :], wup[0, 0, :, c0:c0 + 256].rearrange("(kc p) n -> p kc n", p=128), w=[("wu", s)])
            for j in range(2):
                f = 2 * g + j
                ps = f % 2
                for kc in range(KC):
                    k.mm(fb.pg[ps][:, :N], fb.wg[s][:, kc, j * 128:(j + 1) * 128], fb.h[:, kc, :N], kc == 0, kc == KC - 1,
                         r=[("wg", s), ("h", kc)], w=[("pg", ps)])
                for kc in range(KC):
                    k.mm(fb.pu[ps][:, :N], fb.wu[s][:, kc, j * 128:(j + 1) * 128], fb.h[:, kc, :N], kc == 0, kc == KC - 1,
                         r=[("wu", s), ("h", kc)], w=[("pu", ps)])
                k.act(fb.sg[:, :N], fb.pg[ps][:, :N], AF.Silu, r=[("pg", ps)], w=["sg"])
                k.op("dve", lambda e, f=f, ps=ps: e.tensor_tensor(out=fb.a[:, f, :N], in0=fb.sg[:, :N], in1=fb.pu[ps][:, :N], op=ALU.mult),
                     r=["sg", ("pu", ps)], w=[("a", f)])
        for oc in range(KC):
            s = oc % 2
            k.dma("pool", fb.wd[s][:, :, :], wdown[0, 0, :, oc * 128:(oc + 1) * 128].rearrange("(f p) n -> p f n", p=128), w=[("wd", s)])
            for f in range(FC):
                k.mm(fb.py[s][:, :N], fb.wd[s][:, f, :], fb.a[:, f, :N], f == 0, f == FC - 1, r=[("wd", s), ("a", f)], w=[("py", s)])
            k.act(fb.y[:, oc, :N], fb.py[s][:, :N], AF.Copy, r=[("py", s)], w=[("y", oc)])
        emit_post_residual(k, nb, l, i, x, fb.y, t0, N, tix, ti, colG)


OFF = {"q": 0, "k": 256, "v": 512, "g": 1024, "a": 1536, "four": 1568, "cq": 2080, "ckv": 2464, "kr": 2720, "gates": 2752}
Z_FM = [("q", 0, 256), ("k", 256, 256), ("g", 1024, 512), ("a", 1536, 32), ("cq", 2080, 384), ("ckv", 2464, 256), ("kr", 2720, 32)]
Z_TM = [("v", 512, 512), ("four", 1568, 512), ("ktm", 256, 256)]


class WinBufs:
    def __init__(self, k):
        self.w = [k.sb(f"wi_w{s}", [128, 8, 512], BF16) for s in range(2)]
        self.wkrp = k.sb("wi_wkrp", [128, 8, 32], BF16)
        self.o = [k.sb(f"wi_o{s}", [128, 512], F32) for s in range(2)]
        self.ob = [k.sb(f"wi_ob{s}", [128, 512], BF16) for s in range(2)]
        self.cnt = 0
        self.ocnt = 0


def emit_win(k, wb, fb, l, w_in, hmix, zout):
    for ti, (t0, N, is_ctx) in enumerate(TT):
        def load_w(c0, ncols):
            s = wb.cnt % 2
            wb.cnt += 1
            k.dma("pool", wb.w[s][:, :, :ncols], w_in[0, :, c0:c0 + ncols].rearrange("(kc p) n -> p kc n", p=128), w=[("wiw", s)])
            return s

        def store(ps_ap, key_ps, dst, npart, ncol, as_bf16=True, dkey=None):
            s = wb.ocnt % 2
            wb.ocnt += 1
            buf = wb.ob[s] if as_bf16 else wb.o[s]
            k.act(buf[:npart, :ncol], ps_ap, AF.Copy, r=[key_ps], w=[("wio", as_bf16, s)])
            k.dma("sp", dst, buf[:npart, :ncol], r=[("wio", as_bf16, s)], w=[dkey] if dkey is not None else [])

        for name, c0, nf in Z_FM:
            s = load_w(c0, nf)
            if name == "kr":
                for (d0, s0) in ((0, 8), (8, 0), (16, 24), (24, 16)):
                    k.op("dve", lambda e, s=s, d0=d0, s0=s0: e.tensor_copy(out=wb.wkrp[:, :, d0:d0 + 8], in_=wb.w[s][:, :, s0:s0 + 8]),
                         r=[("wiw", s)], w=["wkrp"])
            for m0 in range(0, nf, 128):
                M = min(128, nf - m0)
                pb = fb.pg[(m0 // 128) % 2]
                pkey = ("pg", (m0 // 128) % 2)
                for kc in range(KC):
                    k.mm(pb[:M, :N], wb.w[s][:, kc, m0:m0 + M], hmix[:, kc, t0:t0 + N], kc == 0, kc == KC - 1,
                         r=[("wiw", s), ("hmix", kc, ti)], w=[pkey])
                f32 = name in ("a", "cq", "ckv", "kr")
                store(pb[:M, :N], pkey, zout[name][m0:m0 + M, t0:t0 + N], M, N, as_bf16=not f32, dkey=("z", name, m0 // 128, ti))
            if name == "kr":
                pb, pkey = fb.pu[0], ("pu", 0)
                for kc in range(KC):
                    k.mm(pb[:32, :N], wb.wkrp[:, kc, :], hmix[:, kc, t0:t0 + N], kc == 0, kc == KC - 1,
                         r=["wkrp", ("hmix", kc, ti)], w=[pkey])
                store(pb[:32, :N], pkey, zout["krp"][:, t0:t0 + N], 32, N, as_bf16=False, dkey=("z", "krp", 0, ti))
        for name, c0, ncol in Z_TM:
            s = load_w(c0, ncol)
            for b0 in range(0, N, 128):
                pb = fb.pu[(b0 // 128) % 2]
                pkey = ("pu", (b0 // 128) % 2)
                for kc in range(KC):
                    k.mm(pb[:, :ncol], hmix[:, kc, t0 + b0:t0 + b0 + 128], wb.w[s][:, kc, :ncol], kc == 0, kc == KC - 1,
                         r=[("wiw", s), ("hmix", kc, ti)], w=[pkey])
                store(pb[:, :ncol], pkey, zout[name][t0 + b0:t0 + b0 + 128, :], 128, ncol, as_bf16=True)


NKEY = CTX + 8192
NKB = NKEY // 128
MH = 8
ATT_SCALE = 96.0 ** -0.5


def emit_mla_attention(k, l, ckvn, krr, cqn, ropeq_cos, ropeq_sin, w_uq, w_ukv, yc_out, ctx_queries):
    with k.phase():
        kv = k.sb("at_ckvn", [128, 2, NKEY], BF16)
        khT = k.sb("at_khT", [96, NKEY], BF16)
        vh = k.sb("at_vh", [128, NKB, 65], BF16)
        wkv = k.sb("at_wkv", [128, 2, 1024], BF16)
        wq = k.sb("at_wq", [128, 3, 768], BF16)
        wqp = k.sb("at_wqp", [128, 3, 32], BF16)
        cq = k.sb("at_cqn", [128, 3, NTOK], BF16)
        cos = k.sb("at_cos", [32, NTOK], F32)
        sin = k.sb("at_sin", [32, NTOK], F32)
        qhT = k.sb("at_qhT", [96, 512], BF16)
        qr = k.sb("at_qr", [32, 512], F32)
        qrp = k.sb("at_qrp", [32, 512], F32)
        qrb = k.sb("at_qrb", [32, 512], BF16)
        pT = [k.sb(f"at_pT{s}", [128, 512], BF16) for s in range(3)]
        osb = k.sb("at_osb", [65, 512], F32)
        sel = k.sb("at_sel", [65, 64], F32)
        rden = k.sb("at_rden", [64, 512], F32)
        ysb = k.sb("at_ysb", [64, 512], BF16)
        ps_s = [k.ps(f"at_ps_s{s}", [128, 512], F32) for s in range(3)]
        ps_o = k.ps("at_ps_o", [128, 512], F32)
        ps_p = [k.ps(f"at_ps_p{s}", [128, 512], F32) for s in range(2)]
        ps_d = k.ps("at_ps_d", [128, 512], F32)

        for c in range(2):
            k.dma("sp", kv[:, c, :], ckvn[c * 128:(c + 1) * 128, :], w=[("kv", c)])
        k.dma("sp", khT[64:96, :], krr[:, :], w=["kh_r"])
        k.dma("pool", wkv[:, :, :], w_ukv.rearrange("(c p) n -> p c n", p=128), w=["wkv"])
        k.dma("pool", wq[:, :, :], w_uq.rearrange("(c p) n -> p c n", p=128), w=["wq"])
        for c in range(3):
            k.dma("sp", cq[:, c, :], cqn[c * 128:(c + 1) * 128, :], w=[("cq", c)])
        k.dma("sp", cos[:, :], ropeq_cos[:, :], w=["cos"])
        k.dma("sp", sin[:, :], ropeq_sin[:, :], w=["sin"])
        k.op("dve", lambda e: e.memset(vh[:, :, 64:65], 1.0), w=["vh_one"])
        k.op("dve", lambda e: e.memset(sel[:, :], 0.0), w=["sel"])
        k.op("dve", lambda e: e.memset(sel[64:65, :], 1.0), r=["sel"], w=["sel"])

        qtiles = [(ti, t0, N, is_ctx) for ti, (t0, N, is_ctx) in enumerate(TT) if (ctx_queries or not is_ctx)]
        for h in range(MH):
            for kb4 in range(0, NKEY, 512):
                pb, pk = ps_p[(kb4 // 512) % 2], ("ps_p", (kb4 // 512) % 2)
                wd_ = min(512, NKEY - kb4)
                for c in range(2):
                    k.mm(pb[:64, :wd_], wkv[:, c, h * 128:h * 128 + 64], kv[:, c, kb4:kb4 + wd_], c == 0, c == 1,
                         r=["wkv", ("kv", c)], w=[pk])
                k.act(khT[0:64, kb4:kb4 + wd_], pb[:64, :wd_], AF.Copy, r=[pk], w=[("kh_n", kb4 // 512)])
            for kb in range(NKB):
                pb, pk = ps_p[kb % 2], ("ps_p", kb % 2)
                for c in range(2):
                    k.mm(pb[:, :64], kv[:, c, kb * 128:(kb + 1) * 128], wkv[:, c, h * 128 + 64:h * 128 + 128], c == 0, c == 1,
                         r=["wkv", ("kv", c)], w=[pk])
                k.op("dve", lambda e, kb=kb, pb=pb: e.tensor_copy(out=vh[:, kb, 0:64], in_=pb[:, :64]), r=[pk], w=[("vh", kb)])
            r0 = h * 96 + 64
            for (d0, s0) in ((0, 8), (8, 0), (16, 24), (24, 16)):
                k.op("dve", lambda e, d0=d0, s0=s0, r0=r0: e.tensor_copy(out=wqp[:, :, d0:d0 + 8], in_=wq[:, :, r0 + s0:r0 + s0 + 8]),
                     r=["wq"], w=["wqp"])
            for (ti, t0, N, is_ctx) in qtiles:
                pb, pk = ps_p[0], ("ps_p", 0)
                for c in range(3):
                    k.mm(pb[:64, :N], wq[:, c, h * 96:h * 96 + 64], cq[:, c, t0:t0 + N], c == 0, c == 2, r=["wq", ("cq", c)], w=[pk])
                k.act(qhT[0:64, :N], pb[:64, :N], AF.Copy, r=[pk], w=["qh_n"])
                pb, pk = ps_p[1], ("ps_p", 1)
                for c in range(3):
                    k.mm(pb[:32, :N], wq[:, c, r0:r0 + 32], cq[:, c, t0:t0 + N], c == 0, c == 2, r=["wq", ("cq", c)], w=[pk])
                k.op("dve", lambda e, pb=pb, t0=t0, N=N: e.tensor_tensor(out=qr[:, :N], in0=pb[:32, :N], in1=cos[:, t0:t0 + N], op=ALU.mult),
                     r=[pk, "cos"], w=["qr"])
                for c in range(3):
                    k.mm(pb[:32, :N], wqp[:, c, :], cq[:, c, t0:t0 + N], c == 0, c == 2, r=["wqp", ("cq", c)], w=[pk])
                k.op("dve", lambda e, pb=pb, t0=t0, N=N: e.tensor_tensor(out=qrp[:, :N], in0=pb[:32, :N], in1=sin[:, t0:t0 + N], op=ALU.mult),
                     r=[pk, "sin"], w=["qrp"])
                k.op("dve", lambda e, N=N: e.tensor_tensor(out=qr[:, :N], in0=qr[:, :N], in1=qrp[:, :N], op=ALU.add), r=["qr", "qrp"], w=["qr"])
                k.op("dve", lambda e, N=N: e.tensor_copy(out=qrb[:, :N], in_=qr[:, :N]), r=["qr"], w=["qr_bf"])
                k.dma("sp", qhT[64:96, :N], qrb[:, :N], r=["qr_bf"], w=["qh_r"])
                kbs = list(range(CTX // 128)) if is_ctx else list(range(NKB))
                LA = 2
                nkb = len(kbs)

                def _S(j):
                    s, kb = j % 3, kbs[j]
                    k.mm(ps_s[s][:, :N], khT[0:96, kb * 128:(kb + 1) * 128], qhT[0:96, :N], True, True,
                         r=[("kh_n", kb // 4), "kh_r", "qh_n", "qh_r"], w=[("ps_s", s)])

                for j in range(min(LA, nkb)):
                    _S(j)
                for j in range(nkb):
                    s, kb = j % 3, kbs[j]
                    k.act(pT[s][:, :N], ps_s[s][:, :N], AF.Exp, scale=ATT_SCALE, r=[("ps_s", s)], w=[("pT", s)])
                    if j + LA < nkb:
                        _S(j + LA)
                    k.mm(ps_o[:65, :N], vh[:, kb, 0:65], pT[s][:, :N], j == 0, j == nkb - 1,
                         r=[("vh", kb), "vh_one", ("pT", s)], w=["ps_o"])
                k.act(osb[:, :N], ps_o[:65, :N], AF.Copy, r=["ps_o"], w=["osb"])
                k.mm(ps_d[:64, :N], sel[:, :], osb[:, :N], True, True, r=["sel", "osb"], w=["ps_d"])
                k.op("dve", lambda e, N=N: e.reciprocal(out=rden[:, :N], in_=ps_d[:64, :N]), r=["ps_d"], w=["rden"])
                k.op("dve", lambda e, N=N: e.tensor_tensor(out=ysb[:, :N], in0=osb[0:64, :N], in1=rden[:, :N], op=ALU.mult),
                     r=["osb", "rden"], w=["ysb"])
                k.dma("sp", yc_out[h * 64:(h + 1) * 64, t0:t0 + N], ysb[:, :N], r=["ysb"])


GH = 4
NBLK = NTOK // 128
NCHK = NTOK // 64
GLA_QSCALE = 64.0 ** -0.5


def emit_gla(k, l, mode, zq, zk, zk_tm, zv_tm, zg, za, w_dec, b_dec, g_norm_col, tri, carry_D=None, carry_S=None, ya_out=None, summ_D=None, summ_S=None):
    assert mode in ("summ", "full")
    with k.phase():
        trit = k.sb("gl_tri", [128, 4, 128], F32)
        for i in range(4):
            k.dma("sp", trit[:, i, :], tri[i], w=[("tri", i)])
        maskb = k.sb("gl_maskb", [128, 2, 128], BF16)
        for d in range(2):
            k.op("dve", lambda e, d=d: e.tensor_copy(out=maskb[:, d, :], in_=trit[:, 2 * d, :]), r=[("tri", 2 * d)], w=[("maskb", d)])
        ones_b = k.sb("gl_ones", [128, 128], BF16)
        k.op("dve", lambda e: e.memset(ones_b[:, :], 1.0 / 128.0), w=["gl_ones"])
        aaug = [k.sb(f"gl_aaug{d}", [17, NTOK], F32) for d in range(2)]
        waug = [k.sb(f"gl_waug{d}", [17, 256], F32) for d in range(2)]
        for d in range(2):
            k.op("dve", lambda e, d=d: e.memset(aaug[d][:, :], 1.0), w=[("aaug", d)])
            k.dma("sp", aaug[d][0:16, :], za[16 * d:16 * d + 16, :], r=[("aaug", d)], w=[("aaug", d)])
            k.dma("sp", waug[d][0:16, :], w_dec[d], w=[("waug", d)])
            k.dma("sp", waug[d][16:17, :], b_dec[d:d + 1, :], r=[("waug", d)], w=[("waug", d)])
        qT = k.sb("gl_qT", [64, NTOK], BF16)
        kT = k.sb("gl_kT", [64, NTOK], BF16)
        ktm = k.sb("gl_ktm", [128, NBLK, 64], BF16)
        vtm = k.sb("gl_vtm", [128, NBLK, 128], BF16)
        gT = k.sb("gl_gT", [128, NTOK], BF16)
        cumT = [k.sb(f"gl_cumT{d}", [64, NTOK], F32) for d in range(2)]
        sall = [k.sb(f"gl_sall{d}", [64, NCHK, 128], BF16) for d in range(2)]
        sfp = k.sb("gl_sfp", [64, 128], F32)
        la = k.sb("gl_la", [128, 64], F32)
        rc = k.sb("gl_rc", [128, 64], F32)
        khat = k.sb("gl_khat", [128, 64], BF16)
        etot = k.sb("gl_etot", [64, 2], F32)
        tsum = k.sb("gl_tsum", [64, 1], F32)
        cD = k.sb("gl_cD", [64, 3], F32)
        cS = k.sb("gl_cS", [64, 3, 128], F32)
        qs = k.sb("gl_qs", [64, 128], BF16)
        ks = k.sb("gl_ks", [64, 128], BF16)
        ecum = k.sb("gl_ecum", [64, 128], F32)
        aT = k.sb("gl_aT", [128, 128], BF16)
        osb = k.sb("gl_osb", [128, 128], F32)
        osq = k.sb("gl_osq", [128, 128], BF16)
        rstd = k.sb("gl_rstd", [128, 128], F32)
        sg = k.sb("gl_sg", [128, 128], F32)
        yb = k.sb("gl_yb", [128, 128], BF16)
        dsum = k.sb("gl_dsum", [64, 1], F32)
        p_la = k.ps("gl_p_la", [128, 512], F32)
        p_cum = k.ps("gl_p_cum", [128, 512], F32)
        p_rc = k.ps("gl_p_rc", [128, 512], F32)
        p_dl = [k.ps(f"gl_p_dl{c}", [128, 512], F32) for c in range(2)]
        p_at = k.ps("gl_p_at", [128, 512], F32)
        p_o = k.ps("gl_p_o", [128, 512], F32)
        p_n = k.ps("gl_p_n", [128, 512], F32)

        def blocks_in_order(d, part):
            bl = [0, 1] if part == "ctx" else list(range(2, NBLK))
            return bl if d == 0 else bl[::-1]

        for h in range(GH):
            hk = ("h", h)
            k.dma("sp", qT[:, :], zq[h * 64:(h + 1) * 64, :], w=["qT"])
            k.dma("sp", kT[:, :], zk[h * 64:(h + 1) * 64, :], w=["kT"])
            k.dma("sp", ktm[:, :, :], zk_tm[:, h * 64:(h + 1) * 64].rearrange("(b p) d -> p b d", p=128), w=["ktm"])
            k.dma("sp", vtm[:, :, :], zv_tm[:, h * 128:(h + 1) * 128].rearrange("(b p) e -> p b e", p=128), w=["vtm"])
            k.dma("sp", gT[:, :], zg[h * 128:(h + 1) * 128, :], w=["gT"])
            for d in range(2):
                if mode == "full":
                    for c3 in range(3):
                        k.dma("sp", cD[:, c3:c3 + 1], carry_D[d, c3, h].rearrange("(p o) -> p o", o=1), w=[("cD", c3)])
                        k.dma("sp", cS[:, c3, :], carry_S[d, c3, h], w=[("cS", c3)])
                for part in (("lat",) if mode == "summ" else ("ctx", "lat")):
                    if part == "ctx" or mode == "summ":
                        k.op("dve", lambda e: e.memset(sfp[:, :], 0.0), w=["sfp"])
                    if mode == "summ":
                        k.op("dve", lambda e: e.memset(dsum[:, :], 0.0), w=["dsum"])
                    if mode == "full" and part == "lat":
                        for c3 in range(3):
                            k.op("dve", lambda e, c3=c3: e.scalar_tensor_tensor(out=sfp[:, :], in0=sfp[:, :], scalar=cD[:, c3:c3 + 1],
                                                                                in1=cS[:, c3, :], op0=ALU.mult, op1=ALU.add),
                                 r=["sfp", ("cD", c3), ("cS", c3)], w=["sfp"])
                    if True:
                        for b in blocks_in_order(d, part):
                            tk = slice(b * 128, (b + 1) * 128)
                            k.mm(p_la[:, :64], aaug[d][:, tk], waug[d][:, h * 64:(h + 1) * 64], True, True, r=[("aaug", d), ("waug", d)], w=["p_la"])
                            k.act(la[:, :], p_la[:, :64], AF.Exp, scale=-1.0, r=["p_la"], w=["la"])
                            k.act(la[:, :], la[:, :], AF.Ln, bias=1.0, r=["la"], w=["la"])
                            k.act(la[:, :], la[:, :], AF.Copy, scale=-1.0 / 16.0, r=["la"], w=["la"])
                            k.mm(p_cum[:64, :128], la[:, :], trit[:, 2 * d, :], True, True, r=["la", ("tri", 2 * d)], w=["p_cum"])
                            k.act(cumT[d][:, tk], p_cum[:64, :128], AF.Copy, r=["p_cum"], w=[("cumT", d, b)])
                            k.mm(p_rc[:, :64], trit[:, 2 * d + 1, :], la[:, :], True, True, r=["la", ("tri", 2 * d + 1)], w=["p_rc"])
                            k.act(rc[:, :], p_rc[:, :64], AF.Exp, r=["p_rc"], w=["rc"])
                            k.op("dve", lambda e, b=b: e.tensor_tensor(out=khat[:, :], in0=ktm[:, b, :], in1=rc[:, :], op=ALU.mult), r=["ktm", "rc"], w=["khat"])
                            cols = (63, 127) if d == 0 else (0, 64)
                            for c in range(2):
                                k.act(etot[:, c:c + 1], p_cum[:64, cols[c]:cols[c] + 1], AF.Exp, r=["p_cum"], w=[("etot", c)])
                                if mode == "summ":
                                    k.op("dve", lambda e, c=c, cols=cols: e.tensor_tensor(out=dsum[:, :], in0=dsum[:, :], in1=p_cum[:64, cols[c]:cols[c] + 1], op=ALU.add),
                                         r=["p_cum", "dsum"], w=["dsum"])
                            for c in range(2):
                                k.mm(p_dl[c][:64, :128], khat[c * 64:(c + 1) * 64, :], vtm[c * 64:(c + 1) * 64, b, :], True, True, r=["khat", "vtm"], w=[("p_dl", c)])
                            for c in ((0, 1) if d == 0 else (1, 0)):
                                ch = 2 * b + c
                                if mode == "full":
                                    k.op("dve", lambda e, ch=ch, d=d: e.tensor_copy(out=sall[d][:, ch, :], in_=sfp[:, :]), r=["sfp"], w=[("sall", d, ch)])
                                k.op("dve", lambda e, c=c: e.scalar_tensor_tensor(out=sfp[:, :], in0=sfp[:, :], scalar=etot[:, c:c + 1], in1=p_dl[c][:64, :128],
                                                                                  op0=ALU.mult, op1=ALU.add), r=["sfp", ("etot", c), ("p_dl", c)], w=["sfp"])
                        if mode == "summ":
                            k.act(dsum[:, :], dsum[:, :], AF.Exp, r=["dsum"], w=["dsum"])
                            k.dma("sp", summ_D[d, h].rearrange("(p o) -> p o", o=1), dsum[:, :], r=["dsum"])
                            k.dma("sp", summ_S[d, h], sfp[:, :], r=["sfp"])
            for b in (range(NBLK) if mode == "full" else ()):
                tk = slice(b * 128, (b + 1) * 128)
                nmm = 0
                for d in range(2):
                    k.act(ecum[:, :], cumT[d][:, tk], AF.Exp, r=[("cumT", d, b)], w=["ecum"])
                    k.op("dve", lambda e, tk=tk: e.scalar_tensor_tensor(out=qs[:, :], in0=qT[:, tk], scalar=GLA_QSCALE, in1=ecum[:, :], op0=ALU.mult, op1=ALU.mult),
                         r=["qT", "ecum"], w=["qs"])
                    k.act(ecum[:, :], cumT[d][:, tk], AF.Exp, scale=-1.0, r=[("cumT", d, b), "qs"], w=["ecum"])
                    k.op("dve", lambda e, tk=tk: e.tensor_tensor(out=ks[:, :], in0=kT[:, tk], in1=ecum[:, :], op=ALU.mult), r=["kT", "ecum"], w=["ks"])
                    k.mm(p_at[:, :128], ks[:, :], qs[:, :], True, True, r=["ks", "qs"], w=["p_at"])
                    k.op("dve", lambda e, d=d: e.tensor_tensor(out=aT[:, :], in0=p_at[:, :128], in1=maskb[:, d, :], op=ALU.mult), r=["p_at", ("maskb", d)], w=["aT"])
                    k.mm(p_o[:, :128], vtm[:, b, :], aT[:, :], nmm == 0, False, r=["vtm", "aT"], w=["p_o"])
                    nmm += 1
                    for c in range(2):
                        ch = 2 * b + c
                        last = (d == 1 and c == 1)
                        k.mm(p_o[:, c * 64:(c + 1) * 64], sall[d][:, ch, :], qs[:, c * 64:(c + 1) * 64], False, last, r=[("sall", d, ch), "qs"], w=["p_o"])
                k.act(osb[:, :], p_o[:, :128], AF.Copy, r=["p_o"], w=["osb"])
                k.act(osq[:, :], p_o[:, :128], AF.Square, r=["p_o"], w=["osq"])
                k.mm(p_n[:, :128], ones_b[:, :], osq[:, :], True, True, r=["gl_ones", "osq"], w=["p_n"])
                k.act(rstd[:, :], p_n[:, :128], AF.Ln, bias=EPS, r=["p_n"], w=["rstd"])
                k.act(rstd[:, :], rstd[:, :], AF.Exp, scale=-0.5, r=["rstd"], w=["rstd"])
                k.act(sg[:, :], gT[:, tk], AF.Silu, r=["gT"], w=["sg"])
                k.op("dve", lambda e: e.scalar_tensor_tensor(out=osb[:, :], in0=osb[:, :], scalar=g_norm_col[:, 0:1], in1=rstd[:, :], op0=ALU.mult, op1=ALU.mult),
                     r=["osb", "rstd", "gnorm"], w=["osb"])
                k.op("dve", lambda e: e.tensor_tensor(out=yb[:, :], in0=osb[:, :], in1=sg[:, :], op=ALU.mult), r=["osb", "sg"], w=["yb"])
                k.dma("sp", ya_out[h * 128:(h + 1) * 128, tk], yb[:, :], r=["yb"])


def fourier_tables(seg):
    n1 = np.arange(64)[:, None]; k1 = np.arange(64)[None, :]
    w64 = np.concatenate([np.cos(2 * np.pi * n1 * k1 / 64), -np.sin(2 * np.pi * n1 * k1 / 64)], 1)
    c = np.arange(128)[:, None]; cp = np.arange(128)[None, :]
    Cc, Sc = np.cos(2 * np.pi * c * cp / 128), np.sin(2 * np.pi * c * cp / 128)
    fca = np.concatenate([Cc, -Sc], 1); fcb = np.concatenate([Sc, Cc], 1)
    n2 = np.arange(128)[:, None, None]; kk = (np.arange(64)[None, :, None] + 64 * (32 * seg + np.arange(32))[None, None, :])
    th = 2 * np.pi * ((n2 * kk) % 8192) / 8192
    e3c, e3s = np.cos(th), np.sin(th)
    n = np.arange(256)[:, None]; k = np.arange(256)[None, :]
    th2 = 2 * np.pi * ((n * k) % 256) / 256
    e256 = np.concatenate([np.cos(th2), -np.sin(th2)], 1)
    f = lambda a: np.ascontiguousarray(a, dtype=np.float32)
    return {"w64": f(w64), "fca": f(fca), "fcb": f(fcb), "e3c": f(e3c.reshape(128, 2048)), "e3s": f(e3s.reshape(128, 2048)), "e256": f(e256)}


def emit_fourier(k, g_all, g_ctx, T, yb_out):
    with k.phase():
        w64 = k.sb("fo_w64", [64, 128], BF16); fca = k.sb("fo_fca", [128, 256], BF16); fcb = k.sb("fo_fcb", [128, 256], BF16)
        e3c = k.sb("fo_e3c", [128, 64, 32], BF16); e3s = k.sb("fo_e3s", [128, 64, 32], BF16)
        k.dma("sp", w64[:, :], T["w64"][:, :], w=["w64"]); k.dma("sp", fca[:, :], T["fca"][:, :], w=["fca"]); k.dma("sp", fcb[:, :], T["fcb"][:, :], w=["fcb"])
        k.dma("sp", e3c[:, :, :], T["e3c"].rearrange("p (a b) -> p a b", b=32), w=["e3c"])
        k.dma("sp", e3s[:, :, :], T["e3s"].rearrange("p (a b) -> p a b", b=32), w=["e3s"])
        g = k.sb("fo_g", [64, 128, 128], BF16)
        A = k.sb("fo_A", [128, 128, 128], BF16)
        B = k.sb("fo_B", [128, 64, 256], BF16)
        yt = k.sb("fo_yt", [128, 2048], BF16)
        ps = [k.ps(f"fo_ps{i}", [128, 512], F32) for i in range(4)]
        pi = [0]

        def nextps():
            i = pi[0] % 4
            pi[0] += 1
            return ps[i], ("fo_ps", i)

        for grp in range(4):
            k.dma("sp", g[:, :, :], g_all[:, grp * 128:(grp + 1) * 128].rearrange("(n1 n2) c -> n1 n2 c", n2=128), w=["fo_g"])
            for q in range(32):
                pb, pk = nextps()
                for j in range(4):
                    n2 = 4 * q + j
                    k.mm(pb[:, j * 128:(j + 1) * 128], g[:, n2, :], w64[:, :], True, True, r=["fo_g", "w64"], w=[pk])
                eng = "act" if q % 2 == 0 else "dve"
                src = pb[:, :].rearrange("p (n r) -> p r n", n=4)
                dst = A[:, :, 4 * q:4 * q + 4]
                if eng == "act":
                    k.act(dst, src, AF.Copy, r=[pk], w=[("fo_A", q)])
                else:
                    k.op("dve", lambda e, dst=dst, src=src: e.tensor_copy(out=dst, in_=src), r=[pk], w=[("fo_A", q)])
            allA = [("fo_A", q) for q in range(32)]
            for kp in range(32):
                pb, pk = nextps()
                for j in range(2):
                    k1 = 2 * kp + j
                    k.mm(pb[:, j * 256:(j + 1) * 256], A[:, k1, :], fca[:, :], True, False, r=allA + ["fca"], w=[pk])
                    k.mm(pb[:, j * 256:(j + 1) * 256], A[:, 64 + k1, :], fcb[:, :], False, True, r=allA + ["fcb"], w=[pk])
                dst = B[:, 2 * kp:2 * kp + 2, :]
                src = pb[:, :].rearrange("p (j x) -> p j x", j=2)
                if kp % 2 == 0:
                    k.act(dst, src, AF.Copy, r=[pk], w=[("fo_B", kp)])
                else:
                    k.op("dve", lambda e, dst=dst, src=src: e.tensor_copy(out=dst, in_=src), r=[pk], w=[("fo_B", kp)])
            for kq in range(4):
                pb, pk = nextps()
                for j in range(16):
                    k1 = 16 * kq + j
                    k.mm(pb[:, j * 32:(j + 1) * 32], B[:, k1, 0:128], e3c[:, k1, :], True, False, r=[("fo_B", k1 // 2), "e3c"], w=[pk])
                    k.mm(pb[:, j * 32:(j + 1) * 32], B[:, k1, 128:256], e3s[:, k1, :], False, True, r=[("fo_B", k1 // 2), "e3s"], w=[pk])
                src = pb[:, :].rearrange("p (k1 k2) -> p k2 k1", k2=32)
                dst = yt[:, :].rearrange("p (k2 k1) -> p k2 k1", k1=64)[:, :, 16 * kq:16 * kq + 16]
                k.act(dst, src, AF.Copy, scale=1.0 / 1024.0, r=[pk], w=[("fo_yt", kq)])
            k.dma("sp", yb_out[grp * 128:(grp + 1) * 128, CTX:NTOK], yt[:, :], r=[("fo_yt", kq) for kq in range(4)])
        if g_ctx is not None:
            e256 = k.sb("fo_e256", [128, 2, 512], BF16)
            k.dma("sp", e256[:, :, :], T["e256"].rearrange("(b p) x -> p b x", p=128), w=["e256"])
            gc = k.sb("fo_gc", [128, 2, 512], BF16)
            k.dma("sp", gc[:, :, :], g_ctx.rearrange("(b p) c -> p b c", p=128), w=["fo_gc"])
            P = k.sb("fo_P", [128, 512], BF16)
            yc = k.sb("fo_yc", [128, 256], BF16)
            for grp in range(4):
                pb, pk = nextps()
                for b in range(2):
                    k.mm(pb[:, :], gc[:, b, grp * 128:(grp + 1) * 128], e256[:, b, :], b == 0, b == 1, r=["fo_gc", "e256"], w=[pk])
                k.act(P[:, :], pb[:, :], AF.Copy, r=[pk], w=["fo_P"])
                pb2, pk2 = nextps()
                k.mm(pb2[:, :256], fca[:, 0:128], P[:, 0:256], True, False, r=["fca", "fo_P"], w=[pk2])
                k.mm(pb2[:, :256], fcb[:, 0:128], P[:, 256:512], False, True, r=["fcb", "fo_P"], w=[pk2])
                k.act(yc[:, :], pb2[:, :256], AF.Copy, scale=float((256.0 * 128.0) ** -0.5), r=[pk2], w=["fo_yc"])
                k.dma("sp", yb_out[grp * 128:(grp + 1) * 128, 0:CTX], yc[:, :], r=["fo_yc"])


def emit_merge(k, nb, l, x, hmix_d, y_d, w_in, w_branch, w_out, colG, with_ctx):
    with k.phase():
        wbr = k.sb("mg_wbr", [128, 12, 1024], BF16)
        wo = k.sb("mg_wo", [128, 8, 1024], BF16)
        k.dma("pool", wbr[:, :, :], w_branch[0].rearrange("i (c p) n -> p (i c) n", p=128), w=["wbr"])
        k.dma("pool", wo[:, :, :], w_out[0].rearrange("(c p) n -> p c n", p=128), w=["wo"])
        wg = [k.sb(f"mg_wg{s}", [128, 8, 512], BF16) for s in range(2)]
        hm = k.sb("mg_hm", [128, 8, 512], BF16)
        yt = k.sb("mg_yt", [128, 12, 512], BF16)
        sgm = k.sb("mg_sgm", [128, 512], F32)
        tmp = k.sb("mg_tmp", [128, 512], F32)
        macc = k.sb("mg_macc", [128, 512], F32)
        mT = k.sb("mg_mT", [128, 8, 512], BF16)
        u = k.sb("mg_u", [128, 8, 512], F32)
        pg = [k.ps(f"mg_pg{s}", [128, 512], F32) for s in range(2)]
        pbch = [k.ps(f"mg_pb{s}", [128, 512], F32) for s in range(2)]
        pu = [k.ps(f"mg_pu{s}", [128, 512], F32) for s in range(2)]
        cnt = 0
        for ti, (t0, N, is_ctx) in enumerate(TT):
            if is_ctx and not with_ctx:
                continue
            tix = 1 if is_ctx else 0
            for c in range(KC):
                k.dma("sp", hm[:, c, :N], hmix_d[c * 128:(c + 1) * 128, t0:t0 + N], w=[("mg_hm", c)])
            for i in range(3):
                for c in range(4):
                    k.dma("sp", yt[:, i * 4 + c, :N], y_d[i][c * 128:(c + 1) * 128, t0:t0 + N], w=[("mg_yt", i * 4 + c)])
            for j in range(6):
                s = cnt % 2
                cnt += 1
                c0 = OFF["gates"] + j * 512
                k.dma("pool", wg[s][:, :, :], w_in[0, :, c0:c0 + 512].rearrange("(kc p) n -> p kc n", p=128), w=[("mg_wg", s)])
                i = j // 2
                for q in range(4):
                    oc = 4 * (j % 2) + q
                    b2 = q % 2
                    for kc in range(KC):
                        k.mm(pg[b2][:, :N], wg[s][:, kc, q * 128:(q + 1) * 128], hm[:, kc, :N], kc == 0, kc == KC - 1,
                             r=[("mg_wg", s), ("mg_hm", kc)], w=[("mg_pg", b2)])
                    for c in range(4):
                        k.mm(pbch[b2][:, :N], wbr[:, i * 4 + c, oc * 128:(oc + 1) * 128], yt[:, i * 4 + c, :N], c == 0, c == 3,
                             r=["wbr", ("mg_yt", i * 4 + c)], w=[("mg_pb", b2)])
                    k.act(sgm[:, :N], pg[b2][:, :N], AF.Sigmoid, r=[("mg_pg", b2)], w=["mg_sgm"])
                    if i == 0:
                        k.op("dve", lambda e, oc=oc, b2=b2, N=N: e.tensor_tensor(out=u[:, oc, :N], in0=sgm[:, :N], in1=pbch[b2][:, :N], op=ALU.mult),
                             r=["mg_sgm", ("mg_pb", b2)], w=[("mg_u", oc)])
                    else:
                        k.op("dve", lambda e, b2=b2, N=N: e.tensor_tensor(out=tmp[:, :N], in0=sgm[:, :N], in1=pbch[b2][:, :N], op=ALU.mult),
                             r=["mg_sgm", ("mg_pb", b2)], w=["mg_tmp"])
                        k.op("dve", lambda e, oc=oc, N=N: e.tensor_tensor(out=u[:, oc, :N], in0=u[:, oc, :N], in1=tmp[:, :N], op=ALU.add),
                             r=["mg_tmp", ("mg_u", oc)], w=[("mg_u", oc)])
            for oc in range(KC):
                k.op("dve", lambda e, oc=oc, N=N: e.tensor_copy(out=mT[:, oc, :N], in_=u[:, oc, :N]), r=[("mg_u", oc)], w=[("mg_mT", oc)])
            for oc in range(KC):
                s = oc % 2
                for c in range(KC):
                    k.mm(pu[s][:, :N], wo[:, c, oc * 128:(oc + 1) * 128], mT[:, c, :N], c == 0, c == KC - 1, r=["wo", ("mg_mT", c)], w=[("mg_pu", s)])
                k.act(u[:, oc, :N], pu[s][:, :N], AF.Copy, r=[("mg_pu", s)] + [("mg_mT", c) for c in range(KC)], w=[("y", oc), ("mg_u", oc)])
            emit_post_residual(k, nb, l, 1, x, u, t0, N, tix, ti, colG)


def emit_pre_exchange(k, nb, zout, qn_col, kvn_col, rope_cos, rope_sin, cqn_d, ckvn_d, krr_d):
    with k.phase():
        on3 = k.sb("px_on3", [128, 128], BF16); on2 = k.sb("px_on2", [128, 128], BF16)
        k.op("dve", lambda e: e.memset(on3[:, :], 1.0 / 384.0), w=["px_on3"])
        k.op("dve", lambda e: e.memset(on2[:, :], 1.0 / 256.0), w=["px_on2"])
        src = k.sb("px_src", [128, 3, 512], F32); ob = k.sb("px_ob", [128, 512], BF16)
        kr = k.sb("px_kr", [32, 512], F32); krp = k.sb("px_krp", [32, 512], F32)
        cs = k.sb("px_cos", [32, NTOK], F32); sn = k.sb("px_sin", [32, NTOK], F32)
        k.dma("sp", cs[:, :], rope_cos[:, :], w=["px_cos"]); k.dma("sp", sn[:, :], rope_sin[:, :], w=["px_sin"])
        for ti, (t0, N, is_ctx) in enumerate(TT):
            for (name, nch, ones, okey, col, ckey, dst) in (("cq", 3, on3, "px_on3", qn_col, "qn_col", cqn_d), ("ckv", 2, on2, "px_on2", kvn_col, "kvn_col", ckvn_d)):
                for c in range(nch):
                    k.dma("sp", src[:, c, :N], zout[name][c * 128:(c + 1) * 128, t0:t0 + N], r=[("z", name, c, ti)], w=[("px_src", c)])
                emit_rstd(k, nb, lambda c: src[:, c, :N], N, lambda c: [("px_src", c)], nch=nch, ones=ones, ones_key=okey)
                for c in range(nch):
                    k.op("dve", lambda e, c=c, col=col, N=N: e.scalar_tensor_tensor(out=ob[:, :N], in0=src[:, c, :N], scalar=col[:, c:c + 1], in1=nb.rstd[:, :N],
                                                                                   op0=ALU.mult, op1=ALU.mult), r=[("px_src", c), ckey, "rstd"], w=["px_ob"])
                    k.dma("sp", dst[c * 128:(c + 1) * 128, t0:t0 + N], ob[:, :N], r=["px_ob"])
            k.dma("sp", kr[:, :N], zout["kr"][:, t0:t0 + N], r=[("z", "kr", 0, ti)], w=["px_kr"])
            k.dma("sp", krp[:, :N], zout["krp"][:, t0:t0 + N], r=[("z", "krp", 0, ti)], w=["px_krp"])
            k.op("dve", lambda e, t0=t0, N=N: e.tensor_tensor(out=kr[:, :N], in0=kr[:, :N], in1=cs[:, t0:t0 + N], op=ALU.mult), r=["px_kr", "px_cos"], w=["px_kr"])
            k.op("dve", lambda e, t0=t0, N=N: e.tensor_tensor(out=krp[:, :N], in0=krp[:, :N], in1=sn[:, t0:t0 + N], op=ALU.mult), r=["px_krp", "px_sin"], w=["px_krp"])
            k.op("dve", lambda e, N=N: e.tensor_tensor(out=ob[0:32, :N], in0=kr[:, :N], in1=krp[:, :N], op=ALU.add), r=["px_kr", "px_krp"], w=["px_ob"])
            k.dma("sp", krr_d[:, t0:t0 + N], ob[0:32, :N], r=["px_ob"])


Z_SPECS = ([(n, [nf, NTOK], ("f32" if n in ("a", "cq", "ckv", "kr") else "bf16")) for n, _, nf in Z_FM] + [("krp", [32, NTOK], "f32")]
           + [(n, [NTOK, nc_], "bf16") for n, _, nc_ in Z_TM])
EXCH_SPECS = [("cqn", [384, NTOK], "bf16"), ("ckvn", [256, NTOK], "bf16"), ("krr", [32, NTOK], "bf16"),
              ("summ_D", [2, GH, 64], "f32"), ("summ_S", [2, GH, 64, 128], "f32")]
LAY_SPECS = [("vec", [128, 128]), ("modT", [128, 144]), ("colA", [128, 48]), ("colG", [128, 48])]
FT_SHAPES = {"w64": [64, 128], "fca": [128, 256], "fcb": [128, 256], "e3c": [128, 2048], "e3s": [128, 2048], "e256": [256, 512]}


def _dt(s):
    return F32 if s == "f32" else BF16


def build_launch(stage, debug=False):
    import contextlib
    with contextlib.ExitStack() as st:
        k = K(st)
        do_back = stage in ("B", "C")
        do_front = stage in ("A", "B")
        lb = {"B": 0, "C": 1}.get(stage)
        lf = {"A": 0, "B": 1}.get(stage)
        ident = k.dram_in("ident", [128, 128])
        xin = k.dram_in("xin", [D, NTOK])
        xout = k.dram_out("xout", [D, NTOK])
        idf = k.sb("idf", [128, 128], F32)
        k.dma("sp", idf[:, :], ident[:, :], w=["ident"])
        x = None

        def load_x():
            xt = k.sb("x", [128, 8, NTOK], F32)
            for kc in range(KC):
                for ti, (t0, N, _) in enumerate(TT):
                    k.dma("sp", xt[:, kc, t0:t0 + N], xin[kc * 128:(kc + 1) * 128, t0:t0 + N], w=[("x", kc, ti)])
            return xt

        if do_back:
            l = 0
            zin = {n: k.dram_in("zi_" + n, s, _dt(d)) for n, s, d in Z_SPECS}
            hmix_in = k.dram_in("hmix_in", [D, NTOK], BF16)
            cqn_in = k.dram_in("cqn_in", [384, NTOK], BF16)
            ckvn_all = k.dram_in("ckvn_all", [256, NKEY], BF16)
            krr_all = k.dram_in("krr_all", [32, NKEY], BF16)
            g_all = k.dram_in("g_all", [8192, 512], BF16)
            carry_D = k.dram_in("carry_D", [2, 3, GH, 64]); carry_S = k.dram_in("carry_S", [2, 3, GH, 64, 128])
            tri = k.dram_in("tri", [4, 128, 128])
            rq_cos = k.dram_in("rope_cos", [32, NTOK]); rq_sin = k.dram_in("rope_sin", [32, NTOK])
            FT = {n: k.dram_in("ft_" + n, s, BF16) for n, s in FT_SHAPES.items()}
            lay_in = {n: k.dram_in("layi_" + n, s) for n, s in LAY_SPECS}
            w_dec = k.dram_in("b_w_dec", [2, 16, 256]); b_dec = k.dram_in("b_b_dec", [2, 256])
            w_uq = k.dram_in("b_w_uq", [384, 768]); w_ukv = k.dram_in("b_w_ukv", [256, 1024])
            w_in_b = k.dram_in("b_w_in", [1, D, DIN]); w_branch = k.dram_in("b_w_branch", [1, 3, 512, D]); w_out = k.dram_in("b_w_out", [1, D, D])
            wg_b = k.dram_in("b_wg", [1, 1, D, DFF]); wu_b = k.dram_in("b_wu", [1, 1, D, DFF]); wd_b = k.dram_in("b_wd", [1, 1, DFF, D])
            if debug:
                y_d = [k.dram_out(f"y_br{i}", [512, NTOK], BF16) for i in range(3)]
                xmix_d = k.dram_out("xmix", [D, NTOK]); xmid_d = k.dram_out("xmid", [D, NTOK])
            else:
                y_d = [k.nc.dram_tensor(f"y_br{i}", [512, NTOK], BF16).ap() for i in range(3)]
            bvec = k.sb("b_vec", [128, 128], F32); bmodT = k.sb("b_modT", [128, 72, 2], F32)
            bcolA = k.sb("b_colA", [128, 3, 8, 2], F32); bcolG = k.sb("b_colG", [128, 3, 8, 2], F32)
            k.dma("sp", bvec[:, :], lay_in["vec"][:, :], w=[("vec", "b"), "gnorm"])
            k.dma("sp", bmodT[:, :, :], lay_in["modT"].rearrange("p (j t) -> p j t", t=2), w=[("modT", "b")])
            k.dma("sp", bcolA[:, :, :, :], lay_in["colA"].rearrange("p (i c t) -> p i c t", i=3, c=8), w=[("colA", "b", i) for i in range(3)])
            k.dma("sp", bcolG[:, :, :, :], lay_in["colG"].rearrange("p (i c t) -> p i c t", i=3, c=8), w=[("colG", "b", i) for i in range(3)])
            with_ctx = (stage == "B")
            emit_gla(k, "b", "full", zin["q"], zin["k"], zin["ktm"], zin["v"], zin["g"], zin["a"], w_dec, b_dec, bvec[:, 120:121], tri,
                     carry_D=carry_D, carry_S=carry_S, ya_out=y_d[0])
            emit_fourier(k, g_all, zin["four"][0:CTX, :] if with_ctx else None, FT, y_d[1])
            emit_mla_attention(k, "b", ckvn_all, krr_all, cqn_in, rq_cos, rq_sin, w_uq, w_ukv, y_d[2], ctx_queries=with_ctx)
            x = load_x()
            with k.phase():
                nb = NormBufs(k)
                emit_merge(k, nb, "b", x, hmix_in, y_d, w_in_b, w_branch, w_out, bcolG, with_ctx)
                if debug:
                    for kc in range(KC):
                        k.dma("sp", xmix_d[kc * 128:(kc + 1) * 128, :], x[:, kc, :], r=[("x", kc, ti) for ti in range(len(TT))])
            with k.phase():
                nb = NormBufs(k); fb = FfnBufs(k)
                emit_half_ffn(k, nb, fb, "b", 2, 0, x, bmodT, bcolA, bcolG, wg_b, wu_b, wd_b, skip_ctx=not with_ctx)
                if debug:
                    for kc in range(KC):
                        k.dma("sp", xmid_d[kc * 128:(kc + 1) * 128, :], x[:, kc, :], r=[("x", kc, ti) for ti in range(len(TT))])
        if do_front:
            w_mod = k.dram_in("f_w_mod", [1, D, 9 * D]); b_mod = k.dram_in("f_b_mod", [1, 9 * D])
            npre = k.dram_in("f_norm_pre", [1, 3, D]); npost = k.dram_in("f_norm_post", [1, 3, D])
            gnrm = k.dram_in("f_gla_norm", [1, 128]); qnrm = k.dram_in("f_q_norm", [3, 128]); kvnrm = k.dram_in("f_kv_norm", [2, 128])
            cc = k.dram_in("cc", [2, D])
            wg_f = k.dram_in("f_wg", [1, 1, D, DFF]); wu_f = k.dram_in("f_wu", [1, 1, D, DFF]); wd_f = k.dram_in("f_wd", [1, 1, DFF, D])
            w_in_f = k.dram_in("f_w_in", [1, D, DIN])
            w_dec_f = k.dram_in("f_w_dec", [2, 16, 256]); b_dec_f = k.dram_in("f_b_dec", [2, 256])
            tri_f = k.dram_in("f_tri", [4, 128, 128])
            rk_cos = k.dram_in("f_rope_cos", [32, NTOK]); rk_sin = k.dram_in("f_rope_sin", [32, NTOK])
            zout = {n: k.dram_out("zo_" + n, s, _dt(d)) for n, s, d in Z_SPECS}
            hmix_out = k.dram_out("hmix_out", [D, NTOK], BF16)
            ex = {n: k.dram_out("ex_" + n, s, _dt(d)) for n, s, d in EXCH_SPECS}
            lay_out = {n: k.dram_out("layo_" + n, s) for n, s in LAY_SPECS}
            vec = k.sb("f_vec", [128, 128], F32); modT = k.sb("f_modT", [128, 72, 2], F32)
            colA = k.sb("f_colA", [128, 3, 8, 2], F32); colG = k.sb("f_colG", [128, 3, 8, 2], F32)
            if x is None:
                x = load_x()
            with k.phase():
                pss = k.ps("pss", [128, 512], F32)
                emit_small_vectors(k, [b_mod[0].rearrange("(r p) -> r p", p=128), npre[0].rearrange("j (r p) -> (j r) p", p=128),
                                       npost[0].rearrange("j (r p) -> (j r) p", p=128), gnrm, qnrm, kvnrm], idf, vec, ("vec", "f"), pss)
                cst = k.sb("cst", [128, 16], F32)
                emit_small_vectors(k, [cc.rearrange("t (r p) -> (t r) p", p=128)], idf, cst, "cvec", pss)
                cs_bf = k.sb("cs_bf", [128, 8, 2], BF16)
                k.act(cs_bf[:, :, :], cst[:, :16].rearrange("p (t r) -> p r t", t=2), AF.Silu, r=["cvec"], w=["cs_bf"])
                wbuf = [k.sb(f"wmodbuf{i}", [128, 8, 512], BF16) for i in range(2)]
                emit_modulation(k, "f", w_mod, vec[:, 0:72], cs_bf, modT, wbuf, pss[:, 0:144].rearrange("p (j t) -> p j t", t=2))
                emit_mod_columns(k, "f", modT, vec, colA, colG)
                k.dma("sp", lay_out["vec"][:, :], vec[:, :], r=[("vec", "f")])
                k.dma("sp", lay_out["modT"].rearrange("p (j t) -> p j t", t=2), modT[:, :, :], r=[("modT", "f")])
                k.dma("sp", lay_out["colA"].rearrange("p (i c t) -> p i c t", i=3, c=8), colA[:, :, :, :], r=[("colA", "f", i) for i in range(3)])
                k.dma("sp", lay_out["colG"].rearrange("p (i c t) -> p i c t", i=3, c=8), colG[:, :, :, :], r=[("colG", "f", i) for i in range(3)])
            with k.phase():
                nb = NormBufs(k); fb = FfnBufs(k)
                emit_half_ffn(k, nb, fb, "f", 0, 0, x, modT, colA, colG, wg_f, wu_f, wd_f)
            with k.phase():
                nb = NormBufs(k); wb = WinBufs(k)

                class PB:
                    pass
                pbuf = PB()
                pbuf.h = k.sb("pn_h", [128, 8, 512], BF16)
                pbuf.pg = [k.ps(f"pn_pg{s}", [128, 512], F32) for s in range(2)]
                pbuf.pu = [k.ps(f"pn_pu{s}", [128, 512], F32) for s in range(2)]
                hmix = k.sb("hmix", [128, 8, NTOK], BF16)
                for ti, (t0, N, is_ctx) in enumerate(TT):
                    emit_prenorm(k, nb, "f", 1, x, pbuf.h, t0, N, 1 if is_ctx else 0, ti, modT, colA)
                    for kc in range(KC):
                        k.op("dve", lambda e, kc=kc, t0=t0, N=N: e.tensor_copy(out=hmix[:, kc, t0:t0 + N], in_=pbuf.h[:, kc, :N]),
                             r=[("h", kc)], w=[("hmix", kc, ti)])
                for kc in range(KC):
                    k.dma("sp", hmix_out[kc * 128:(kc + 1) * 128, :], hmix[:, kc, :], r=[("hmix", kc, ti) for ti in range(len(TT))])
                emit_win(k, wb, pbuf, 0, w_in_f, hmix, zout)
            with k.phase():
                nb = NormBufs(k)
                k.op("dve", lambda e: e.tensor_copy(out=nb.tmp[:, 0:1], in_=vec[:, 0:1]), r=[("vec", "f")], w=["qn_col", "kvn_col"])
                emit_pre_exchange(k, nb, zout, vec[:, 121:124], vec[:, 124:126], rk_cos, rk_sin, ex["cqn"], ex["ckvn"], ex["krr"])
            emit_gla(k, "f", "summ", zout["q"], zout["k"], zout["ktm"], zout["v"], zout["g"], zout["a"], w_dec_f, b_dec_f, vec[:, 120:121], tri_f,
                     summ_D=ex["summ_D"], summ_S=ex["summ_S"])
        for kc in range(KC):
            k.dma("sp", xout[kc * 128:(kc + 1) * 128, :], x[:, kc, :], r=[("x", kc, ti) for ti in range(len(TT))])
        k.flush(final=True)
        return k.nc


def _tri_consts():
    s = np.arange(128)[:, None]; t = np.arange(128)[None, :]; same = (s // 64) == (t // 64)
    return np.stack([same & (s <= t), same & (s > t), same & (s >= t), same & (s < t)]).astype(np.float32)


def _rope_tables(seg):
    pos = seg * SEG + np.arange(SEG)
    inv = 10000.0 ** (-np.arange(0, 16, 2, dtype=np.float32) / 16)
    ar, ac = (pos // 64)[:, None] * inv, (pos % 64)[:, None] * inv
    cos = np.ones((NTOK, 32), np.float32); sin = np.zeros((NTOK, 32), np.float32)
    cos[CTX:] = np.concatenate([np.cos(ar), np.cos(ar), np.cos(ac), np.cos(ac)], 1)
    sin[CTX:] = np.concatenate([-np.sin(ar), np.sin(ar), -np.sin(ac), np.sin(ac)], 1)
    return np.ascontiguousarray(cos.T), np.ascontiguousarray(sin.T)


def _host_gather(outs):
    import ml_dtypes
    res = []
    for core in range(8):
        b, s = divmod(core, 4)
        grp = [outs[4 * b + j] for j in range(4)]
        me = outs[core]
        ckvn_all = np.concatenate([np.asarray(me["ex_ckvn"])[:, :CTX]] + [np.asarray(g["ex_ckvn"])[:, CTX:] for g in grp], 1)
        krr_all = np.concatenate([np.asarray(me["ex_krr"])[:, :CTX]] + [np.asarray(g["ex_krr"])[:, CTX:] for g in grp], 1)
        g_all = np.concatenate([np.asarray(g["zo_four"])[CTX:] for g in grp], 0)
        cD = np.ones((2, 3, GH, 64), np.float32); cS = np.zeros((2, 3, GH, 64, 128), np.float32)
        for i, j in enumerate((0, 1, 2)):
            if j < s:
                cD[0, i] = np.asarray(grp[j]["ex_summ_D"])[0]; cS[0, i] = np.asarray(grp[j]["ex_summ_S"])[0]
        for i, j in enumerate((3, 2, 1)):
            if j > s:
                cD[1, i] = np.asarray(grp[j]["ex_summ_D"])[1]; cS[1, i] = np.asarray(grp[j]["ex_summ_S"])[1]
        d = {"ckvn_all": np.ascontiguousarray(ckvn_all), "krr_all": np.ascontiguousarray(krr_all), "g_all": np.ascontiguousarray(g_all),
             "carry_D": cD, "carry_S": cS, "cqn_in": np.asarray(me["ex_cqn"]), "hmix_in": np.asarray(me["hmix_out"]), "xin": np.asarray(me["xout"])}
        for n, _, _ in Z_SPECS:
            d["zi_" + n] = np.asarray(me["zo_" + n])
        for n, _ in LAY_SPECS:
            d["layi_" + n] = np.asarray(me["layo_" + n])
        res.append(d)
    return res


def kernel_unfused(x, c, ctx, c_ctx, w_mod, b_mod, norm_pre, norm_post, ffn_w_gate, ffn_w_up, ffn_w_down, w_in, gla_w_decay, gla_b_decay,
           gla_norm, mla_q_norm, mla_w_uq, mla_kv_norm, mla_w_ukv, w_branch, w_out):
    import ml_dtypes
    bf = ml_dtypes.bfloat16
    f32 = lambda a: np.ascontiguousarray(np.asarray(a, np.float32))
    x, ctx, c, c_ctx = f32(x), f32(ctx), f32(c), f32(c_ctx)
    ident = np.eye(128, dtype=np.float32); tri = _tri_consts()

    def front_w(l):
        return {"f_w_mod": f32(w_mod[l:l + 1]), "f_b_mod": f32(b_mod[l:l + 1]), "f_norm_pre": f32(norm_pre[l:l + 1]), "f_norm_post": f32(norm_post[l:l + 1]),
                "f_gla_norm": f32(gla_norm[l]).reshape(1, 128), "f_q_norm": f32(mla_q_norm[l]).reshape(3, 128), "f_kv_norm": f32(mla_kv_norm[l]).reshape(2, 128),
                "f_wg": f32(ffn_w_gate[l:l + 1, 0:1]), "f_wu": f32(ffn_w_up[l:l + 1, 0:1]), "f_wd": f32(ffn_w_down[l:l + 1, 0:1]), "f_w_in": f32(w_in[l:l + 1]),
                "f_w_dec": f32(gla_w_decay[l]), "f_b_dec": f32(gla_b_decay[l]), "f_tri": tri}

    def back_w(l):
        return {"b_w_dec": f32(gla_w_decay[l]), "b_b_dec": f32(gla_b_decay[l]), "b_w_uq": f32(mla_w_uq[l]), "b_w_ukv": f32(mla_w_ukv[l]),
                "b_w_in": f32(w_in[l:l + 1]), "b_w_branch": f32(w_branch[l:l + 1]), "b_w_out": f32(w_out[l:l + 1]),
                "b_wg": f32(ffn_w_gate[l:l + 1, 1:2]), "b_wu": f32(ffn_w_up[l:l + 1, 1:2]), "b_wd": f32(ffn_w_down[l:l + 1, 1:2]), "tri": tri}

    per_core = []
    for core in range(8):
        b, s = divmod(core, 4)
        rc, rs = _rope_tables(s)
        ft = {"ft_" + n: v.astype(bf) for n, v in fourier_tables(s).items()}
        per_core.append({"cc": np.stack([c[b], c_ctx], 0), "rope": (rc, rs), "ft": ft})
    ncA = build_launch("A")
    fw = front_w(0)
    maps = []
    for core in range(8):
        b, s = divmod(core, 4)
        xl = np.concatenate([ctx[b], x[b, s * SEG:(s + 1) * SEG]], 0)
        m = {"ident": ident, "xin": np.ascontiguousarray(xl.T), "cc": per_core[core]["cc"], "f_rope_cos": per_core[core]["rope"][0], "f_rope_sin": per_core[core]["rope"][1]}
        m.update(fw)
        maps.append(m)
    outs = run_bass_kernel_spmd(ncA, maps, core_ids=list(range(8))).results
    ncB = build_launch("B")
    g = _host_gather(outs)
    bw, fw = back_w(0), front_w(1)
    maps = []
    for core in range(8):
        m = {"ident": ident, "cc": per_core[core]["cc"], "rope_cos": per_core[core]["rope"][0], "rope_sin": per_core[core]["rope"][1],
             "f_rope_cos": per_core[core]["rope"][0], "f_rope_sin": per_core[core]["rope"][1]}
        m.update(per_core[core]["ft"]); m.update(g[core]); m.update(bw); m.update(fw)
        maps.append(m)
    outs = run_bass_kernel_spmd(ncB, maps, core_ids=list(range(8))).results
    ncC = build_launch("C")
    g = _host_gather(outs)
    bw = back_w(1)
    maps = []
    for core in range(8):
        m = {"ident": ident, "rope_cos": per_core[core]["rope"][0], "rope_sin": per_core[core]["rope"][1]}
        m.update(per_core[core]["ft"]); m.update(g[core]); m.update(bw)
        maps.append(m)
    outs = run_bass_kernel_spmd(ncC, maps, core_ids=list(range(8))).results
    out = np.empty((2, 8192, D), np.float32)
    for core in range(8):
        b, s = divmod(core, 4)
        out[b, s * SEG:(s + 1) * SEG] = np.asarray(outs[core]["xout"]).T[CTX:]
    return out


class KF(K):
    def __init__(self, st):
        self.st = st
        self._stacks = [st]
        self.nc = nc = bass.Bass("TRN2", target_bir_lowering=False)
        E = st.enter_context
        sems = {e: E(nc.semaphore("s_" + e)) for e in Sched.ENGS}
        dsems = {e: [E(nc.semaphore(f"d_{e}{i}")) for i in range(12)] for e in ("sp", "pool", "act")}
        self.cc_sem = E(nc.semaphore("cc_sem"))
        self.cc_n = 0
        self.sems2 = {e: E(nc.semaphore("s2_" + e)) for e in Sched.ENGS}
        self.S = Sched(nc, sems, dsems)
        self.uid = 0

    def dram_in(self, name, shape, dt=F32):
        return self.nc.declare_dram_parameter(name, list(shape), dt, isOutput=False).ap()

    def dram_out(self, name, shape, dt=F32):
        return self.nc.declare_dram_parameter(name, list(shape), dt, isOutput=True).ap()

    def scratch(self, name, shape, dt):
        return self.nc.dram_tensor(self._uname(name), list(shape), dt)

    def allreduce(self, gin, gout):
        self.flush(final=True)
        kk = self
        with self.nc.Block() as blk:
            @blk.gpsimd
            def _(g):
                kk.cc_n += 1
                g.collective_compute("AllReduce", ALU.add, replica_groups=[list(range(8))],
                                     ins=[gin.ap().opt()], outs=[gout.ap().opt()]).then_inc(kk.cc_sem)
                g.wait_ge(kk.cc_sem, kk.cc_n)


def emit_exchange(k, ex, zout, M, G):
    W = SEG
    if not hasattr(k, "_gbufs"):
        k._gbufs = (k.scratch("g1i", [8, 288, W], BF16), k.scratch("g1o", [8, 288, W], BF16),
                    k.scratch("g2i", [8, W, 512], BF16), k.scratch("g2o", [8, W, 512], BF16),
                    k.scratch("g3i", [8, 64, 2 * GH * 129], F32), k.scratch("g3o", [8, 64, 2 * GH * 129], F32))
    g1i, g1o, g2i, g2o, g3i, g3o = k._gbufs
    with k.phase():
        a = k.sb("xc_a", [128, 3, W], BF16)
        f = k.sb("xc_f", [128, 16, 512], BF16)
        s3 = k.sb("xc_s3", [64, 2 * GH, 129], F32)
        sta = k.sb("xc_sta", [128, 3, W], BF16)
        stf = k.sb("xc_stf", [128, 16, 512], BF16)
        st3 = k.sb("xc_st3", [64, 2 * GH, 129], F32)
        k.op("dve", lambda e: e.memset(a[:, 2, :], 0.0), w=[("xa", 2)])
        for c in range(2):
            k.dma("sp", a[:, c, :], ex["ckvn"][c * 128:(c + 1) * 128, CTX:NTOK], w=[("xa", c)])
        k.dma("sp", a[0:32, 2, :], ex["krr"][:, CTX:NTOK], r=[("xa", 2)], w=[("xa", 2)])
        k.dma("sp", f[:, :, :], zout["four"][CTX:NTOK, :].rearrange("(b p) c -> p b c", p=128), w=["xf"])
        k.dma("sp", s3[:, :, 0:1], ex["summ_D"].rearrange("d h (p o) -> p (d h) o", o=1), w=["xs3d"], slow=True)
        k.dma("sp", s3[:, :, 1:129], ex["summ_S"].rearrange("d h p e -> p (d h) e"), w=["xs3s"])
        for j in range(8):
            mj = M["m8"][:, j:j + 1]
            k.op("dve", lambda e, mj=mj: e.tensor_scalar(out=sta[:, :, :], in0=a[:, :, :], scalar1=mj, scalar2=None, op0=ALU.mult),
                 r=[("xa", 0), ("xa", 1), ("xa", 2), "m8"], w=["xsta"])
            for c in range(2):
                k.dma("sp", g1i.ap()[j, c * 128:(c + 1) * 128, :], sta[:, c, :], r=["xsta"])
            k.dma("sp", g1i.ap()[j, 256:288, :], sta[0:32, 2, :], r=["xsta"])
            k.op("dve", lambda e, mj=mj: e.tensor_scalar(out=stf[:, :, :], in0=f[:, :, :], scalar1=mj, scalar2=None, op0=ALU.mult),
                 r=["xf", "m8"], w=["xstf"])
            k.dma("sp", g2i.ap()[j].rearrange("(b p) c -> p b c", p=128), stf[:, :, :], r=["xstf"])
            k.op("dve", lambda e, mj=mj: e.tensor_scalar(out=st3[:, :, :], in0=s3[:, :, :], scalar1=mj[0:64, :], scalar2=None, op0=ALU.mult),
                 r=["xs3d", "xs3s", "m8"], w=["xst3"])
            k.dma("sp", g3i.ap()[j].rearrange("p (q e) -> p q e", e=129), st3[:, :, :], r=["xst3"])
    k.allreduce(g1i, g1o)
    k.allreduce(g2i, g2o)
    k.allreduce(g3i, g3o)
    with k.phase():
        t0_ = k.sb("xs_t0", [128, 16, 512], BF16); t1_ = k.sb("xs_t1", [128, 16, 512], BF16)
        u0 = k.sb("xs_u0", [128, 3, W], BF16); u1 = k.sb("xs_u1", [128, 3, W], BF16)
        k.op("dve", lambda e: e.memset(u0[:, :, :], 0.0), w=["xu0"])
        k.op("dve", lambda e: e.memset(u1[:, :, :], 0.0), w=["xu1"])
        for s in range(4):
            for (buf, slot, key) in ((u0, s, "xu0"), (u1, 4 + s, "xu1")):
                for c in range(2):
                    k.dma("sp", buf[:, c, :], g1o.ap()[slot, c * 128:(c + 1) * 128, :], r=[key], w=[key])
                k.dma("sp", buf[0:32, 2, :], g1o.ap()[slot, 256:288, :], r=[key], w=[key])
            k.op("dve", lambda e: e.tensor_scalar(out=u0[:, :, :], in0=u0[:, :, :], scalar1=M["mb"][:, 0:1], scalar2=None, op0=ALU.mult), r=["xu0", "mb"], w=["xu0"])
            k.op("dve", lambda e: e.scalar_tensor_tensor(out=u0[:, :, :], in0=u1[:, :, :], scalar=M["mb"][:, 1:2], in1=u0[:, :, :], op0=ALU.mult, op1=ALU.add),
                 r=["xu0", "xu1", "mb"], w=["xu0"])
            c0 = CTX + s * W
            for c in range(2):
                k.dma("sp", G["ckvn_all"].ap()[c * 128:(c + 1) * 128, c0:c0 + W], u0[:, c, :], r=["xu0"], w=["xu0"])
            k.dma("sp", G["krr_all"].ap()[:, c0:c0 + W], u0[0:32, 2, :], r=["xu0"], w=["xu0"])
            k.dma("sp", t0_[:, :, :], g2o.ap()[s].rearrange("(b p) c -> p b c", p=128), w=["xt0"])
            k.dma("sp", t1_[:, :, :], g2o.ap()[4 + s].rearrange("(b p) c -> p b c", p=128), w=["xt1"])
            k.op("dve", lambda e: e.tensor_scalar(out=t0_[:, :, :], in0=t0_[:, :, :], scalar1=M["mb"][:, 0:1], scalar2=None, op0=ALU.mult), r=["xt0", "mb"], w=["xt0"])
            k.op("dve", lambda e: e.scalar_tensor_tensor(out=t0_[:, :, :], in0=t1_[:, :, :], scalar=M["mb"][:, 1:2], in1=t0_[:, :, :], op0=ALU.mult, op1=ALU.add),
                 r=["xt0", "xt1", "mb"], w=["xt0"])
            k.dma("sp", G["g_all"].ap()[s * W:(s + 1) * W, :].rearrange("(b p) c -> p b c", p=128), t0_[:, :, :], r=["xt0"], w=["xt0"])
        cx = k.sb("xs_cx", [128, 3, CTX], BF16)
        for c in range(2):
            k.dma("sp", cx[:, c, :], ex["ckvn"][c * 128:(c + 1) * 128, 0:CTX], w=[("xcx", c)])
            k.dma("sp", G["ckvn_all"].ap()[c * 128:(c + 1) * 128, 0:CTX], cx[:, c, :], r=[("xcx", c)])
        k.dma("sp", cx[0:32, 2, :], ex["krr"][:, 0:CTX], w=[("xcx", 2)])
        k.dma("sp", G["krr_all"].ap()[:, 0:CTX], cx[0:32, 2, :], r=[("xcx", 2)])
        g3 = k.sb("xs_g3", [64, 8, 2 * GH * 129], F32)
        k.dma("sp", g3[:, :, :], g3o.ap().rearrange("s p q -> p s q"), w=["xg3"])
        acc = k.sb("xs_acc", [64, GH, 129], F32)
        for d in range(2):
            for i in range(3):
                for slot in range(8):
                    wcol = M["cw"][0:64, (d * 3 + i) * 8 + slot:(d * 3 + i) * 8 + slot + 1]
                    src = g3[:, slot, :].rearrange("p (q e) -> p q e", e=129)[:, d * GH:(d + 1) * GH, :]
                    if slot == 0:
                        k.op("dve", lambda e, wcol=wcol, src=src: e.tensor_scalar(out=acc[:, :, :], in0=src, scalar1=wcol, scalar2=None, op0=ALU.mult),
                             r=["xg3", "cw"], w=["xacc"])
                    else:
                        k.op("dve", lambda e, wcol=wcol, src=src: e.scalar_tensor_tensor(out=acc[:, :, :], in0=src, scalar=wcol, in1=acc[:, :, :], op0=ALU.mult, op1=ALU.add),
                             r=["xg3", "cw", "xacc"], w=["xacc"])
                k.op("dve", lambda e, d=d, i=i: e.tensor_scalar(out=acc[:, :, 0:1], in0=acc[:, :, 0:1], scalar1=M["cid"][0:64, d * 3 + i:d * 3 + i + 1], scalar2=None, op0=ALU.add),
                     r=["xacc", "cid"], w=["xacc"])
                k.dma("sp", G["carry_D"].ap()[d, i].rearrange("h (p o) -> p h o", o=1), acc[:, :, 0:1], r=["xacc"], w=["xacc"], slow=True)
                k.dma("sp", G["carry_S"].ap()[d, i].rearrange("h p e -> p h e"), acc[:, :, 1:129], r=["xacc"], w=["xacc"])


def _exchange_consts(core):
    b, s = divmod(core, 4)
    m8 = np.zeros(8, np.float32); m8[core] = 1.0
    mb = np.zeros(2, np.float32); mb[b] = 1.0
    cw = np.zeros((2, 3, 8), np.float32); cid = np.ones((2, 3), np.float32)
    for i, j in enumerate((0, 1, 2)):
        if j < s:
            cw[0, i, 4 * b + j] = 1.0; cid[0, i] = 0.0
    for i, j in enumerate((3, 2, 1)):
        if j > s:
            cw[1, i, 4 * b + j] = 1.0; cid[1, i] = 0.0
    t = lambda v: np.ascontiguousarray(np.tile(v.reshape(1, -1), (128, 1)), dtype=np.float32)
    return {"m8": t(m8), "mb": t(mb), "cw": t(cw), "cid": t(cid)}


FW_SPECS = [("w_mod", [1, D, 9 * D]), ("b_mod", [1, 9 * D]), ("norm_pre", [1, 3, D]), ("norm_post", [1, 3, D]), ("gla_norm", [1, 128]), ("q_norm", [3, 128]),
            ("kv_norm", [2, 128]), ("wg", [1, 1, D, DFF]), ("wu", [1, 1, D, DFF]), ("wd", [1, 1, DFF, D]), ("w_in", [1, D, DIN]), ("w_dec", [2, 16, 256]), ("b_dec", [2, 256])]
BW_SPECS = [("w_uq", [384, 768]), ("w_ukv", [256, 1024]), ("w_branch", [1, 3, 512, D]), ("w_out", [1, D, D]), ("wg", [1, 1, D, DFF]), ("wu", [1, 1, D, DFF]), ("wd", [1, 1, DFF, D])]


def build_fused(exchange=True):
    import contextlib
    with contextlib.ExitStack() as st:
        k = KF(st)
        ident = k.dram_in("ident", [128, 128]); xin = k.dram_in("xin", [D, NTOK]); xout = k.dram_out("xout", [D, NTOK])
        cc = k.dram_in("cc", [2, D]); tri = k.dram_in("tri", [4, 128, 128])
        r_cos = k.dram_in("rope_cos", [32, NTOK]); r_sin = k.dram_in("rope_sin", [32, NTOK])
        FT = {n: k.dram_in("ft_" + n, s, BF16) for n, s in FT_SHAPES.items()}
        Min = {n: k.dram_in("M_" + n, [128, w]) for n, w in (("m8", 8), ("mb", 2), ("cw", 48), ("cid", 6))}
        FW = [{n: k.dram_in(f"f{l}_{n}", s) for n, s in FW_SPECS} for l in range(2)]
        BW = [{n: k.dram_in(f"b{l}_{n}", s) for n, s in BW_SPECS} for l in range(2)]
        idf = k.sb("idf", [128, 128], F32)
        k.dma("sp", idf[:, :], ident[:, :], w=["ident"])
        M = {n: k.sb("M_" + n, [128, Min[n].shape[1]], F32) for n in Min}
        for n in Min:
            k.dma("sp", M[n][:, :], Min[n][:, :], w=[n])
        x = k.sb("x", [128, 8, NTOK], F32)
        for kc in range(KC):
            for ti, (t0, N, _) in enumerate(TT):
                k.dma("sp", x[:, kc, t0:t0 + N], xin[kc * 128:(kc + 1) * 128, t0:t0 + N], w=[("x", kc, ti)])
        counts = []
        for l in range(2):
            lab = f"L{l}"
            fw, bw = FW[l], BW[l]
            if l == 1:
                k.flush(final=True)
                counts.append(dict(k.S.count))
                k.S.rotate_engine_sems(k.sems2)
            zout = {n: k.scratch(f"z{l}_{n}", s, _dt(d)).ap() for n, s, d in Z_SPECS}
            ex = {n: k.scratch(f"ex{l}_{n}", s, _dt(d)).ap() for n, s, d in EXCH_SPECS}
            hmix_d = k.scratch(f"hmix{l}", [D, NTOK], BF16).ap()
            y_d = [k.scratch(f"ybr{l}_{i}", [512, NTOK], BF16).ap() for i in range(3)]
            G = {"ckvn_all": k.scratch(f"ckvn_all{l}", [256, NKEY], BF16), "krr_all": k.scratch(f"krr_all{l}", [32, NKEY], BF16),
                 "g_all": k.scratch(f"g_all{l}", [8192, 512], BF16), "carry_D": k.scratch(f"carry_D{l}", [2, 3, GH, 64], F32),
                 "carry_S": k.scratch(f"carry_S{l}", [2, 3, GH, 64, 128], F32)}
            vec = k.sb("vec", [128, 128], F32); modT = k.sb("modT", [128, 72, 2], F32)
            colA = k.sb("colA", [128, 3, 8, 2], F32); colG = k.sb("colG", [128, 3, 8, 2], F32)
            with k.phase():
                pss = k.ps("pss", [128, 512], F32)
                emit_small_vectors(k, [fw["b_mod"][0].rearrange("(r p) -> r p", p=128), fw["norm_pre"][0].rearrange("j (r p) -> (j r) p", p=128),
                                       fw["norm_post"][0].rearrange("j (r p) -> (j r) p", p=128), fw["gla_norm"], fw["q_norm"], fw["kv_norm"]], idf, vec, ("vec", lab), pss)
                cst = k.sb("cst", [128, 16], F32)
                emit_small_vectors(k, [cc.rearrange("t (r p) -> (t r) p", p=128)], idf, cst, ("cvec", lab), pss)
                cs_bf = k.sb("cs_bf", [128, 8, 2], BF16)
                k.act(cs_bf[:, :, :], cst[:, :16].rearrange("p (t r) -> p r t", t=2), AF.Silu, r=[("cvec", lab)], w=["cs_bf"])
                wbuf = [k.sb(f"wmodbuf{i}", [128, 8, 512], BF16) for i in range(2)]
                emit_modulation(k, lab, fw["w_mod"], vec[:, 0:72], cs_bf, modT, wbuf, pss[:, 0:144].rearrange("p (j t) -> p j t", t=2))
                emit_mod_columns(k, lab, modT, vec, colA, colG)
            with k.phase():
                nb = NormBufs(k); fb = FfnBufs(k)
                emit_half_ffn(k, nb, fb, lab, 0, 0, x, modT, colA, colG, fw["wg"], fw["wu"], fw["wd"])
            with k.phase():
                nb = NormBufs(k); wb = WinBufs(k)

                class PB:
                    pass
                pbuf = PB()
                pbuf.h = k.sb("pn_h", [128, 8, 512], BF16)
                pbuf.pg = [k.ps(f"pn_pg{s}", [128, 512], F32) for s in range(2)]
                pbuf.pu = [k.ps(f"pn_pu{s}", [128, 512], F32) for s in range(2)]
                hmix = k.sb("hmix", [128, 8, NTOK], BF16)
                for ti, (t0, N, is_ctx) in enumerate(TT):
                    emit_prenorm(k, nb, lab, 1, x, pbuf.h, t0, N, 1 if is_ctx else 0, ti, modT, colA)
                    for kc in range(KC):
                        k.op("dve", lambda e, kc=kc, t0=t0, N=N: e.tensor_copy(out=hmix[:, kc, t0:t0 + N], in_=pbuf.h[:, kc, :N]),
                             r=[("h", kc)], w=[("hmix", kc, ti)])
                for kc in range(KC):
                    k.dma("sp", hmix_d[kc * 128:(kc + 1) * 128, :], hmix[:, kc, :], r=[("hmix", kc, ti) for ti in range(len(TT))])
                emit_win(k, wb, pbuf, 0, fw["w_in"], hmix, zout)
            with k.phase():
                nb = NormBufs(k)
                emit_pre_exchange(k, nb, zout, vec[:, 121:124], vec[:, 124:126], r_cos, r_sin, ex["cqn"], ex["ckvn"], ex["krr"])
            emit_gla(k, lab, "summ", zout["q"], zout["k"], zout["ktm"], zout["v"], zout["g"], zout["a"], fw["w_dec"], fw["b_dec"], vec[:, 120:121], tri,
                     summ_D=ex["summ_D"], summ_S=ex["summ_S"])
            if exchange:
                emit_exchange(k, ex, zout, M, G)
            with_ctx = (l == 0)
            emit_gla(k, lab, "full", zout["q"], zout["k"], zout["ktm"], zout["v"], zout["g"], zout["a"], fw["w_dec"], fw["b_dec"], vec[:, 120:121], tri,
                     carry_D=G["carry_D"].ap(), carry_S=G["carry_S"].ap(), ya_out=y_d[0])
            emit_fourier(k, G["g_all"].ap(), zout["four"][0:CTX, :] if with_ctx else None, FT, y_d[1])
            emit_mla_attention(k, lab, G["ckvn_all"].ap(), G["krr_all"].ap(), ex["cqn"], r_cos, r_sin, bw["w_uq"], bw["w_ukv"], y_d[2], ctx_queries=with_ctx)
            with k.phase():
                nb = NormBufs(k)
                emit_merge(k, nb, lab, x, hmix_d, y_d, fw["w_in"], bw["w_branch"], bw["w_out"], colG, with_ctx)
            with k.phase():
                nb = NormBufs(k); fb = FfnBufs(k)
                emit_half_ffn(k, nb, fb, lab, 2, 0, x, modT, colA, colG, bw["wg"], bw["wu"], bw["wd"], skip_ctx=not with_ctx)
        for kc in range(KC):
            k.dma("sp", xout[kc * 128:(kc + 1) * 128, :], x[:, kc, :], r=[("x", kc, ti) for ti in range(len(TT))])
        k.flush(final=True)
        counts.append(dict(k.S.count))
        build_fused.last_counts = counts
        return k.nc


def kernel_fused(x, c, ctx, c_ctx, w_mod, b_mod, norm_pre, norm_post, ffn_w_gate, ffn_w_up, ffn_w_down, w_in, gla_w_decay, gla_b_decay,
           gla_norm, mla_q_norm, mla_w_uq, mla_kv_norm, mla_w_ukv, w_branch, w_out):
    import ml_dtypes
    bf = ml_dtypes.bfloat16
    f32 = lambda a: np.ascontiguousarray(np.asarray(a, np.float32))
    x, ctx, c, c_ctx = f32(x), f32(ctx), f32(c), f32(c_ctx)
    shared = {"ident": np.eye(128, dtype=np.float32), "tri": _tri_consts()}
    for l in range(2):
        shared.update({f"f{l}_w_mod": f32(w_mod[l:l + 1]), f"f{l}_b_mod": f32(b_mod[l:l + 1]), f"f{l}_norm_pre": f32(norm_pre[l:l + 1]),
                       f"f{l}_norm_post": f32(norm_post[l:l + 1]), f"f{l}_gla_norm": f32(gla_norm[l]).reshape(1, 128), f"f{l}_q_norm": f32(mla_q_norm[l]).reshape(3, 128),
                       f"f{l}_kv_norm": f32(mla_kv_norm[l]).reshape(2, 128), f"f{l}_wg": f32(ffn_w_gate[l:l + 1, 0:1]), f"f{l}_wu": f32(ffn_w_up[l:l + 1, 0:1]),
                       f"f{l}_wd": f32(ffn_w_down[l:l + 1, 0:1]), f"f{l}_w_in": f32(w_in[l:l + 1]), f"f{l}_w_dec": f32(gla_w_decay[l]), f"f{l}_b_dec": f32(gla_b_decay[l]),
                       f"b{l}_w_uq": f32(mla_w_uq[l]), f"b{l}_w_ukv": f32(mla_w_ukv[l]), f"b{l}_w_branch": f32(w_branch[l:l + 1]), f"b{l}_w_out": f32(w_out[l:l + 1]),
                       f"b{l}_wg": f32(ffn_w_gate[l:l + 1, 1:2]), f"b{l}_wu": f32(ffn_w_up[l:l + 1, 1:2]), f"b{l}_wd": f32(ffn_w_down[l:l + 1, 1:2])})
    nc = build_fused()
    maps = []
    for core in range(8):
        b, s = divmod(core, 4)
        rc, rs = _rope_tables(s)
        xl = np.concatenate([ctx[b], x[b, s * SEG:(s + 1) * SEG]], 0)
        m = {"xin": np.ascontiguousarray(xl.T), "cc": np.stack([c[b], c_ctx], 0), "rope_cos": rc, "rope_sin": rs}
        m.update({"ft_" + n: v.astype(bf) for n, v in fourier_tables(s).items()})
        m.update({"M_" + n: v for n, v in _exchange_consts(core).items()})
        m.update(shared)
        maps.append(m)
    outs = run_bass_kernel_spmd(nc, maps, core_ids=list(range(8))).results
    out = np.empty((2, 8192, D), np.float32)
    for core in range(8):
        b, s = divmod(core, 4)
        out[b, s * SEG:(s + 1) * SEG] = np.asarray(outs[core]["xout"]).T[CTX:]
    return out


USE_FUSED = True


def kernel(**inputs):
    return (kernel_fused if USE_FUSED else kernel_unfused)(**inputs)
```

```python
SCORE_MAX_MEASURED = {"layer0": 6.08, "layer1": 5.99}

import numpy as np
import concourse.bass as bass
import concourse.mybir as mybir
from concourse.bass_utils import run_bass_kernel_spmd

F32 = mybir.dt.float32
BF16 = mybir.dt.bfloat16
AF = mybir.ActivationFunctionType
ALU = mybir.AluOpType
AX = mybir.AxisListType


class Op:
    __slots__ = ("eng", "fn", "deps", "signal", "val", "dma", "dsem", "dval", "name", "gen", "reuse")


class Sched:
    ENGS = ("pe", "act", "dve", "pool", "sp")

    def __init__(self, nc, sems, dma_sems, same_engine_sync=True):
        self.nc = nc
        self.sem = sems
        self.dma_sems = dma_sems
        self.dma_rr = {e: 0 for e in self.ENGS}
        self.dma_val = {}
        self.count = {e: 0 for e in self.ENGS}
        self.waited = {e: {} for e in self.ENGS}
        self.last_w = {}
        self.readers = {}
        self.ops = {e: [] for e in self.ENGS}
        self.same = same_engine_sync
        self.all_dma_sems = {}
        self.gen = 0

    def add(self, eng, fn, reads=(), writes=(), dma=False, name=None):
        op = Op()
        op.eng, op.fn, op.dma, op.signal, op.name = eng, fn, dma, False, name
        op.val = op.dsem = op.dval = None
        op.gen = None
        deps = set()
        for k in reads:
            w = self.last_w.get(k)
            if w is not None:
                deps.add(w)
        for k in writes:
            w = self.last_w.get(k)
            if w is not None:
                deps.add(w)
            for r in self.readers.get(k, ()):
                deps.add(r)
        for k in reads:
            self.readers.setdefault(k, []).append(op)
        for k in writes:
            self.last_w[k] = op
            self.readers[k] = []
        deps.discard(op)
        op.deps = deps
        self.ops[eng].append(op)
        return op

    def rotate_engine_sems(self, new_sems):
        assert all(not self.ops[e] for e in self.ENGS), "rotate only at a phase boundary"
        self.sem = new_sems
        self.count = {e: 0 for e in self.ENGS}

    def _needs_wait(self, op, d):
        if d.dma:
            return True
        if d.eng == op.eng and not op.dma:
            if op.eng == "pe":
                return False
            return self.same
        return True

    def flush(self, block, final=False):
        self.gen = getattr(self, "gen", 0) + 1
        gen = self.gen
        for e in self.ENGS:
            for op in self.ops[e]:
                op.gen = gen
        for e in self.ENGS:
            for op in self.ops[e]:
                for d in op.deps:
                    if d.gen == gen and not d.dma and self._needs_wait(op, d):
                        d.signal = True
        for e in self.ENGS:
            c = self.count[e]
            for op in self.ops[e]:
                if op.dma:
                    lst = self.dma_sems[e]
                    s = lst[self.dma_rr[e] % len(lst)]
                    self.dma_rr[e] += 1
                    prev = self.dma_val.get(id(s), 0)
                    op.reuse = (s, prev)
                    op.dsem, op.dval = s, prev + 16
                    self.dma_val[id(s)] = prev + 16
                    self.all_dma_sems[id(s)] = s
                elif op.signal:
                    c += 1
                    op.val = c
            self.count[e] = c
        pending = self.ops
        self.ops = {e: [] for e in self.ENGS}
        handles = {"pe": block.tensor, "act": block.scalar, "dve": block.vector,
                   "pool": block.gpsimd, "sp": block.sync}
        for e in self.ENGS:
            if not pending[e] and not (final and e == "sp"):
                continue
            self._emit_engine(handles[e], e, pending[e], gen, final and e == "sp")

    def _emit_engine(self, deco, e, ops, gen, final):
        sched = self

        def body(eng):
            waited = sched.waited[e]
            for op in ops:
                need = {}
                for d in op.deps:
                    if d.dma:
                        s, v = d.dsem, d.dval
                    else:
                        if d.gen != gen or not sched._needs_wait(op, d):
                            continue
                        s, v = sched.sem[d.eng], d.val
                        assert v is not None and v > 0, (op.name, d.name, d.eng, v)
                    key = id(s)
                    if waited.get(key, 0) >= v:
                        continue
                    if key not in need or need[key][1] < v:
                        need[key] = (s, v)
                if op.dma:
                    s, prev = op.reuse
                    if prev > 0 and waited.get(id(s), 0) < prev:
                        if id(s) not in need or need[id(s)][1] < prev:
                            need[id(s)] = (s, prev)
                for key, (s, v) in need.items():
                    eng.wait_ge(s, v)
                    waited[key] = v
                inst = op.fn(eng)
                if op.dma:
                    inst.then_inc(op.dsem, 16)
                elif op.signal:
                    inst.then_inc(sched.sem[e], 1)
            if final:
                for key, s in sched.all_dma_sems.items():
                    v = sched.dma_val[key]
                    if waited.get(key, 0) < v:
                        eng.wait_ge(s, v)
                        waited[key] = v

        deco(body)


D = 1024
KC = 8
DFF = 2816
FC = 22
NMOD = 9
DIN = 5824
CTX = 256
SEG = 2048
NTOK = CTX + SEG
EPS = 1e-6
TT = [(0, 256, True)] + [(CTX + 512 * i, 512, False) for i in range(4)]


class K:
    def __init__(self, st):
        self.st = st
        self._stacks = [st]
        self.nc = nc = bass.Bass("TRN2", target_bir_lowering=False)
        E = st.enter_context
        sems = {e: E(nc.semaphore("s_" + e)) for e in Sched.ENGS}
        dsems = {e: [E(nc.semaphore(f"d_{e}{i}")) for i in range(12)] for e in ("sp", "pool", "act")}
        self.S = Sched(nc, sems, dsems)
        self.allsems = list(sems.values()) + sum(dsems.values(), [])
        self.uid = 0
        with nc.Block() as blk:
            @blk.sync
            def _(sp):
                for s in self.allsems:
                    sp.sem_clear(s)

    def _uname(self, name):
        self.uid += 1
        return f"{name}_{self.uid}"

    def sb(self, name, shape, dt):
        return self._stacks[-1].enter_context(self.nc.sbuf_tensor(self._uname(name), shape, dt))

    def ps(self, name, shape, dt=F32):
        return self._stacks[-1].enter_context(self.nc.psum_tensor(self._uname(name), shape, dt))

    def phase(self):
        import contextlib
        kk = self

        @contextlib.contextmanager
        def cm():
            sub = contextlib.ExitStack()
            kk._stacks.append(sub)
            try:
                yield
                kk.flush(final=True)
            finally:
                kk._stacks.pop()
                sub.close()
        return cm()

    def dram_in(self, name, shape, dt=F32):
        return self.nc.dram_tensor(name, list(shape), dt, kind="ExternalInput").ap()

    def dram_out(self, name, shape, dt=F32):
        return self.nc.dram_tensor(name, list(shape), dt, kind="ExternalOutput").ap()

    def flush(self, final=False):
        with self.nc.Block() as blk:
            self.S.flush(blk, final=final)

    def dma(self, q, out, in_, r=(), w=(), slow=False):
        if slow:
            return self.S.add(q, lambda e: e.dma_start(out=out, in_=in_, allow_slow_non_contiguous=True), reads=r, writes=w, dma=True)
        return self.S.add(q, lambda e: e.dma_start(out=out, in_=in_), reads=r, writes=w, dma=True)

    def mm(self, out, lhsT, rhs, start, stop, r=(), w=()):
        return self.S.add("pe", lambda e: e.matmul(out, lhsT=lhsT, rhs=rhs, start=start, stop=stop), reads=r, writes=w)

    def tr(self, out, in_, ident, r=(), w=()):
        return self.S.add("pe", lambda e: e.transpose(out, in_, ident), reads=r, writes=w)

    def act(self, out, in_, func, r=(), w=(), **kw):
        return self.S.add("act", lambda e: e.activation(out=out, in_=in_, func=func, **kw), reads=r, writes=w)

    def op(self, eng, fn, r=(), w=()):
        return self.S.add(eng, fn, reads=r, writes=w)


def emit_small_vectors(k, vecs, ident_f32, out_cols, key, pss):
    R = sum(v.shape[0] for v in vecs)
    assert R <= 128
    stg = k.sb(f"stg_{key}", [128, 128], F32)
    pst = pss
    r0 = 0
    for i, v in enumerate(vecs):
        n = v.shape[0]
        k.dma("sp", stg[r0:r0 + n, :], v, w=[(key, "stg", i)])
        r0 += n
    k.tr(pst[:, :R], stg[:R, :], ident_f32[:R, :R], r=[(key, "stg", i) for i in range(len(vecs))] + ["ident"], w=["pss"])
    k.op("dve", lambda e: e.memset(out_cols[:, :], 0.0), w=[key])
    k.op("dve", lambda e: e.tensor_copy(out=out_cols[:, :R], in_=pst[:, :R]), r=["pss", key], w=[key])


def emit_modulation(k, l, w_mod, b_cols, cs_bf, modT, wbuf, psm):
    NB = 18
    for nb in range(NB):
        slot = nb % 2
        wt = wbuf[slot]
        k.dma("pool", wt[:, :, :], w_mod[0, :, nb * 512:(nb + 1) * 512].rearrange("(kc p) n -> p kc n", p=128),
              w=[("wmod", slot)])
        for oc in range(4):
            jk = nb * 4 + oc
            for kc in range(KC):
                k.mm(psm[:, jk, :], wt[:, kc, oc * 128:(oc + 1) * 128], cs_bf[:, kc, :], kc == 0, kc == KC - 1,
                     r=[("wmod", slot), "cs_bf"], w=["pss"])
    k.op("dve", lambda e: e.tensor_tensor(out=modT[:, :, :], in0=psm[:, :, :],
                                          in1=b_cols.unsqueeze(2).to_broadcast([128, 72, 2]), op=ALU.add),
         r=["pss", ("vec", l)], w=[("modT", l)])


def emit_mod_columns(k, l, modT, vec, colA, colG):
    for i in range(3):
        half = 1.0 if i == 1 else 0.5
        sc = modT[:, (3 * i + 1) * 8:(3 * i + 2) * 8, :]
        gt = modT[:, (3 * i + 2) * 8:(3 * i + 3) * 8, :]
        npre = vec[:, 72 + 8 * i:72 + 8 * i + 8].unsqueeze(2).to_broadcast([128, 8, 2])
        npost = vec[:, 96 + 8 * i:96 + 8 * i + 8].unsqueeze(2).to_broadcast([128, 8, 2])
        k.op("dve", lambda e, sc=sc, npre=npre, i=i: e.scalar_tensor_tensor(out=colA[:, i, :, :], in0=sc, scalar=1.0, in1=npre,
                                                                            op0=ALU.add, op1=ALU.mult),
             r=[("modT", l), ("vec", l)], w=[("colA", l, i)])
        k.op("dve", lambda e, gt=gt, npost=npost, i=i, half=half: e.scalar_tensor_tensor(out=colG[:, i, :, :], in0=gt, scalar=half,
                                                                                         in1=npost, op0=ALU.mult, op1=ALU.mult),
             r=[("modT", l), ("vec", l)], w=[("colG", l, i)])


class NormBufs:
    def __init__(self, k):
        self.sq = k.sb("nb_sq", [128, 8, 512], BF16)
        self.ms = k.ps("nb_ms", [128, 512], F32)
        self.rstd = k.sb("nb_rstd", [128, 512], F32)
        self.tmp = k.sb("nb_tmp", [128, 512], F32)
        self.ones = k.sb("nb_ones", [128, 128], BF16)
        k.op("dve", lambda e: e.memset(self.ones[:, :], 1.0 / D), w=["ones"])


def emit_rstd(k, nb, src_chunks, N, rkeys, nch=KC, ones=None, ones_key="ones"):
    ones = nb.ones if ones is None else ones
    for kc in range(nch):
        k.act(nb.sq[:, kc, :N], src_chunks(kc), AF.Square, r=list(rkeys(kc)), w=[("sq", kc)])
    for kc in range(nch):
        k.mm(nb.ms[:, :N], ones[:, :], nb.sq[:, kc, :N], kc == 0, kc == nch - 1, r=[ones_key, ("sq", kc)], w=["ms"])
    k.act(nb.rstd[:, :N], nb.ms[:, :N], AF.Ln, bias=EPS, r=["ms"], w=["rstd"])
    k.act(nb.rstd[:, :N], nb.rstd[:, :N], AF.Exp, scale=-0.5, r=["rstd"], w=["rstd"])


def emit_prenorm(k, nb, l, i, x, h, t0, N, tix, ti, modT, colA):
    emit_rstd(k, nb, lambda kc: x[:, kc, t0:t0 + N], N, lambda kc: [("x", kc, ti)])
    for kc in range(KC):
        k.op("dve", lambda e, kc=kc: e.scalar_tensor_tensor(out=nb.tmp[:, :N], in0=x[:, kc, t0:t0 + N],
                                                            scalar=colA[:, i, kc, tix:tix + 1], in1=nb.rstd[:, :N],
                                                            op0=ALU.mult, op1=ALU.mult),
             r=[("x", kc, ti), ("colA", l, i), "rstd"], w=["tmp"])
        k.act(h[:, kc, :N], nb.tmp[:, :N], AF.Identity, bias=modT[:, (3 * i) * 8 + kc, tix:tix + 1],
              r=["tmp", ("modT", l)], w=[("h", kc)])


def emit_post_residual(k, nb, l, i, x, y, t0, N, tix, ti, colG):
    emit_rstd(k, nb, lambda kc: y[:, kc, :N], N, lambda kc: [("y", kc)])
    for kc in range(KC):
        k.op("dve", lambda e, kc=kc: e.scalar_tensor_tensor(out=nb.tmp[:, :N], in0=y[:, kc, :N],
                                                            scalar=colG[:, i, kc, tix:tix + 1], in1=nb.rstd[:, :N],
                                                            op0=ALU.mult, op1=ALU.mult),
             r=[("y", kc), ("colG", l, i), "rstd"], w=["tmp"])
        k.op("dve", lambda e, kc=kc: e.tensor_tensor(out=x[:, kc, t0:t0 + N], in0=x[:, kc, t0:t0 + N], in1=nb.tmp[:, :N], op=ALU.add),
             r=["tmp", ("x", kc, ti)], w=[("x", kc, ti)])


class FfnBufs:
    def __init__(self, k):
        self.wg = [k.sb(f"ff_wg{s}", [128, 8, 256], BF16) for s in range(2)]
        self.wu = [k.sb(f"ff_wu{s}", [128, 8, 256], BF16) for s in range(2)]
        self.wd = [k.sb(f"ff_wd{s}", [128, FC, 128], BF16) for s in range(2)]
        self.pg = [k.ps(f"ff_pg{s}", [128, 512], F32) for s in range(2)]
        self.pu = [k.ps(f"ff_pu{s}", [128, 512], F32) for s in range(2)]
        self.py = [k.ps(f"ff_py{s}", [128, 512], F32) for s in range(2)]
        self.sg = k.sb("ff_sg", [128, 512], F32)
        self.a = k.sb("ff_a", [128, FC, 512], BF16)
        self.h = k.sb("ff_h", [128, 8, 512], BF16)
        self.y = k.sb("ff_y", [128, 8, 512], F32)
        self.cnt = 0


def emit_half_ffn(k, nb, fb, l, i, fi, x, modT, colA, colG, wgate, wup, wdown, skip_ctx=False):
    for ti, (t0, N, is_ctx) in enumerate(TT):
        if is_ctx and skip_ctx:
            continue
        tix = 1 if is_ctx else 0
        emit_prenorm(k, nb, l, i, x, fb.h, t0, N, tix, ti, modT, colA)
        for g in range(FC // 2):
            s = fb.cnt % 2
            fb.cnt += 1
            c0 = g * 256
            k.dma("pool", fb.wg[s][:, :, :], wgate[0, 0, :, c0:c0 + 256].rearrange("(kc p) n -> p kc n", p=128), w=[("wg", s)])
            k.dma("pool", fb.wu[s][:, :, :], wup[0, 0, :, c0:c0 + 256].rearrange("(kc p) n -> p kc n", p=128), w=[("wu", s)])
            for j in range(2):
                f = 2 * g + j
                ps = f % 2
                for kc in range(KC):
                    k.mm(fb.pg[ps][:, :N], fb.wg[s][:, kc, j * 128:(j + 1) * 128], fb.h[:, kc, :N], kc == 0, kc == KC - 1,
                         r=[("wg", s), ("h", kc)], w=[("pg", ps)])
                for kc in range(KC):
                    k.mm(fb.pu[ps][:, :N], fb.wu[s][:, kc, j * 128:(j + 1) * 128], fb.h[:, kc, :N], kc == 0, kc == KC - 1,
                         r=[("wu", s), ("h", kc)], w=[("pu", ps)])
                k.act(fb.sg[:, :N], fb.pg[ps][:, :N], AF.Silu, r=[("pg", ps)], w=["sg"])
                k.op("dve", lambda e, f=f, ps=ps: e.tensor_tensor(out=fb.a[:, f, :N], in0=fb.sg[:, :N], in1=fb.pu[ps][:, :N], op=ALU.mult),
                     r=["sg", ("pu", ps)], w=[("a", f)])
        for oc in range(KC):
            s = oc % 2
            k.dma("pool", fb.wd[s][:, :, :], wdown[0, 0, :, oc * 128:(oc + 1) * 128].rearrange("(f p) n -> p f n", p=128), w=[("wd", s)])
            for f in range(FC):
                k.mm(fb.py[s][:, :N], fb.wd[s][:, f, :], fb.a[:, f, :N], f == 0, f == FC - 1, r=[("wd", s), ("a", f)], w=[("py", s)])
            k.act(fb.y[:, oc, :N], fb.py[s][:, :N], AF.Copy, r=[("py", s)], w=[("y", oc)])
        emit_post_residual(k, nb, l, i, x, fb.y, t0, N, tix, ti, colG)


OFF = {"q": 0, "k": 256, "v": 512, "g": 1024, "a": 1536, "four": 1568, "cq": 2080, "ckv": 2464, "kr": 2720, "gates": 2752}
Z_FM = [("q", 0, 256), ("k", 256, 256), ("g", 1024, 512), ("a", 1536, 32), ("cq", 2080, 384), ("ckv", 2464, 256), ("kr", 2720, 32)]
Z_TM = [("v", 512, 512), ("four", 1568, 512), ("ktm", 256, 256)]


class WinBufs:
    def __init__(self, k):
        self.w = [k.sb(f"wi_w{s}", [128, 8, 512], BF16) for s in range(2)]
        self.wkrp = k.sb("wi_wkrp", [128, 8, 32], BF16)
        self.o = [k.sb(f"wi_o{s}", [128, 512], F32) for s in range(2)]
        self.ob = [k.sb(f"wi_ob{s}", [128, 512], BF16) for s in range(2)]
        self.cnt = 0
        self.ocnt = 0


def emit_win(k, wb, fb, l, w_in, hmix, zout):
    for ti, (t0, N, is_ctx) in enumerate(TT):
        def load_w(c0, ncols):
            s = wb.cnt % 2
            wb.cnt += 1
            k.dma("pool", wb.w[s][:, :, :ncols], w_in[0, :, c0:c0 + ncols].rearrange("(kc p) n -> p kc n", p=128), w=[("wiw", s)])
            return s

        def store(ps_ap, key_ps, dst, npart, ncol, as_bf16=True, dkey=None):
            s = wb.ocnt % 2
            wb.ocnt += 1
            buf = wb.ob[s] if as_bf16 else wb.o[s]
            k.act(buf[:npart, :ncol], ps_ap, AF.Copy, r=[key_ps], w=[("wio", as_bf16, s)])
            k.dma("sp", dst, buf[:npart, :ncol], r=[("wio", as_bf16, s)], w=[dkey] if dkey is not None else [])

        for name, c0, nf in Z_FM:
            s = load_w(c0, nf)
            if name == "kr":
                for (d0, s0) in ((0, 8), (8, 0), (16, 24), (24, 16)):
                    k.op("dve", lambda e, s=s, d0=d0, s0=s0: e.tensor_copy(out=wb.wkrp[:, :, d0:d0 + 8], in_=wb.w[s][:, :, s0:s0 + 8]),
                         r=[("wiw", s)], w=["wkrp"])
            for m0 in range(0, nf, 128):
                M = min(128, nf - m0)
                pb = fb.pg[(m0 // 128) % 2]
                pkey = ("pg", (m0 // 128) % 2)
                for kc in range(KC):
                    k.mm(pb[:M, :N], wb.w[s][:, kc, m0:m0 + M], hmix[:, kc, t0:t0 + N], kc == 0, kc == KC - 1,
                         r=[("wiw", s), ("hmix", kc, ti)], w=[pkey])
                f32 = name in ("a", "cq", "ckv", "kr")
                store(pb[:M, :N], pkey, zout[name][m0:m0 + M, t0:t0 + N], M, N, as_bf16=not f32, dkey=("z", name, m0 // 128, ti))
            if name == "kr":
                pb, pkey = fb.pu[0], ("pu", 0)
                for kc in range(KC):
                    k.mm(pb[:32, :N], wb.wkrp[:, kc, :], hmix[:, kc, t0:t0 + N], kc == 0, kc == KC - 1,
                         r=["wkrp", ("hmix", kc, ti)], w=[pkey])
                store(pb[:32, :N], pkey, zout["krp"][:, t0:t0 + N], 32, N, as_bf16=False, dkey=("z", "krp", 0, ti))
        for name, c0, ncol in Z_TM:
            s = load_w(c0, ncol)
            for b0 in range(0, N, 128):
                pb = fb.pu[(b0 // 128) % 2]
                pkey = ("pu", (b0 // 128) % 2)
                for kc in range(KC):
                    k.mm(pb[:, :ncol], hmix[:, kc, t0 + b0:t0 + b0 + 128], wb.w[s][:, kc, :ncol], kc == 0, kc == KC - 1,
                         r=[("wiw", s), ("hmix", kc, ti)], w=[pkey])
                store(pb[:, :ncol], pkey, zout[name][t0 + b0:t0 + b0 + 128, :], 128, ncol, as_bf16=True)


NKEY = CTX + 8192
NKB = NKEY // 128
MH = 8
ATT_SCALE = 96.0 ** -0.5


def emit_mla_attention(k, l, ckvn, krr, cqn, ropeq_cos, ropeq_sin, w_uq, w_ukv, yc_out, ctx_queries):
    with k.phase():
        kv = k.sb("at_ckvn", [128, 2, NKEY], BF16)
        khT = k.sb("at_khT", [96, NKEY], BF16)
        vh = k.sb("at_vh", [128, NKB, 65], BF16)
        wkv = k.sb("at_wkv", [128, 2, 1024], BF16)
        wq = k.sb("at_wq", [128, 3, 768], BF16)
        wqp = k.sb("at_wqp", [128, 3, 32], BF16)
        cq = k.sb("at_cqn", [128, 3, NTOK], BF16)
        cos = k.sb("at_cos", [32, NTOK], F32)
        sin = k.sb("at_sin", [32, NTOK], F32)
        qhT = k.sb("at_qhT", [96, 512], BF16)
        qr = k.sb("at_qr", [32, 512], F32)
        qrp = k.sb("at_qrp", [32, 512], F32)
        qrb = k.sb("at_qrb", [32, 512], BF16)
        pT = [k.sb(f"at_pT{s}", [128, 512], BF16) for s in range(3)]
        osb = k.sb("at_osb", [65, 512], F32)
        sel = k.sb("at_sel", [65, 64], F32)
        rden = k.sb("at_rden", [64, 512], F32)
        ysb = k.sb("at_ysb", [64, 512], BF16)
        ps_s = [k.ps(f"at_ps_s{s}", [128, 512], F32) for s in range(3)]
        ps_o = k.ps("at_ps_o", [128, 512], F32)
        ps_p = [k.ps(f"at_ps_p{s}", [128, 512], F32) for s in range(2)]
        ps_d = k.ps("at_ps_d", [128, 512], F32)

        for c in range(2):
            k.dma("sp", kv[:, c, :], ckvn[c * 128:(c + 1) * 128, :], w=[("kv", c)])
        k.dma("sp", khT[64:96, :], krr[:, :], w=["kh_r"])
        k.dma("pool", wkv[:, :, :], w_ukv.rearrange("(c p) n -> p c n", p=128), w=["wkv"])
        k.dma("pool", wq[:, :, :], w_uq.rearrange("(c p) n -> p c n", p=128), w=["wq"])
        for c in range(3):
            k.dma("sp", cq[:, c, :], cqn[c * 128:(c + 1) * 128, :], w=[("cq", c)])
        k.dma("sp", cos[:, :], ropeq_cos[:, :], w=["cos"])
        k.dma("sp", sin[:, :], ropeq_sin[:, :], w=["sin"])
        k.op("dve", lambda e: e.memset(vh[:, :, 64:65], 1.0), w=["vh_one"])
        k.op("dve", lambda e: e.memset(sel[:, :], 0.0), w=["sel"])
        k.op("dve", lambda e: e.memset(sel[64:65, :], 1.0), r=["sel"], w=["sel"])

        qtiles = [(ti, t0, N, is_ctx) for ti, (t0, N, is_ctx) in enumerate(TT) if (ctx_queries or not is_ctx)]
        for h in range(MH):
            for kb4 in range(0, NKEY, 512):
                pb, pk = ps_p[(kb4 // 512) % 2], ("ps_p", (kb4 // 512) % 2)
                wd_ = min(512, NKEY - kb4)
                for c in range(2):
                    k.mm(pb[:64, :wd_], wkv[:, c, h * 128:h * 128 + 64], kv[:, c, kb4:kb4 + wd_], c == 0, c == 1,
                         r=["wkv", ("kv", c)], w=[pk])
                k.act(khT[0:64, kb4:kb4 + wd_], pb[:64, :wd_], AF.Copy, r=[pk], w=[("kh_n", kb4 // 512)])
            for kb in range(NKB):
                pb, pk = ps_p[kb % 2], ("ps_p", kb % 2)
                for c in range(2):
                    k.mm(pb[:, :64], kv[:, c, kb * 128:(kb + 1) * 128], wkv[:, c, h * 128 + 64:h * 128 + 128], c == 0, c == 1,
                         r=["wkv", ("kv", c)], w=[pk])
                k.op("dve", lambda e, kb=kb, pb=pb: e.tensor_copy(out=vh[:, kb, 0:64], in_=pb[:, :64]), r=[pk], w=[("vh", kb)])
            r0 = h * 96 + 64
            for (d0, s0) in ((0, 8), (8, 0), (16, 24), (24, 16)):
                k.op("dve", lambda e, d0=d0, s0=s0, r0=r0: e.tensor_copy(out=wqp[:, :, d0:d0 + 8], in_=wq[:, :, r0 + s0:r0 + s0 + 8]),
                     r=["wq"], w=["wqp"])
            for (ti, t0, N, is_ctx) in qtiles:
                pb, pk = ps_p[0], ("ps_p", 0)
                for c in range(3):
                    k.mm(pb[:64, :N], wq[:, c, h * 96:h * 96 + 64], cq[:, c, t0:t0 + N], c == 0, c == 2, r=["wq", ("cq", c)], w=[pk])
                k.act(qhT[0:64, :N], pb[:64, :N], AF.Copy, r=[pk], w=["qh_n"])
                pb, pk = ps_p[1], ("ps_p", 1)
                for c in range(3):
                    k.mm(pb[:32, :N], wq[:, c, r0:r0 + 32], cq[:, c, t0:t0 + N], c == 0, c == 2, r=["wq", ("cq", c)], w=[pk])
                k.op("dve", lambda e, pb=pb, t0=t0, N=N: e.tensor_tensor(out=qr[:, :N], in0=pb[:32, :N], in1=cos[:, t0:t0 + N], op=ALU.mult),
                     r=[pk, "cos"], w=["qr"])
                for c in range(3):
                    k.mm(pb[:32, :N], wqp[:, c, :], cq[:, c, t0:t0 + N], c == 0, c == 2, r=["wqp", ("cq", c)], w=[pk])
                k.op("dve", lambda e, pb=pb, t0=t0, N=N: e.tensor_tensor(out=qrp[:, :N], in0=pb[:32, :N], in1=sin[:, t0:t0 + N], op=ALU.mult),
                     r=[pk, "sin"], w=["qrp"])
                k.op("dve", lambda e, N=N: e.tensor_tensor(out=qr[:, :N], in0=qr[:, :N], in1=qrp[:, :N], op=ALU.add), r=["qr", "qrp"], w=["qr"])
                k.op("dve", lambda e, N=N: e.tensor_copy(out=qrb[:, :N], in_=qr[:, :N]), r=["qr"], w=["qr_bf"])
                k.dma("sp", qhT[64:96, :N], qrb[:, :N], r=["qr_bf"], w=["qh_r"])
                kbs = list(range(CTX // 128)) if is_ctx else list(range(NKB))
                LA = 2
                nkb = len(kbs)

                def _S(j):
                    s, kb = j % 3, kbs[j]
                    k.mm(ps_s[s][:, :N], khT[0:96, kb * 128:(kb + 1) * 128], qhT[0:96, :N], True, True,
                         r=[("kh_n", kb // 4), "kh_r", "qh_n", "qh_r"], w=[("ps_s", s)])

                for j in range(min(LA, nkb)):
                    _S(j)
                for j in range(nkb):
                    s, kb = j % 3, kbs[j]
                    k.act(pT[s][:, :N], ps_s[s][:, :N], AF.Exp, scale=ATT_SCALE, r=[("ps_s", s)], w=[("pT", s)])
                    if j + LA < nkb:
                        _S(j + LA)
                    k.mm(ps_o[:65, :N], vh[:, kb, 0:65], pT[s][:, :N], j == 0, j == nkb - 1,
                         r=[("vh", kb), "vh_one", ("pT", s)], w=["ps_o"])
                k.act(osb[:, :N], ps_o[:65, :N], AF.Copy, r=["ps_o"], w=["osb"])
                k.mm(ps_d[:64, :N], sel[:, :], osb[:, :N], True, True, r=["sel", "osb"], w=["ps_d"])
                k.op("dve", lambda e, N=N: e.reciprocal(out=rden[:, :N], in_=ps_d[:64, :N]), r=["ps_d"], w=["rden"])
                k.op("dve", lambda e, N=N: e.tensor_tensor(out=ysb[:, :N], in0=osb[0:64, :N], in1=rden[:, :N], op=ALU.mult),
                     r=["osb", "rden"], w=["ysb"])
                k.dma("sp", yc_out[h * 64:(h + 1) * 64, t0:t0 + N], ysb[:, :N], r=["ysb"])


GH = 4
NBLK = NTOK // 128
NCHK = NTOK // 64
GLA_QSCALE = 64.0 ** -0.5


def emit_gla(k, l, mode, zq, zk, zk_tm, zv_tm, zg, za, w_dec, b_dec, g_norm_col, tri, carry_D=None, carry_S=None, ya_out=None, summ_D=None, summ_S=None):
    assert mode in ("summ", "full")
    with k.phase():
        trit = k.sb("gl_tri", [128, 4, 128], F32)
        for i in range(4):
            k.dma("sp", trit[:, i, :], tri[i], w=[("tri", i)])
        maskb = k.sb("gl_maskb", [128, 2, 128], BF16)
        for d in range(2):
            k.op("dve", lambda e, d=d: e.tensor_copy(out=maskb[:, d, :], in_=trit[:, 2 * d, :]), r=[("tri", 2 * d)], w=[("maskb", d)])
        ones_b = k.sb("gl_ones", [128, 128], BF16)
        k.op("dve", lambda e: e.memset(ones_b[:, :], 1.0 / 128.0), w=["gl_ones"])
        aaug = [k.sb(f"gl_aaug{d}", [17, NTOK], F32) for d in range(2)]
        waug = [k.sb(f"gl_waug{d}", [17, 256], F32) for d in range(2)]
        for d in range(2):
            k.op("dve", lambda e, d=d: e.memset(aaug[d][:, :], 1.0), w=[("aaug", d)])
            k.dma("sp", aaug[d][0:16, :], za[16 * d:16 * d + 16, :], r=[("aaug", d)], w=[("aaug", d)])
            k.dma("sp", waug[d][0:16, :], w_dec[d], w=[("waug", d)])
            k.dma("sp", waug[d][16:17, :], b_dec[d:d + 1, :], r=[("waug", d)], w=[("waug", d)])
        qT = k.sb("gl_qT", [64, NTOK], BF16)
        kT = k.sb("gl_kT", [64, NTOK], BF16)
        ktm = k.sb("gl_ktm", [128, NBLK, 64], BF16)
        vtm = k.sb("gl_vtm", [128, NBLK, 128], BF16)
        gT = k.sb("gl_gT", [128, NTOK], BF16)
        cumT = [k.sb(f"gl_cumT{d}", [64, NTOK], F32) for d in range(2)]
        sall = [k.sb(f"gl_sall{d}", [64, NCHK, 128], BF16) for d in range(2)]
        sfp = k.sb("gl_sfp", [64, 128], F32)
        la = [k.sb(f"gl_la{p}", [128, 64], F32) for p in range(2)]
        rc = [k.sb(f"gl_rc{p}", [128, 64], F32) for p in range(2)]
        khatA = [k.sb(f"gl_khatA{p}", [128, 64], BF16) for p in range(2)]
        khatB = [k.sb(f"gl_khatB{p}", [128, 64], BF16) for p in range(2)]
        for p in range(2):
            k.op("dve", lambda e, p=p: e.memset(khatA[p][:, :], 0.0), w=[("khatA", p)])
            k.op("dve", lambda e, p=p: e.memset(khatB[p][:, :], 0.0), w=[("khatB", p)])
        etot = [k.sb(f"gl_etot{p}", [64, 2], F32) for p in range(2)]
        tsum = k.sb("gl_tsum", [64, 1], F32)
        cD = k.sb("gl_cD", [64, 3], F32)
        cS = k.sb("gl_cS", [64, 3, 128], F32)
        qs = [k.sb(f"gl_qs{p}", [64, 128], BF16) for p in range(2)]
        ks = [k.sb(f"gl_ks{p}", [64, 128], BF16) for p in range(2)]
        ecum = [k.sb(f"gl_ecum{p}", [64, 128], F32) for p in range(2)]
        ecum2 = [k.sb(f"gl_ecumn{p}", [64, 128], F32) for p in range(2)]
        aT = [k.sb(f"gl_aT{p}", [128, 128], BF16) for p in range(2)]
        osb = [k.sb(f"gl_osb{p}", [128, 128], F32) for p in range(2)]
        osq = [k.sb(f"gl_osq{p}", [128, 128], BF16) for p in range(2)]
        rstd = [k.sb(f"gl_rstd{p}", [128, 128], F32) for p in range(2)]
        sg = [k.sb(f"gl_sg{p}", [128, 128], F32) for p in range(2)]
        yb = [k.sb(f"gl_yb{p}", [128, 128], BF16) for p in range(2)]
        dsum = k.sb("gl_dsum", [64, 1], F32)
        p_la = [k.ps(f"gl_p_la{p}", [128, 512], F32) for p in range(2)]
        p_st = [k.ps(f"gl_p_st{p}", [128, 512], F32) for p in range(2)]
        p_o = [k.ps(f"gl_p_o{p}", [128, 512], F32) for p in range(2)]
        p_at1 = k.ps("gl_p_at", [128, 512], F32)
        p_n1 = k.ps("gl_p_n", [128, 512], F32)
        p_at = [p_at1, p_at1]
        p_n = [p_n1, p_n1]
        blkcnt = [0]

        def blocks_in_order(d, part):
            bl = [0, 1] if part == "ctx" else list(range(2, NBLK))
            return bl if d == 0 else bl[::-1]

        for h in range(GH):
            hk = ("h", h)
            k.dma("sp", qT[:, :], zq[h * 64:(h + 1) * 64, :], w=["qT"])
            k.dma("sp", kT[:, :], zk[h * 64:(h + 1) * 64, :], w=["kT"])
            k.dma("sp", ktm[:, :, :], zk_tm[:, h * 64:(h + 1) * 64].rearrange("(b p) d -> p b d", p=128), w=["ktm"])
            k.dma("sp", vtm[:, :, :], zv_tm[:, h * 128:(h + 1) * 128].rearrange("(b p) e -> p b e", p=128), w=["vtm"])
            k.dma("sp", gT[:, :], zg[h * 128:(h + 1) * 128, :], w=["gT"])
            for d in range(2):
                if mode == "full":
                    for c3 in range(3):
                        k.dma("sp", cD[:, c3:c3 + 1], carry_D[d, c3, h].rearrange("(p o) -> p o", o=1), w=[("cD", c3)])
                        k.dma("sp", cS[:, c3, :], carry_S[d, c3, h], w=[("cS", c3)])
                for part in (("lat",) if mode == "summ" else ("ctx", "lat")):
                    if part == "ctx" or mode == "summ":
                        k.op("dve", lambda e: e.memset(sfp[:, :], 0.0), w=["sfp"])
                    if mode == "summ":
                        k.op("dve", lambda e: e.memset(dsum[:, :], 0.0), w=["dsum"])
                    if mode == "full" and part == "lat":
                        for c3 in range(3):
                            k.op("dve", lambda e, c3=c3: e.scalar_tensor_tensor(out=sfp[:, :], in0=sfp[:, :], scalar=cD[:, c3:c3 + 1],
                                                                                in1=cS[:, c3, :], op0=ALU.mult, op1=ALU.add),
                                 r=["sfp", ("cD", c3), ("cS", c3)], w=["sfp"])
                    if True:
                        for b in blocks_in_order(d, part):
                            tk = slice(b * 128, (b + 1) * 128)
                            P = blkcnt[0] % 2
                            blkcnt[0] += 1
                            pst, pla = p_st[P], p_la[P]
                            kst, kla = ("p_st", P), ("p_la", P)
                            k.mm(pla[:, 0:64], aaug[d][:, tk], waug[d][:, h * 64:(h + 1) * 64], True, True, r=[("aaug", d), ("waug", d)], w=[kla])
                            k.act(la[P][:, :], pla[:, 0:64], AF.Exp, scale=-1.0, r=[kla], w=[("la", P)])
                            k.act(la[P][:, :], la[P][:, :], AF.Ln, bias=1.0, r=[("la", P)], w=[("la", P)])
                            k.act(la[P][:, :], la[P][:, :], AF.Copy, scale=-1.0 / 16.0, r=[("la", P)], w=[("la", P)])
                            k.mm(pst[:64, 0:128], la[P][:, :], trit[:, 2 * d, :], True, True, r=[("la", P), ("tri", 2 * d)], w=[kst])
                            k.mm(pst[:, 128:192], trit[:, 2 * d + 1, :], la[P][:, :], True, True, r=[("la", P), ("tri", 2 * d + 1)], w=[kst])
                            k.act(cumT[d][:, tk], pst[:64, 0:128], AF.Copy, r=[kst], w=[("cumT", d, b)])
                            k.act(rc[P][:, :], pst[:, 128:192], AF.Exp, r=[kst], w=[("rc", P)])
                            k.op("dve", lambda e, b=b, P=P: e.tensor_tensor(out=khatA[P][0:64, :], in0=ktm[0:64, b, :], in1=rc[P][0:64, :], op=ALU.mult),
                                 r=["ktm", ("rc", P), ("khatA", P)], w=[("khatA", P)])
                            k.op("dve", lambda e, b=b, P=P: e.tensor_tensor(out=khatB[P][64:128, :], in0=ktm[64:128, b, :], in1=rc[P][64:128, :], op=ALU.mult),
                                 r=["ktm", ("rc", P), ("khatB", P)], w=[("khatB", P)])
                            cols = (63, 127) if d == 0 else (0, 64)
                            for c in range(2):
                                k.act(etot[P][:, c:c + 1], pst[:64, cols[c]:cols[c] + 1], AF.Exp, r=[kst], w=[("etot", P, c)])
                                if mode == "summ":
                                    k.op("dve", lambda e, c=c, cols=cols, pst=pst: e.tensor_tensor(out=dsum[:, :], in0=dsum[:, :], in1=pst[:64, cols[c]:cols[c] + 1], op=ALU.add),
                                         r=[kst, "dsum"], w=["dsum"])
                            for c, kh, kk_ in ((0, khatA, "khatA"), (1, khatB, "khatB")):
                                k.mm(pst[:64, 192 + c * 128:192 + (c + 1) * 128], kh[P][:, :], vtm[:, b, :], True, True, r=[(kk_, P), "vtm"], w=[kst])
                            for c in ((0, 1) if d == 0 else (1, 0)):
                                ch = 2 * b + c
                                if mode == "full":
                                    k.op("dve", lambda e, ch=ch, d=d: e.tensor_copy(out=sall[d][:, ch, :], in_=sfp[:, :]), r=["sfp"], w=[("sall", d, ch)])
                                k.op("dve", lambda e, c=c, P=P, pst=pst: e.scalar_tensor_tensor(out=sfp[:, :], in0=sfp[:, :], scalar=etot[P][:, c:c + 1],
                                                                                                in1=pst[:64, 192 + c * 128:192 + (c + 1) * 128], op0=ALU.mult, op1=ALU.add),
                                     r=["sfp", ("etot", P, c), ("p_st", P)], w=["sfp"])
                        if mode == "summ":
                            k.act(dsum[:, :], dsum[:, :], AF.Exp, r=["dsum"], w=["dsum"])
                            k.dma("sp", summ_D[d, h].rearrange("(p o) -> p o", o=1), dsum[:, :], r=["dsum"])
                            k.dma("sp", summ_S[d, h], sfp[:, :], r=["sfp"])
            for b in (range(NBLK) if mode == "full" else ()):
                tk = slice(b * 128, (b + 1) * 128)
                B = b % 2
                po, pn = p_o[B], p_n[B]
                nmm = 0
                for d in range(2):
                    Q = (2 * b + d) % 2
                    k.act(ecum[Q][:, :], cumT[d][:, tk], AF.Exp, r=[("cumT", d, b)], w=[("ecum", Q)])
                    k.op("dve", lambda e, tk=tk, Q=Q: e.scalar_tensor_tensor(out=qs[Q][:, :], in0=qT[:, tk], scalar=GLA_QSCALE, in1=ecum[Q][:, :], op0=ALU.mult, op1=ALU.mult),
                         r=["qT", ("ecum", Q)], w=[("qs", Q)])
                    k.act(ecum2[Q][:, :], cumT[d][:, tk], AF.Exp, scale=-1.0, r=[("cumT", d, b)], w=[("ecumn", Q)])
                    k.op("dve", lambda e, tk=tk, Q=Q: e.tensor_tensor(out=ks[Q][:, :], in0=kT[:, tk], in1=ecum2[Q][:, :], op=ALU.mult), r=["kT", ("ecumn", Q)], w=[("ks", Q)])
                    k.mm(p_at[Q][:, :128], ks[Q][:, :], qs[Q][:, :], True, True, r=[("ks", Q), ("qs", Q)], w=["p_at"])
                    k.op("dve", lambda e, d=d, Q=Q: e.tensor_tensor(out=aT[Q][:, :], in0=p_at[Q][:, :128], in1=maskb[:, d, :], op=ALU.mult), r=["p_at", ("maskb", d)], w=[("aT", Q)])
                    k.mm(po[:, :128], vtm[:, b, :], aT[Q][:, :], nmm == 0, False, r=["vtm", ("aT", Q)], w=[("p_o", B)])
                    nmm += 1
                    for c in range(2):
                        ch = 2 * b + c
                        last = (d == 1 and c == 1)
                        k.mm(po[:, c * 64:(c + 1) * 64], sall[d][:, ch, :], qs[Q][:, c * 64:(c + 1) * 64], False, last, r=[("sall", d, ch), ("qs", Q)], w=[("p_o", B)])
                k.act(osb[B][:, :], po[:, :128], AF.Copy, r=[("p_o", B)], w=[("osb", B)])
                k.act(osq[B][:, :], po[:, :128], AF.Square, r=[("p_o", B)], w=[("osq", B)])
                k.mm(pn[:, :128], ones_b[:, :], osq[B][:, :], True, True, r=["gl_ones", ("osq", B)], w=["p_n"])
                k.act(rstd[B][:, :], pn[:, :128], AF.Ln, bias=EPS, r=["p_n"], w=[("rstd", B)])
                k.act(rstd[B][:, :], rstd[B][:, :], AF.Exp, scale=-0.5, r=[("rstd", B)], w=[("rstd", B)])
                k.act(sg[B][:, :], gT[:, tk], AF.Silu, r=["gT"], w=[("sg", B)])
                k.op("dve", lambda e, B=B: e.scalar_tensor_tensor(out=osb[B][:, :], in0=osb[B][:, :], scalar=g_norm_col[:, 0:1], in1=rstd[B][:, :], op0=ALU.mult, op1=ALU.mult),
                     r=[("osb", B), ("rstd", B), "gnorm"], w=[("osb", B)])
                k.op("dve", lambda e, B=B: e.tensor_tensor(out=yb[B][:, :], in0=osb[B][:, :], in1=sg[B][:, :], op=ALU.mult), r=[("osb", B), ("sg", B)], w=[("yb", B)])
                k.dma("sp", ya_out[h * 128:(h + 1) * 128, tk], yb[B][:, :], r=[("yb", B)])


def fourier_tables(seg):
    n1 = np.arange(64)[:, None]; k1 = np.arange(64)[None, :]
    w64 = np.concatenate([np.cos(2 * np.pi * n1 * k1 / 64), -np.sin(2 * np.pi * n1 * k1 / 64)], 1)
    c = np.arange(128)[:, None]; cp = np.arange(128)[None, :]
    Cc, Sc = np.cos(2 * np.pi * c * cp / 128), np.sin(2 * np.pi * c * cp / 128)
    fca = np.concatenate([Cc, -Sc], 1); fcb = np.concatenate([Sc, Cc], 1)
    n2 = np.arange(128)[:, None, None]; kk = (np.arange(64)[None, :, None] + 64 * (32 * seg + np.arange(32))[None, None, :])
    th = 2 * np.pi * ((n2 * kk) % 8192) / 8192
    e3c, e3s = np.cos(th), np.sin(th)
    n = np.arange(256)[:, None]; k = np.arange(256)[None, :]
    th2 = 2 * np.pi * ((n * k) % 256) / 256
    e256 = np.concatenate([np.cos(th2), -np.sin(th2)], 1)
    f = lambda a: np.ascontiguousarray(a, dtype=np.float32)
    return {"w64": f(w64), "fca": f(fca), "fcb": f(fcb), "e3c": f(e3c.reshape(128, 2048)), "e3s": f(e3s.reshape(128, 2048)), "e256": f(e256)}


def emit_fourier(k, g_all, g_ctx, T, yb_out):
    with k.phase():
        w64 = k.sb("fo_w64", [64, 128], BF16); fca = k.sb("fo_fca", [128, 256], BF16); fcb = k.sb("fo_fcb", [128, 256], BF16)
        e3c = k.sb("fo_e3c", [128, 64, 32], BF16); e3s = k.sb("fo_e3s", [128, 64, 32], BF16)
        k.dma("sp", w64[:, :], T["w64"][:, :], w=["w64"]); k.dma("sp", fca[:, :], T["fca"][:, :], w=["fca"]); k.dma("sp", fcb[:, :], T["fcb"][:, :], w=["fcb"])
        k.dma("sp", e3c[:, :, :], T["e3c"].rearrange("p (a b) -> p a b", b=32), w=["e3c"])
        k.dma("sp", e3s[:, :, :], T["e3s"].rearrange("p (a b) -> p a b", b=32), w=["e3s"])
        g = k.sb("fo_g", [64, 128, 128], BF16)
        A = k.sb("fo_A", [128, 128, 128], BF16)
        B = k.sb("fo_B", [128, 64, 256], BF16)
        yt = k.sb("fo_yt", [128, 2048], BF16)
        ps = [k.ps(f"fo_ps{i}", [128, 512], F32) for i in range(4)]
        pi = [0]

        def nextps():
            i = pi[0] % 4
            pi[0] += 1
            return ps[i], ("fo_ps", i)

        for grp in range(4):
            k.dma("sp", g[:, :, :], g_all[:, grp * 128:(grp + 1) * 128].rearrange("(n1 n2) c -> n1 n2 c", n2=128), w=["fo_g"])
            for q in range(32):
                pb, pk = nextps()
                for j in range(4):
                    n2 = 4 * q + j
                    k.mm(pb[:, j * 128:(j + 1) * 128], g[:, n2, :], w64[:, :], True, True, r=["fo_g", "w64"], w=[pk])
                eng = "act" if q % 2 == 0 else "dve"
                src = pb[:, :].rearrange("p (n r) -> p r n", n=4)
                dst = A[:, :, 4 * q:4 * q + 4]
                if eng == "act":
                    k.act(dst, src, AF.Copy, r=[pk], w=[("fo_A", q)])
                else:
                    k.op("dve", lambda e, dst=dst, src=src: e.tensor_copy(out=dst, in_=src), r=[pk], w=[("fo_A", q)])
            allA = [("fo_A", q) for q in range(32)]
            for kp in range(32):
                pb, pk = nextps()
                for j in range(2):
                    k1 = 2 * kp + j
                    k.mm(pb[:, j * 256:(j + 1) * 256], A[:, k1, :], fca[:, :], True, False, r=allA + ["fca"], w=[pk])
                    k.mm(pb[:, j * 256:(j + 1) * 256], A[:, 64 + k1, :], fcb[:, :], False, True, r=allA + ["fcb"], w=[pk])
                dst = B[:, 2 * kp:2 * kp + 2, :]
                src = pb[:, :].rearrange("p (j x) -> p j x", j=2)
                if kp % 2 == 0:
                    k.act(dst, src, AF.Copy, r=[pk], w=[("fo_B", kp)])
                else:
                    k.op("dve", lambda e, dst=dst, src=src: e.tensor_copy(out=dst, in_=src), r=[pk], w=[("fo_B", kp)])
            for kq in range(4):
                pb, pk = nextps()
                for j in range(16):
                    k1 = 16 * kq + j
                    k.mm(pb[:, j * 32:(j + 1) * 32], B[:, k1, 0:128], e3c[:, k1, :], True, False, r=[("fo_B", k1 // 2), "e3c"], w=[pk])
                    k.mm(pb[:, j * 32:(j + 1) * 32], B[:, k1, 128:256], e3s[:, k1, :], False, True, r=[("fo_B", k1 // 2), "e3s"], w=[pk])
                src = pb[:, :].rearrange("p (k1 k2) -> p k2 k1", k2=32)
                dst = yt[:, :].rearrange("p (k2 k1) -> p k2 k1", k1=64)[:, :, 16 * kq:16 * kq + 16]
                k.act(dst, src, AF.Copy, scale=1.0 / 1024.0, r=[pk], w=[("fo_yt", kq)])
            k.dma("sp", yb_out[grp * 128:(grp + 1) * 128, CTX:NTOK], yt[:, :], r=[("fo_yt", kq) for kq in range(4)])
        if g_ctx is not None:
            e256 = k.sb("fo_e256", [128, 2, 512], BF16)
            k.dma("sp", e256[:, :, :], T["e256"].rearrange("(b p) x -> p b x", p=128), w=["e256"])
            gc = k.sb("fo_gc", [128, 2, 512], BF16)
            k.dma("sp", gc[:, :, :], g_ctx.rearrange("(b p) c -> p b c", p=128), w=["fo_gc"])
            P = k.sb("fo_P", [128, 512], BF16)
            yc = k.sb("fo_yc", [128, 256], BF16)
            for grp in range(4):
                pb, pk = nextps()
                for b in range(2):
                    k.mm(pb[:, :], gc[:, b, grp * 128:(grp + 1) * 128], e256[:, b, :], b == 0, b == 1, r=["fo_gc", "e256"], w=[pk])
                k.act(P[:, :], pb[:, :], AF.Copy, r=[pk], w=["fo_P"])
                pb2, pk2 = nextps()
                k.mm(pb2[:, :256], fca[:, 0:128], P[:, 0:256], True, False, r=["fca", "fo_P"], w=[pk2])
                k.mm(pb2[:, :256], fcb[:, 0:128], P[:, 256:512], False, True, r=["fcb", "fo_P"], w=[pk2])
                k.act(yc[:, :], pb2[:, :256], AF.Copy, scale=float((256.0 * 128.0) ** -0.5), r=[pk2], w=["fo_yc"])
                k.dma("sp", yb_out[grp * 128:(grp + 1) * 128, 0:CTX], yc[:, :], r=["fo_yc"])


def emit_merge(k, nb, l, x, hmix_d, y_d, w_in, w_branch, w_out, colG, with_ctx):
    with k.phase():
        wbr = k.sb("mg_wbr", [128, 12, 1024], BF16)
        wo = k.sb("mg_wo", [128, 8, 1024], BF16)
        k.dma("pool", wbr[:, :, :], w_branch[0].rearrange("i (c p) n -> p (i c) n", p=128), w=["wbr"])
        k.dma("pool", wo[:, :, :], w_out[0].rearrange("(c p) n -> p c n", p=128), w=["wo"])
        wg = [k.sb(f"mg_wg{s}", [128, 8, 512], BF16) for s in range(2)]
        hm = k.sb("mg_hm", [128, 8, 512], BF16)
        yt = k.sb("mg_yt", [128, 12, 512], BF16)
        sgm = k.sb("mg_sgm", [128, 512], F32)
        tmp = k.sb("mg_tmp", [128, 512], F32)
        macc = k.sb("mg_macc", [128, 512], F32)
        mT = k.sb("mg_mT", [128, 8, 512], BF16)
        u = k.sb("mg_u", [128, 8, 512], F32)
        pg = [k.ps(f"mg_pg{s}", [128, 512], F32) for s in range(2)]
        pbch = [k.ps(f"mg_pb{s}", [128, 512], F32) for s in range(2)]
        pu = [k.ps(f"mg_pu{s}", [128, 512], F32) for s in range(2)]
        cnt = 0
        for ti, (t0, N, is_ctx) in enumerate(TT):
            if is_ctx and not with_ctx:
                continue
            tix = 1 if is_ctx else 0
            for c in range(KC):
                k.dma("sp", hm[:, c, :N], hmix_d[c * 128:(c + 1) * 128, t0:t0 + N], w=[("mg_hm", c)])
            for i in range(3):
                for c in range(4):
                    k.dma("sp", yt[:, i * 4 + c, :N], y_d[i][c * 128:(c + 1) * 128, t0:t0 + N], w=[("mg_yt", i * 4 + c)])
            for j in range(6):
                s = cnt % 2
                cnt += 1
                c0 = OFF["gates"] + j * 512
                k.dma("pool", wg[s][:, :, :], w_in[0, :, c0:c0 + 512].rearrange("(kc p) n -> p kc n", p=128), w=[("mg_wg", s)])
                i = j // 2
                for q in range(4):
                    oc = 4 * (j % 2) + q
                    b2 = q % 2
                    for kc in range(KC):
                        k.mm(pg[b2][:, :N], wg[s][:, kc, q * 128:(q + 1) * 128], hm[:, kc, :N], kc == 0, kc == KC - 1,
                             r=[("mg_wg", s), ("mg_hm", kc)], w=[("mg_pg", b2)])
                    for c in range(4):
                        k.mm(pbch[b2][:, :N], wbr[:, i * 4 + c, oc * 128:(oc + 1) * 128], yt[:, i * 4 + c, :N], c == 0, c == 3,
                             r=["wbr", ("mg_yt", i * 4 + c)], w=[("mg_pb", b2)])
                    k.act(sgm[:, :N], pg[b2][:, :N], AF.Sigmoid, r=[("mg_pg", b2)], w=["mg_sgm"])
                    if i == 0:
                        k.op("dve", lambda e, oc=oc, b2=b2, N=N: e.tensor_tensor(out=u[:, oc, :N], in0=sgm[:, :N], in1=pbch[b2][:, :N], op=ALU.mult),
                             r=["mg_sgm", ("mg_pb", b2)], w=[("mg_u", oc)])
                    else:
                        k.op("dve", lambda e, b2=b2, N=N: e.tensor_tensor(out=tmp[:, :N], in0=sgm[:, :N], in1=pbch[b2][:, :N], op=ALU.mult),
                             r=["mg_sgm", ("mg_pb", b2)], w=["mg_tmp"])
                        k.op("dve", lambda e, oc=oc, N=N: e.tensor_tensor(out=u[:, oc, :N], in0=u[:, oc, :N], in1=tmp[:, :N], op=ALU.add),
                             r=["mg_tmp", ("mg_u", oc)], w=[("mg_u", oc)])
            for oc in range(KC):
                k.op("dve", lambda e, oc=oc, N=N: e.tensor_copy(out=mT[:, oc, :N], in_=u[:, oc, :N]), r=[("mg_u", oc)], w=[("mg_mT", oc)])
            for oc in range(KC):
                s = oc % 2
                for c in range(KC):
                    k.mm(pu[s][:, :N], wo[:, c, oc * 128:(oc + 1) * 128], mT[:, c, :N], c == 0, c == KC - 1, r=["wo", ("mg_mT", c)], w=[("mg_pu", s)])
                k.act(u[:, oc, :N], pu[s][:, :N], AF.Copy, r=[("mg_pu", s)] + [("mg_mT", c) for c in range(KC)], w=[("y", oc), ("mg_u", oc)])
            emit_post_residual(k, nb, l, 1, x, u, t0, N, tix, ti, colG)


def emit_pre_exchange(k, nb, zout, qn_col, kvn_col, rope_cos, rope_sin, cqn_d, ckvn_d, krr_d):
    with k.phase():
        on3 = k.sb("px_on3", [128, 128], BF16); on2 = k.sb("px_on2", [128, 128], BF16)
        k.op("dve", lambda e: e.memset(on3[:, :], 1.0 / 384.0), w=["px_on3"])
        k.op("dve", lambda e: e.memset(on2[:, :], 1.0 / 256.0), w=["px_on2"])
        src = k.sb("px_src", [128, 3, 512], F32); ob = k.sb("px_ob", [128, 512], BF16)
        kr = k.sb("px_kr", [32, 512], F32); krp = k.sb("px_krp", [32, 512], F32)
        cs = k.sb("px_cos", [32, NTOK], F32); sn = k.sb("px_sin", [32, NTOK], F32)
        k.dma("sp", cs[:, :], rope_cos[:, :], w=["px_cos"]); k.dma("sp", sn[:, :], rope_sin[:, :], w=["px_sin"])
        for ti, (t0, N, is_ctx) in enumerate(TT):
            for (name, nch, ones, okey, col, ckey, dst) in (("cq", 3, on3, "px_on3", qn_col, "qn_col", cqn_d), ("ckv", 2, on2, "px_on2", kvn_col, "kvn_col", ckvn_d)):
                for c in range(nch):
                    k.dma("sp", src[:, c, :N], zout[name][c * 128:(c + 1) * 128, t0:t0 + N], r=[("z", name, c, ti)], w=[("px_src", c)])
                emit_rstd(k, nb, lambda c: src[:, c, :N], N, lambda c: [("px_src", c)], nch=nch, ones=ones, ones_key=okey)
                for c in range(nch):
                    k.op("dve", lambda e, c=c, col=col, N=N: e.scalar_tensor_tensor(out=ob[:, :N], in0=src[:, c, :N], scalar=col[:, c:c + 1], in1=nb.rstd[:, :N],
                                                                                   op0=ALU.mult, op1=ALU.mult), r=[("px_src", c), ckey, "rstd"], w=["px_ob"])
                    k.dma("sp", dst[c * 128:(c + 1) * 128, t0:t0 + N], ob[:, :N], r=["px_ob"])
            k.dma("sp", kr[:, :N], zout["kr"][:, t0:t0 + N], r=[("z", "kr", 0, ti)], w=["px_kr"])
            k.dma("sp", krp[:, :N], zout["krp"][:, t0:t0 + N], r=[("z", "krp", 0, ti)], w=["px_krp"])
            k.op("dve", lambda e, t0=t0, N=N: e.tensor_tensor(out=kr[:, :N], in0=kr[:, :N], in1=cs[:, t0:t0 + N], op=ALU.mult), r=["px_kr", "px_cos"], w=["px_kr"])
            k.op("dve", lambda e, t0=t0, N=N: e.tensor_tensor(out=krp[:, :N], in0=krp[:, :N], in1=sn[:, t0:t0 + N], op=ALU.mult), r=["px_krp", "px_sin"], w=["px_krp"])
            k.op("dve", lambda e, N=N: e.tensor_tensor(out=ob[0:32, :N], in0=kr[:, :N], in1=krp[:, :N], op=ALU.add), r=["px_kr", "px_krp"], w=["px_ob"])
            k.dma("sp", krr_d[:, t0:t0 + N], ob[0:32, :N], r=["px_ob"])


Z_SPECS = ([(n, [nf, NTOK], ("f32" if n in ("a", "cq", "ckv", "kr") else "bf16")) for n, _, nf in Z_FM] + [("krp", [32, NTOK], "f32")]
           + [(n, [NTOK, nc_], "bf16") for n, _, nc_ in Z_TM])
EXCH_SPECS = [("cqn", [384, NTOK], "bf16"), ("ckvn", [256, NTOK], "bf16"), ("krr", [32, NTOK], "bf16"),
              ("summ_D", [2, GH, 64], "f32"), ("summ_S", [2, GH, 64, 128], "f32")]
LAY_SPECS = [("vec", [128, 128]), ("modT", [128, 144]), ("colA", [128, 48]), ("colG", [128, 48])]
FT_SHAPES = {"w64": [64, 128], "fca": [128, 256], "fcb": [128, 256], "e3c": [128, 2048], "e3s": [128, 2048], "e256": [256, 512]}


def _dt(s):
    return F32 if s == "f32" else BF16


def build_launch(stage, debug=False):
    import contextlib
    with contextlib.ExitStack() as st:
        k = K(st)
        do_back = stage in ("B", "C")
        do_front = stage in ("A", "B")
        lb = {"B": 0, "C": 1}.get(stage)
        lf = {"A": 0, "B": 1}.get(stage)
        ident = k.dram_in("ident", [128, 128])
        xin = k.dram_in("xin", [D, NTOK])
        xout = k.dram_out("xout", [D, NTOK])
        idf = k.sb("idf", [128, 128], F32)
        k.dma("sp", idf[:, :], ident[:, :], w=["ident"])
        x = None

        def load_x():
            xt = k.sb("x", [128, 8, NTOK], F32)
            for kc in range(KC):
                for ti, (t0, N, _) in enumerate(TT):
                    k.dma("sp", xt[:, kc, t0:t0 + N], xin[kc * 128:(kc + 1) * 128, t0:t0 + N], w=[("x", kc, ti)])
            return xt

        if do_back:
            l = 0
            zin = {n: k.dram_in("zi_" + n, s, _dt(d)) for n, s, d in Z_SPECS}
            hmix_in = k.dram_in("hmix_in", [D, NTOK], BF16)
            cqn_in = k.dram_in("cqn_in", [384, NTOK], BF16)
            ckvn_all = k.dram_in("ckvn_all", [256, NKEY], BF16)
            krr_all = k.dram_in("krr_all", [32, NKEY], BF16)
            g_all = k.dram_in("g_all", [8192, 512], BF16)
            carry_D = k.dram_in("carry_D", [2, 3, GH, 64]); carry_S = k.dram_in("carry_S", [2, 3, GH, 64, 128])
            tri = k.dram_in("tri", [4, 128, 128])
            rq_cos = k.dram_in("rope_cos", [32, NTOK]); rq_sin = k.dram_in("rope_sin", [32, NTOK])
            FT = {n: k.dram_in("ft_" + n, s, BF16) for n, s in FT_SHAPES.items()}
            lay_in = {n: k.dram_in("layi_" + n, s) for n, s in LAY_SPECS}
            w_dec = k.dram_in("b_w_dec", [2, 16, 256]); b_dec = k.dram_in("b_b_dec", [2, 256])
            w_uq = k.dram_in("b_w_uq", [384, 768]); w_ukv = k.dram_in("b_w_ukv", [256, 1024])
            w_in_b = k.dram_in("b_w_in", [1, D, DIN]); w_branch = k.dram_in("b_w_branch", [1, 3, 512, D]); w_out = k.dram_in("b_w_out", [1, D, D])
            wg_b = k.dram_in("b_wg", [1, 1, D, DFF]); wu_b = k.dram_in("b_wu", [1, 1, D, DFF]); wd_b = k.dram_in("b_wd", [1, 1, DFF, D])
            if debug:
                y_d = [k.dram_out(f"y_br{i}", [512, NTOK], BF16) for i in range(3)]
                xmix_d = k.dram_out("xmix", [D, NTOK]); xmid_d = k.dram_out("xmid", [D, NTOK])
            else:
                y_d = [k.nc.dram_tensor(f"y_br{i}", [512, NTOK], BF16).ap() for i in range(3)]
            bvec = k.sb("b_vec", [128, 128], F32); bmodT = k.sb("b_modT", [128, 72, 2], F32)
            bcolA = k.sb("b_colA", [128, 3, 8, 2], F32); bcolG = k.sb("b_colG", [128, 3, 8, 2], F32)
            k.dma("sp", bvec[:, :], lay_in["vec"][:, :], w=[("vec", "b"), "gnorm"])
            k.dma("sp", bmodT[:, :, :], lay_in["modT"].rearrange("p (j t) -> p j t", t=2), w=[("modT", "b")])
            k.dma("sp", bcolA[:, :, :, :], lay_in["colA"].rearrange("p (i c t) -> p i c t", i=3, c=8), w=[("colA", "b", i) for i in range(3)])
            k.dma("sp", bcolG[:, :, :, :], lay_in["colG"].rearrange("p (i c t) -> p i c t", i=3, c=8), w=[("colG", "b", i) for i in range(3)])
            with_ctx = (stage == "B")
            emit_gla(k, "b", "full", zin["q"], zin["k"], zin["ktm"], zin["v"], zin["g"], zin["a"], w_dec, b_dec, bvec[:, 120:121], tri,
                     carry_D=carry_D, carry_S=carry_S, ya_out=y_d[0])
            emit_fourier(k, g_all, zin["four"][0:CTX, :] if with_ctx else None, FT, y_d[1])
            emit_mla_attention(k, "b", ckvn_all, krr_all, cqn_in, rq_cos, rq_sin, w_uq, w_ukv, y_d[2], ctx_queries=with_ctx)
            x = load_x()
            with k.phase():
                nb = NormBufs(k)
                emit_merge(k, nb, "b", x, hmix_in, y_d, w_in_b, w_branch, w_out, bcolG, with_ctx)
                if debug:
                    for kc in range(KC):
                        k.dma("sp", xmix_d[kc * 128:(kc + 1) * 128, :], x[:, kc, :], r=[("x", kc, ti) for ti in range(len(TT))])
            with k.phase():
                nb = NormBufs(k); fb = FfnBufs(k)
                emit_half_ffn(k, nb, fb, "b", 2, 0, x, bmodT, bcolA, bcolG, wg_b, wu_b, wd_b, skip_ctx=not with_ctx)
                if debug:
                    for kc in range(KC):
                        k.dma("sp", xmid_d[kc * 128:(kc + 1) * 128, :], x[:, kc, :], r=[("x", kc, ti) for ti in range(len(TT))])
        if do_front:
            w_mod = k.dram_in("f_w_mod", [1, D, 9 * D]); b_mod = k.dram_in("f_b_mod", [1, 9 * D])
            npre = k.dram_in("f_norm_pre", [1, 3, D]); npost = k.dram_in("f_norm_post", [1, 3, D])
            gnrm = k.dram_in("f_gla_norm", [1, 128]); qnrm = k.dram_in("f_q_norm", [3, 128]); kvnrm = k.dram_in("f_kv_norm", [2, 128])
            cc = k.dram_in("cc", [2, D])
            wg_f = k.dram_in("f_wg", [1, 1, D, DFF]); wu_f = k.dram_in("f_wu", [1, 1, D, DFF]); wd_f = k.dram_in("f_wd", [1, 1, DFF, D])
            w_in_f = k.dram_in("f_w_in", [1, D, DIN])
            w_dec_f = k.dram_in("f_w_dec", [2, 16, 256]); b_dec_f = k.dram_in("f_b_dec", [2, 256])
            tri_f = k.dram_in("f_tri", [4, 128, 128])
            rk_cos = k.dram_in("f_rope_cos", [32, NTOK]); rk_sin = k.dram_in("f_rope_sin", [32, NTOK])
            zout = {n: k.dram_out("zo_" + n, s, _dt(d)) for n, s, d in Z_SPECS}
            hmix_out = k.dram_out("hmix_out", [D, NTOK], BF16)
            ex = {n: k.dram_out("ex_" + n, s, _dt(d)) for n, s, d in EXCH_SPECS}
            lay_out = {n: k.dram_out("layo_" + n, s) for n, s in LAY_SPECS}
            vec = k.sb("f_vec", [128, 128], F32); modT = k.sb("f_modT", [128, 72, 2], F32)
            colA = k.sb("f_colA", [128, 3, 8, 2], F32); colG = k.sb("f_colG", [128, 3, 8, 2], F32)
            if x is None:
                x = load_x()
            with k.phase():
                pss = k.ps("pss", [128, 512], F32)
                emit_small_vectors(k, [b_mod[0].rearrange("(r p) -> r p", p=128), npre[0].rearrange("j (r p) -> (j r) p", p=128),
                                       npost[0].rearrange("j (r p) -> (j r) p", p=128), gnrm, qnrm, kvnrm], idf, vec, ("vec", "f"), pss)
                cst = k.sb("cst", [128, 16], F32)
                emit_small_vectors(k, [cc.rearrange("t (r p) -> (t r) p", p=128)], idf, cst, "cvec", pss)
                cs_bf = k.sb("cs_bf", [128, 8, 2], BF16)
                k.act(cs_bf[:, :, :], cst[:, :16].rearrange("p (t r) -> p r t", t=2), AF.Silu, r=["cvec"], w=["cs_bf"])
                wbuf = [k.sb(f"wmodbuf{i}", [128, 8, 512], BF16) for i in range(2)]
                emit_modulation(k, "f", w_mod, vec[:, 0:72], cs_bf, modT, wbuf, pss[:, 0:144].rearrange("p (j t) -> p j t", t=2))
                emit_mod_columns(k, "f", modT, vec, colA, colG)
                k.dma("sp", lay_out["vec"][:, :], vec[:, :], r=[("vec", "f")])
                k.dma("sp", lay_out["modT"].rearrange("p (j t) -> p j t", t=2), modT[:, :, :], r=[("modT", "f")])
                k.dma("sp", lay_out["colA"].rearrange("p (i c t) -> p i c t", i=3, c=8), colA[:, :, :, :], r=[("colA", "f", i) for i in range(3)])
                k.dma("sp", lay_out["colG"].rearrange("p (i c t) -> p i c t", i=3, c=8), colG[:, :, :, :], r=[("colG", "f", i) for i in range(3)])
            with k.phase():
                nb = NormBufs(k); fb = FfnBufs(k)
                emit_half_ffn(k, nb, fb, "f", 0, 0, x, modT, colA, colG, wg_f, wu_f, wd_f)
            with k.phase():
                nb = NormBufs(k); wb = WinBufs(k)

                class PB:
                    pass
                pbuf = PB()
                pbuf.h = k.sb("pn_h", [128, 8, 512], BF16)
                pbuf.pg = [k.ps(f"pn_pg{s}", [128, 512], F32) for s in range(2)]
                pbuf.pu = [k.ps(f"pn_pu{s}", [128, 512], F32) for s in range(2)]
                hmix = k.sb("hmix", [128, 8, NTOK], BF16)
                for ti, (t0, N, is_ctx) in enumerate(TT):
                    emit_prenorm(k, nb, "f", 1, x, pbuf.h, t0, N, 1 if is_ctx else 0, ti, modT, colA)
                    for kc in range(KC):
                        k.op("dve", lambda e, kc=kc, t0=t0, N=N: e.tensor_copy(out=hmix[:, kc, t0:t0 + N], in_=pbuf.h[:, kc, :N]),
                             r=[("h", kc)], w=[("hmix", kc, ti)])
                for kc in range(KC):
                    k.dma("sp", hmix_out[kc * 128:(kc + 1) * 128, :], hmix[:, kc, :], r=[("hmix", kc, ti) for ti in range(len(TT))])
                emit_win(k, wb, pbuf, 0, w_in_f, hmix, zout)
            with k.phase():
                nb = NormBufs(k)
                k.op("dve", lambda e: e.tensor_copy(out=nb.tmp[:, 0:1], in_=vec[:, 0:1]), r=[("vec", "f")], w=["qn_col", "kvn_col"])
                emit_pre_exchange(k, nb, zout, vec[:, 121:124], vec[:, 124:126], rk_cos, rk_sin, ex["cqn"], ex["ckvn"], ex["krr"])
            emit_gla(k, "f", "summ", zout["q"], zout["k"], zout["ktm"], zout["v"], zout["g"], zout["a"], w_dec_f, b_dec_f, vec[:, 120:121], tri_f,
                     summ_D=ex["summ_D"], summ_S=ex["summ_S"])
        for kc in range(KC):
            k.dma("sp", xout[kc * 128:(kc + 1) * 128, :], x[:, kc, :], r=[("x", kc, ti) for ti in range(len(TT))])
        k.flush(final=True)
        return k.nc


def _tri_consts():
    s = np.arange(128)[:, None]; t = np.arange(128)[None, :]; same = (s // 64) == (t // 64)
    return np.stack([same & (s <= t), same & (s > t), same & (s >= t), same & (s < t)]).astype(np.float32)


def _rope_tables(seg):
    pos = seg * SEG + np.arange(SEG)
    inv = 10000.0 ** (-np.arange(0, 16, 2, dtype=np.float32) / 16)
    ar, ac = (pos // 64)[:, None] * inv, (pos % 64)[:, None] * inv
    cos = np.ones((NTOK, 32), np.float32); sin = np.zeros((NTOK, 32), np.float32)
    cos[CTX:] = np.concatenate([np.cos(ar), np.cos(ar), np.cos(ac), np.cos(ac)], 1)
    sin[CTX:] = np.concatenate([-np.sin(ar), np.sin(ar), -np.sin(ac), np.sin(ac)], 1)
    return np.ascontiguousarray(cos.T), np.ascontiguousarray(sin.T)


def _host_gather(outs):
    import ml_dtypes
    res = []
    for core in range(8):
        b, s = divmod(core, 4)
        grp = [outs[4 * b + j] for j in range(4)]
        me = outs[core]
        ckvn_all = np.concatenate([np.asarray(me["ex_ckvn"])[:, :CTX]] + [np.asarray(g["ex_ckvn"])[:, CTX:] for g in grp], 1)
        krr_all = np.concatenate([np.asarray(me["ex_krr"])[:, :CTX]] + [np.asarray(g["ex_krr"])[:, CTX:] for g in grp], 1)
        g_all = np.concatenate([np.asarray(g["zo_four"])[CTX:] for g in grp], 0)
        cD = np.ones((2, 3, GH, 64), np.float32); cS = np.zeros((2, 3, GH, 64, 128), np.float32)
        for i, j in enumerate((0, 1, 2)):
            if j < s:
                cD[0, i] = np.asarray(grp[j]["ex_summ_D"])[0]; cS[0, i] = np.asarray(grp[j]["ex_summ_S"])[0]
        for i, j in enumerate((3, 2, 1)):
            if j > s:
                cD[1, i] = np.asarray(grp[j]["ex_summ_D"])[1]; cS[1, i] = np.asarray(grp[j]["ex_summ_S"])[1]
        d = {"ckvn_all": np.ascontiguousarray(ckvn_all), "krr_all": np.ascontiguousarray(krr_all), "g_all": np.ascontiguousarray(g_all),
             "carry_D": cD, "carry_S": cS, "cqn_in": np.asarray(me["ex_cqn"]), "hmix_in": np.asarray(me["hmix_out"]), "xin": np.asarray(me["xout"])}
        for n, _, _ in Z_SPECS:
            d["zi_" + n] = np.asarray(me["zo_" + n])
        for n, _ in LAY_SPECS:
            d["layi_" + n] = np.asarray(me["layo_" + n])
        res.append(d)
    return res


def kernel_unfused(x, c, ctx, c_ctx, w_mod, b_mod, norm_pre, norm_post, ffn_w_gate, ffn_w_up, ffn_w_down, w_in, gla_w_decay, gla_b_decay,
           gla_norm, mla_q_norm, mla_w_uq, mla_kv_norm, mla_w_ukv, w_branch, w_out):
    import ml_dtypes
    bf = ml_dtypes.bfloat16
    f32 = lambda a: np.ascontiguousarray(np.asarray(a, np.float32))
    x, ctx, c, c_ctx = f32(x), f32(ctx), f32(c), f32(c_ctx)
    ident = np.eye(128, dtype=np.float32); tri = _tri_consts()

    def front_w(l):
        return {"f_w_mod": f32(w_mod[l:l + 1]), "f_b_mod": f32(b_mod[l:l + 1]), "f_norm_pre": f32(norm_pre[l:l + 1]), "f_norm_post": f32(norm_post[l:l + 1]),
                "f_gla_norm": f32(gla_norm[l]).reshape(1, 128), "f_q_norm": f32(mla_q_norm[l]).reshape(3, 128), "f_kv_norm": f32(mla_kv_norm[l]).reshape(2, 128),
                "f_wg": f32(ffn_w_gate[l:l + 1, 0:1]), "f_wu": f32(ffn_w_up[l:l + 1, 0:1]), "f_wd": f32(ffn_w_down[l:l + 1, 0:1]), "f_w_in": f32(w_in[l:l + 1]),
                "f_w_dec": f32(gla_w_decay[l]), "f_b_dec": f32(gla_b_decay[l]), "f_tri": tri}

    def back_w(l):
        return {"b_w_dec": f32(gla_w_decay[l]), "b_b_dec": f32(gla_b_decay[l]), "b_w_uq": f32(mla_w_uq[l]), "b_w_ukv": f32(mla_w_ukv[l]),
                "b_w_in": f32(w_in[l:l + 1]), "b_w_branch": f32(w_branch[l:l + 1]), "b_w_out": f32(w_out[l:l + 1]),
                "b_wg": f32(ffn_w_gate[l:l + 1, 1:2]), "b_wu": f32(ffn_w_up[l:l + 1, 1:2]), "b_wd": f32(ffn_w_down[l:l + 1, 1:2]), "tri": tri}

    per_core = []
    for core in range(8):
        b, s = divmod(core, 4)
        rc, rs = _rope_tables(s)
        ft = {"ft_" + n: v.astype(bf) for n, v in fourier_tables(s).items()}
        per_core.append({"cc": np.stack([c[b], c_ctx], 0), "rope": (rc, rs), "ft": ft})
    ncA = build_launch("A")
    fw = front_w(0)
    maps = []
    for core in range(8):
        b, s = divmod(core, 4)
        xl = np.concatenate([ctx[b], x[b, s * SEG:(s + 1) * SEG]], 0)
        m = {"ident": ident, "xin": np.ascontiguousarray(xl.T), "cc": per_core[core]["cc"], "f_rope_cos": per_core[core]["rope"][0], "f_rope_sin": per_core[core]["rope"][1]}
        m.update(fw)
        maps.append(m)
    outs = run_bass_kernel_spmd(ncA, maps, core_ids=list(range(8))).results
    ncB = build_launch("B")
    g = _host_gather(outs)
    bw, fw = back_w(0), front_w(1)
    maps = []
    for core in range(8):
        m = {"ident": ident, "cc": per_core[core]["cc"], "rope_cos": per_core[core]["rope"][0], "rope_sin": per_core[core]["rope"][1],
             "f_rope_cos": per_core[core]["rope"][0], "f_rope_sin": per_core[core]["rope"][1]}
        m.update(per_core[core]["ft"]); m.update(g[core]); m.update(bw); m.update(fw)
        maps.append(m)
    outs = run_bass_kernel_spmd(ncB, maps, core_ids=list(range(8))).results
    ncC = build_launch("C")
    g = _host_gather(outs)
    bw = back_w(1)
    maps = []
    for core in range(8):
        m = {"ident": ident, "rope_cos": per_core[core]["rope"][0], "rope_sin": per_core[core]["rope"][1]}
        m.update(per_core[core]["ft"]); m.update(g[core]); m.update(bw)
        maps.append(m)
    outs = run_bass_kernel_spmd(ncC, maps, core_ids=list(range(8))).results
    out = np.empty((2, 8192, D), np.float32)
    for core in range(8):
        b, s = divmod(core, 4)
        out[b, s * SEG:(s + 1) * SEG] = np.asarray(outs[core]["xout"]).T[CTX:]
    return out


class KF(K):
    def __init__(self, st):
        self.st = st
        self._stacks = [st]
        self.nc = nc = bass.Bass("TRN2", target_bir_lowering=False)
        E = st.enter_context
        sems = {e: E(nc.semaphore("s_" + e)) for e in Sched.ENGS}
        dsems = {e: [E(nc.semaphore(f"d_{e}{i}")) for i in range(12)] for e in ("sp", "pool", "act")}
        self.cc_sem = E(nc.semaphore("cc_sem"))
        self.cc_n = 0
        self.sems2 = {e: E(nc.semaphore("s2_" + e)) for e in Sched.ENGS}
        self.S = Sched(nc, sems, dsems)
        self.uid = 0

    def dram_in(self, name, shape, dt=F32):
        return self.nc.declare_dram_parameter(name, list(shape), dt, isOutput=False).ap()

    def dram_out(self, name, shape, dt=F32):
        return self.nc.declare_dram_parameter(name, list(shape), dt, isOutput=True).ap()

    def scratch(self, name, shape, dt):
        return self.nc.dram_tensor(self._uname(name), list(shape), dt)

    def allreduce(self, gin, gout):
        self.flush(final=True)
        kk = self
        with self.nc.Block() as blk:
            @blk.gpsimd
            def _(g):
                kk.cc_n += 1
                g.collective_compute("AllReduce", ALU.add, replica_groups=[list(range(8))],
                                     ins=[gin.ap().opt()], outs=[gout.ap().opt()]).then_inc(kk.cc_sem)
                g.wait_ge(kk.cc_sem, kk.cc_n)


def emit_exchange(k, ex, zout, M, G):
    W = SEG
    if not hasattr(k, "_gbufs"):
        k._gbufs = (k.scratch("g1i", [8, 288, W], BF16), k.scratch("g1o", [8, 288, W], BF16),
                    k.scratch("g2i", [8, W, 512], BF16), k.scratch("g2o", [8, W, 512], BF16),
                    k.scratch("g3i", [8, 64, 2 * GH * 129], F32), k.scratch("g3o", [8, 64, 2 * GH * 129], F32))
    g1i, g1o, g2i, g2o, g3i, g3o = k._gbufs
    with k.phase():
        a = k.sb("xc_a", [128, 3, W], BF16)
        f = k.sb("xc_f", [128, 16, 512], BF16)
        s3 = k.sb("xc_s3", [64, 2 * GH, 129], F32)
        sta = k.sb("xc_sta", [128, 3, W], BF16)
        stf = k.sb("xc_stf", [128, 16, 512], BF16)
        st3 = k.sb("xc_st3", [64, 2 * GH, 129], F32)
        k.op("dve", lambda e: e.memset(a[:, 2, :], 0.0), w=[("xa", 2)])
        for c in range(2):
            k.dma("sp", a[:, c, :], ex["ckvn"][c * 128:(c + 1) * 128, CTX:NTOK], w=[("xa", c)])
        k.dma("sp", a[0:32, 2, :], ex["krr"][:, CTX:NTOK], r=[("xa", 2)], w=[("xa", 2)])
        k.dma("sp", f[:, :, :], zout["four"][CTX:NTOK, :].rearrange("(b p) c -> p b c", p=128), w=["xf"])
        k.dma("sp", s3[:, :, 0:1], ex["summ_D"].rearrange("d h (p o) -> p (d h) o", o=1), w=["xs3d"], slow=True)
        k.dma("sp", s3[:, :, 1:129], ex["summ_S"].rearrange("d h p e -> p (d h) e"), w=["xs3s"])
        for j in range(8):
            mj = M["m8"][:, j:j + 1]
            k.op("dve", lambda e, mj=mj: e.tensor_scalar(out=sta[:, :, :], in0=a[:, :, :], scalar1=mj, scalar2=None, op0=ALU.mult),
                 r=[("xa", 0), ("xa", 1), ("xa", 2), "m8"], w=["xsta"])
            for c in range(2):
                k.dma("sp", g1i.ap()[j, c * 128:(c + 1) * 128, :], sta[:, c, :], r=["xsta"])
            k.dma("sp", g1i.ap()[j, 256:288, :], sta[0:32, 2, :], r=["xsta"])
            k.op("dve", lambda e, mj=mj: e.tensor_scalar(out=stf[:, :, :], in0=f[:, :, :], scalar1=mj, scalar2=None, op0=ALU.mult),
                 r=["xf", "m8"], w=["xstf"])
            k.dma("sp", g2i.ap()[j].rearrange("(b p) c -> p b c", p=128), stf[:, :, :], r=["xstf"])
            k.op("dve", lambda e, mj=mj: e.tensor_scalar(out=st3[:, :, :], in0=s3[:, :, :], scalar1=mj[0:64, :], scalar2=None, op0=ALU.mult),
                 r=["xs3d", "xs3s", "m8"], w=["xst3"])
            k.dma("sp", g3i.ap()[j].rearrange("p (q e) -> p q e", e=129), st3[:, :, :], r=["xst3"])
    k.allreduce(g1i, g1o)
    k.allreduce(g2i, g2o)
    k.allreduce(g3i, g3o)
    with k.phase():
        t0_ = k.sb("xs_t0", [128, 16, 512], BF16); t1_ = k.sb("xs_t1", [128, 16, 512], BF16)
        u0 = k.sb("xs_u0", [128, 3, W], BF16); u1 = k.sb("xs_u1", [128, 3, W], BF16)
        k.op("dve", lambda e: e.memset(u0[:, :, :], 0.0), w=["xu0"])
        k.op("dve", lambda e: e.memset(u1[:, :, :], 0.0), w=["xu1"])
        for s in range(4):
            for (buf, slot, key) in ((u0, s, "xu0"), (u1, 4 + s, "xu1")):
                for c in range(2):
                    k.dma("sp", buf[:, c, :], g1o.ap()[slot, c * 128:(c + 1) * 128, :], r=[key], w=[key])
                k.dma("sp", buf[0:32, 2, :], g1o.ap()[slot, 256:288, :], r=[key], w=[key])
            k.op("dve", lambda e: e.tensor_scalar(out=u0[:, :, :], in0=u0[:, :, :], scalar1=M["mb"][:, 0:1], scalar2=None, op0=ALU.mult), r=["xu0", "mb"], w=["xu0"])
            k.op("dve", lambda e: e.scalar_tensor_tensor(out=u0[:, :, :], in0=u1[:, :, :], scalar=M["mb"][:, 1:2], in1=u0[:, :, :], op0=ALU.mult, op1=ALU.add),
                 r=["xu0", "xu1", "mb"], w=["xu0"])
            c0 = CTX + s * W
            for c in range(2):
                k.dma("sp", G["ckvn_all"].ap()[c * 128:(c + 1) * 128, c0:c0 + W], u0[:, c, :], r=["xu0"], w=["xu0"])
            k.dma("sp", G["krr_all"].ap()[:, c0:c0 + W], u0[0:32, 2, :], r=["xu0"], w=["xu0"])
            k.dma("sp", t0_[:, :, :], g2o.ap()[s].rearrange("(b p) c -> p b c", p=128), w=["xt0"])
            k.dma("sp", t1_[:, :, :], g2o.ap()[4 + s].rearrange("(b p) c -> p b c", p=128), w=["xt1"])
            k.op("dve", lambda e: e.tensor_scalar(out=t0_[:, :, :], in0=t0_[:, :, :], scalar1=M["mb"][:, 0:1], scalar2=None, op0=ALU.mult), r=["xt0", "mb"], w=["xt0"])
            k.op("dve", lambda e: e.scalar_tensor_tensor(out=t0_[:, :, :], in0=t1_[:, :, :], scalar=M["mb"][:, 1:2], in1=t0_[:, :, :], op0=ALU.mult, op1=ALU.add),
                 r=["xt0", "xt1", "mb"], w=["xt0"])
            k.dma("sp", G["g_all"].ap()[s * W:(s + 1) * W, :].rearrange("(b p) c -> p b c", p=128), t0_[:, :, :], r=["xt0"], w=["xt0"])
        cx = k.sb("xs_cx", [128, 3, CTX], BF16)
        for c in range(2):
            k.dma("sp", cx[:, c, :], ex["ckvn"][c * 128:(c + 1) * 128, 0:CTX], w=[("xcx", c)])
            k.dma("sp", G["ckvn_all"].ap()[c * 128:(c + 1) * 128, 0:CTX], cx[:, c, :], r=[("xcx", c)])
        k.dma("sp", cx[0:32, 2, :], ex["krr"][:, 0:CTX], w=[("xcx", 2)])
        k.dma("sp", G["krr_all"].ap()[:, 0:CTX], cx[0:32, 2, :], r=[("xcx", 2)])
        g3 = k.sb("xs_g3", [64, 8, 2 * GH * 129], F32)
        k.dma("sp", g3[:, :, :], g3o.ap().rearrange("s p q -> p s q"), w=["xg3"])
        acc = k.sb("xs_acc", [64, GH, 129], F32)
        for d in range(2):
            for i in range(3):
                for slot in range(8):
                    wcol = M["cw"][0:64, (d * 3 + i) * 8 + slot:(d * 3 + i) * 8 + slot + 1]
                    src = g3[:, slot, :].rearrange("p (q e) -> p q e", e=129)[:, d * GH:(d + 1) * GH, :]
                    if slot == 0:
                        k.op("dve", lambda e, wcol=wcol, src=src: e.tensor_scalar(out=acc[:, :, :], in0=src, scalar1=wcol, scalar2=None, op0=ALU.mult),
                             r=["xg3", "cw"], w=["xacc"])
                    else:
                        k.op("dve", lambda e, wcol=wcol, src=src: e.scalar_tensor_tensor(out=acc[:, :, :], in0=src, scalar=wcol, in1=acc[:, :, :], op0=ALU.mult, op1=ALU.add),
                             r=["xg3", "cw", "xacc"], w=["xacc"])
                k.op("dve", lambda e, d=d, i=i: e.tensor_scalar(out=acc[:, :, 0:1], in0=acc[:, :, 0:1], scalar1=M["cid"][0:64, d * 3 + i:d * 3 + i + 1], scalar2=None, op0=ALU.add),
                     r=["xacc", "cid"], w=["xacc"])
                k.dma("sp", G["carry_D"].ap()[d, i].rearrange("h (p o) -> p h o", o=1), acc[:, :, 0:1], r=["xacc"], w=["xacc"], slow=True)
                k.dma("sp", G["carry_S"].ap()[d, i].rearrange("h p e -> p h e"), acc[:, :, 1:129], r=["xacc"], w=["xacc"])


def _exchange_consts(core):
    b, s = divmod(core, 4)
    m8 = np.zeros(8, np.float32); m8[core] = 1.0
    mb = np.zeros(2, np.float32); mb[b] = 1.0
    cw = np.zeros((2, 3, 8), np.float32); cid = np.ones((2, 3), np.float32)
    for i, j in enumerate((0, 1, 2)):
        if j < s:
            cw[0, i, 4 * b + j] = 1.0; cid[0, i] = 0.0
    for i, j in enumerate((3, 2, 1)):
        if j > s:
            cw[1, i, 4 * b + j] = 1.0; cid[1, i] = 0.0
    t = lambda v: np.ascontiguousarray(np.tile(v.reshape(1, -1), (128, 1)), dtype=np.float32)
    return {"m8": t(m8), "mb": t(mb), "cw": t(cw), "cid": t(cid)}


FW_SPECS = [("w_mod", [1, D, 9 * D]), ("b_mod", [1, 9 * D]), ("norm_pre", [1, 3, D]), ("norm_post", [1, 3, D]), ("gla_norm", [1, 128]), ("q_norm", [3, 128]),
            ("kv_norm", [2, 128]), ("wg", [1, 1, D, DFF]), ("wu", [1, 1, D, DFF]), ("wd", [1, 1, DFF, D]), ("w_in", [1, D, DIN]), ("w_dec", [2, 16, 256]), ("b_dec", [2, 256])]
BW_SPECS = [("w_uq", [384, 768]), ("w_ukv", [256, 1024]), ("w_branch", [1, 3, 512, D]), ("w_out", [1, D, D]), ("wg", [1, 1, D, DFF]), ("wu", [1, 1, D, DFF]), ("wd", [1, 1, DFF, D])]


def build_fused(exchange=True):
    import contextlib
    with contextlib.ExitStack() as st:
        k = KF(st)
        ident = k.dram_in("ident", [128, 128]); xin = k.dram_in("xin", [D, NTOK]); xout = k.dram_out("xout", [D, NTOK])
        cc = k.dram_in("cc", [2, D]); tri = k.dram_in("tri", [4, 128, 128])
        r_cos = k.dram_in("rope_cos", [32, NTOK]); r_sin = k.dram_in("rope_sin", [32, NTOK])
        FT = {n: k.dram_in("ft_" + n, s, BF16) for n, s in FT_SHAPES.items()}
        Min = {n: k.dram_in("M_" + n, [128, w]) for n, w in (("m8", 8), ("mb", 2), ("cw", 48), ("cid", 6))}
        FW = [{n: k.dram_in(f"f{l}_{n}", s) for n, s in FW_SPECS} for l in range(2)]
        BW = [{n: k.dram_in(f"b{l}_{n}", s) for n, s in BW_SPECS} for l in range(2)]
        idf = k.sb("idf", [128, 128], F32)
        k.dma("sp", idf[:, :], ident[:, :], w=["ident"])
        M = {n: k.sb("M_" + n, [128, Min[n].shape[1]], F32) for n in Min}
        for n in Min:
            k.dma("sp", M[n][:, :], Min[n][:, :], w=[n])
        x = k.sb("x", [128, 8, NTOK], F32)
        for kc in range(KC):
            for ti, (t0, N, _) in enumerate(TT):
                k.dma("sp", x[:, kc, t0:t0 + N], xin[kc * 128:(kc + 1) * 128, t0:t0 + N], w=[("x", kc, ti)])
        counts = []
        for l in range(2):
            lab = f"L{l}"
            fw, bw = FW[l], BW[l]
            if l == 1:
                k.flush(final=True)
                counts.append(dict(k.S.count))
                k.S.rotate_engine_sems(k.sems2)
            zout = {n: k.scratch(f"z{l}_{n}", s, _dt(d)).ap() for n, s, d in Z_SPECS}
            ex = {n: k.scratch(f"ex{l}_{n}", s, _dt(d)).ap() for n, s, d in EXCH_SPECS}
            hmix_d = k.scratch(f"hmix{l}", [D, NTOK], BF16).ap()
            y_d = [k.scratch(f"ybr{l}_{i}", [512, NTOK], BF16).ap() for i in range(3)]
            G = {"ckvn_all": k.scratch(f"ckvn_all{l}", [256, NKEY], BF16), "krr_all": k.scratch(f"krr_all{l}", [32, NKEY], BF16),
                 "g_all": k.scratch(f"g_all{l}", [8192, 512], BF16), "carry_D": k.scratch(f"carry_D{l}", [2, 3, GH, 64], F32),
                 "carry_S": k.scratch(f"carry_S{l}", [2, 3, GH, 64, 128], F32)}
            vec = k.sb("vec", [128, 128], F32); modT = k.sb("modT", [128, 72, 2], F32)
            colA = k.sb("colA", [128, 3, 8, 2], F32); colG = k.sb("colG", [128, 3, 8, 2], F32)
            with k.phase():
                pss = k.ps("pss", [128, 512], F32)
                emit_small_vectors(k, [fw["b_mod"][0].rearrange("(r p) -> r p", p=128), fw["norm_pre"][0].rearrange("j (r p) -> (j r) p", p=128),
                                       fw["norm_post"][0].rearrange("j (r p) -> (j r) p", p=128), fw["gla_norm"], fw["q_norm"], fw["kv_norm"]], idf, vec, ("vec", lab), pss)
                cst = k.sb("cst", [128, 16], F32)
                emit_small_vectors(k, [cc.rearrange("t (r p) -> (t r) p", p=128)], idf, cst, ("cvec", lab), pss)
                cs_bf = k.sb("cs_bf", [128, 8, 2], BF16)
                k.act(cs_bf[:, :, :], cst[:, :16].rearrange("p (t r) -> p r t", t=2), AF.Silu, r=[("cvec", lab)], w=["cs_bf"])
                wbuf = [k.sb(f"wmodbuf{i}", [128, 8, 512], BF16) for i in range(2)]
                emit_modulation(k, lab, fw["w_mod"], vec[:, 0:72], cs_bf, modT, wbuf, pss[:, 0:144].rearrange("p (j t) -> p j t", t=2))
                emit_mod_columns(k, lab, modT, vec, colA, colG)
            with k.phase():
                nb = NormBufs(k); fb = FfnBufs(k)
                emit_half_ffn(k, nb, fb, lab, 0, 0, x, modT, colA, colG, fw["wg"], fw["wu"], fw["wd"])
            with k.phase():
                nb = NormBufs(k); wb = WinBufs(k)

                class PB:
                    pass
                pbuf = PB()
                pbuf.h = k.sb("pn_h", [128, 8, 512], BF16)
                pbuf.pg = [k.ps(f"pn_pg{s}", [128, 512], F32) for s in range(2)]
                pbuf.pu = [k.ps(f"pn_pu{s}", [128, 512], F32) for s in range(2)]
                hmix = k.sb("hmix", [128, 8, NTOK], BF16)
                for ti, (t0, N, is_ctx) in enumerate(TT):
                    emit_prenorm(k, nb, lab, 1, x, pbuf.h, t0, N, 1 if is_ctx else 0, ti, modT, colA)
                    for kc in range(KC):
                        k.op("dve", lambda e, kc=kc, t0=t0, N=N: e.tensor_copy(out=hmix[:, kc, t0:t0 + N], in_=pbuf.h[:, kc, :N]),
                             r=[("h", kc)], w=[("hmix", kc, ti)])
                for kc in range(KC):
                    k.dma("sp", hmix_d[kc * 128:(kc + 1) * 128, :], hmix[:, kc, :], r=[("hmix", kc, ti) for ti in range(len(TT))])
                emit_win(k, wb, pbuf, 0, fw["w_in"], hmix, zout)
            with k.phase():
                nb = NormBufs(k)
                emit_pre_exchange(k, nb, zout, vec[:, 121:124], vec[:, 124:126], r_cos, r_sin, ex["cqn"], ex["ckvn"], ex["krr"])
            emit_gla(k, lab, "summ", zout["q"], zout["k"], zout["ktm"], zout["v"], zout["g"], zout["a"], fw["w_dec"], fw["b_dec"], vec[:, 120:121], tri,
                     summ_D=ex["summ_D"], summ_S=ex["summ_S"])
            if exchange:
                emit_exchange(k, ex, zout, M, G)
            with_ctx = (l == 0)
            emit_gla(k, lab, "full", zout["q"], zout["k"], zout["ktm"], zout["v"], zout["g"], zout["a"], fw["w_dec"], fw["b_dec"], vec[:, 120:121], tri,
                     carry_D=G["carry_D"].ap(), carry_S=G["carry_S"].ap(), ya_out=y_d[0])
            emit_fourier(k, G["g_all"].ap(), zout["four"][0:CTX, :] if with_ctx else None, FT, y_d[1])
            emit_mla_attention(k, lab, G["ckvn_all"].ap(), G["krr_all"].ap(), ex["cqn"], r_cos, r_sin, bw["w_uq"], bw["w_ukv"], y_d[2], ctx_queries=with_ctx)
            with k.phase():
                nb = NormBufs(k)
                emit_merge(k, nb, lab, x, hmix_d, y_d, fw["w_in"], bw["w_branch"], bw["w_out"], colG, with_ctx)
            with k.phase():
                nb = NormBufs(k); fb = FfnBufs(k)
                emit_half_ffn(k, nb, fb, lab, 2, 0, x, modT, colA, colG, bw["wg"], bw["wu"], bw["wd"], skip_ctx=not with_ctx)
        for kc in range(KC):
            k.dma("sp", xout[kc * 128:(kc + 1) * 128, :], x[:, kc, :], r=[("x", kc, ti) for ti in range(len(TT))])
        k.flush(final=True)
        counts.append(dict(k.S.count))
        build_fused.last_counts = counts
        return k.nc


def kernel_fused(x, c, ctx, c_ctx, w_mod, b_mod, norm_pre, norm_post, ffn_w_gate, ffn_w_up, ffn_w_down, w_in, gla_w_decay, gla_b_decay,
           gla_norm, mla_q_norm, mla_w_uq, mla_kv_norm, mla_w_ukv, w_branch, w_out):
    import ml_dtypes
    bf = ml_dtypes.bfloat16
    f32 = lambda a: np.ascontiguousarray(np.asarray(a, np.float32))
    x, ctx, c, c_ctx = f32(x), f32(ctx), f32(c), f32(c_ctx)
    shared = {"ident": np.eye(128, dtype=np.float32), "tri": _tri_consts()}
    for l in range(2):
        shared.update({f"f{l}_w_mod": f32(w_mod[l:l + 1]), f"f{l}_b_mod": f32(b_mod[l:l + 1]), f"f{l}_norm_pre": f32(norm_pre[l:l + 1]),
                       f"f{l}_norm_post": f32(norm_post[l:l + 1]), f"f{l}_gla_norm": f32(gla_norm[l]).reshape(1, 128), f"f{l}_q_norm": f32(mla_q_norm[l]).reshape(3, 128),
                       f"f{l}_kv_norm": f32(mla_kv_norm[l]).reshape(2, 128), f"f{l}_wg": f32(ffn_w_gate[l:l + 1, 0:1]), f"f{l}_wu": f32(ffn_w_up[l:l + 1, 0:1]),
                       f"f{l}_wd": f32(ffn_w_down[l:l + 1, 0:1]), f"f{l}_w_in": f32(w_in[l:l + 1]), f"f{l}_w_dec": f32(gla_w_decay[l]), f"f{l}_b_dec": f32(gla_b_decay[l]),
                       f"b{l}_w_uq": f32(mla_w_uq[l]), f"b{l}_w_ukv": f32(mla_w_ukv[l]), f"b{l}_w_branch": f32(w_branch[l:l + 1]), f"b{l}_w_out": f32(w_out[l:l + 1]),
                       f"b{l}_wg": f32(ffn_w_gate[l:l + 1, 1:2]), f"b{l}_wu": f32(ffn_w_up[l:l + 1, 1:2]), f"b{l}_wd": f32(ffn_w_down[l:l + 1, 1:2])})
    nc = build_fused()
    maps = []
    for core in range(8):
        b, s = divmod(core, 4)
        rc, rs = _rope_tables(s)
        xl = np.concatenate([ctx[b], x[b, s * SEG:(s + 1) * SEG]], 0)
        m = {"xin": np.ascontiguousarray(xl.T), "cc": np.stack([c[b], c_ctx], 0), "rope_cos": rc, "rope_sin": rs}
        m.update({"ft_" + n: v.astype(bf) for n, v in fourier_tables(s).items()})
        m.update({"M_" + n: v for n, v in _exchange_consts(core).items()})
        m.update(shared)
        maps.append(m)
    outs = run_bass_kernel_spmd(nc, maps, core_ids=list(range(8))).results
    out = np.empty((2, 8192, D), np.float32)
    for core in range(8):
        b, s = divmod(core, 4)
        out[b, s * SEG:(s + 1) * SEG] = np.asarray(outs[core]["xout"]).T[CTX:]
    return out


USE_FUSED = False


def kernel(**inputs):
    return (kernel_fused if USE_FUSED else kernel_unfused)(**inputs)
```
